# Optimizing a Trainium2 kernel written in Bass

```python
import math
import jax
import jax.numpy as jnp
from jax import lax
import numpy as np

D_MODEL = 2048
BATCH = 32
SEQ = 256
DEPTH = 4
DEC_BATCH = 8
DEC_SEQ = 1024
PAST_LEN = 256

GRID_W = 64
EPS = 1e-6

HG_WIDTH = D_MODEL // 2
HG_DK = 128
HG_HEADS = HG_WIDTH // HG_DK
HG_DV = HG_WIDTH // HG_HEADS
HG_CHUNK = 32

AT_HD = 128
AT_Q_HEADS = (D_MODEL // 2) // AT_HD
AT_KV_HEADS = AT_Q_HEADS // 4
AT_GROUP = AT_Q_HEADS // AT_KV_HEADS
AT_WIDTH = AT_Q_HEADS * AT_HD
AT_KV_WIDTH = AT_KV_HEADS * AT_HD
Q_BLOCK = 128
ROPE_THETA = 10000.0
ROPE_AX = AT_HD // 2

S5_WIDTH = D_MODEL // 2
S5_GROUP_CH = 16
S5_GROUPS = S5_WIDTH // S5_GROUP_CH
S5_STATE = 64

IN_COLS = 5 * HG_WIDTH + 2 * AT_WIDTH + 2 * AT_KV_WIDTH + 2 * S5_WIDTH + 3 * D_MODEL

kernel_name = 'hybrid_hgrn2_gqa_s5_diffusion_step'


def rms_norm(x, w):
    xf = x.astype(jnp.float32)
    y = xf * lax.rsqrt(jnp.mean(xf * xf, axis=-1, keepdims=True) + EPS)
    return (y * w.astype(jnp.float32)).astype(x.dtype)


def split_columns(z):
    sizes = (HG_WIDTH,) * 5 + (AT_WIDTH, AT_KV_WIDTH, AT_KV_WIDTH, AT_WIDTH, S5_WIDTH, S5_WIDTH, 3 * D_MODEL)
    offsets = []
    acc = 0
    for s in sizes[:-1]:
        acc += s
        offsets.append(acc)
    return jnp.split(z, offsets, axis=-1)


def hgrn_lower_bounds(logits):
    p = jax.nn.softmax(logits.astype(jnp.float32), axis=1)
    cs = jnp.cumsum(p, axis=1)
    return cs - cs[:, :1]


def hgrn_forget(f_raw, lb):
    bsz, seq, _ = f_raw.shape
    lb = lb.astype(jnp.float32)
    f = lb + (1.0 - lb) * jax.nn.sigmoid(f_raw.astype(jnp.float32))
    shape = (bsz, seq, HG_HEADS, HG_DK)
    return jnp.log(f).reshape(shape), (1.0 - f).reshape(shape)


def hgrn_chunk_scan(q, k, v, log_f, s0):
    bsz, seq, nh, _ = q.shape
    dv = v.shape[-1]
    n_chunks = seq // HG_CHUNK

    def chunks(a):
        return a.astype(jnp.float32).reshape(bsz, n_chunks, HG_CHUNK, nh, a.shape[-1]).transpose(1, 0, 3, 2, 4)

    causal = jnp.tril(jnp.ones((HG_CHUNK, HG_CHUNK), dtype=bool))[:, :, None]

    def step(state, blk):
        qc, kc, vc, gc = blk
        b = jnp.cumsum(gc, axis=2)
        rel = b[:, :, :, None, :] - b[:, :, None, :, :]
        decay = jnp.exp(jnp.where(causal, rel, -jnp.inf))
        scores = jnp.einsum('bhtd,bhtsd,bhsd->bhts', qc, decay, kc)
        o = jnp.einsum('bhts,bhsv->bhtv', scores, vc) + jnp.einsum('bhtd,bhdv->bhtv', qc * jnp.exp(b), state)
        b_last = b[:, :, -1:, :]
        new_state = jnp.exp(b_last[:, :, 0, :])[..., None] * state + jnp.einsum(
            'bhsd,bhsv->bhdv', kc * jnp.exp(b_last - b), vc)
        return new_state, o

    s_final, o = lax.scan(step, s0.astype(jnp.float32), (chunks(q), chunks(k), chunks(v), chunks(log_f)))
    o = o.transpose(1, 0, 3, 2, 4).reshape(bsz, seq, nh, dv)
    return o, s_final


def hgrn_branch(q_raw, i_raw, ff_raw, fb_raw, g_raw, lb_f, lb_b, onorm_w, s0):
    bsz, seq, _ = q_raw.shape
    q = q_raw.reshape(bsz, seq, HG_HEADS, HG_DK)
    v = i_raw.reshape(bsz, seq, HG_HEADS, HG_DV)
    logf_f, k_f = hgrn_forget(ff_raw, lb_f)
    logf_b, k_b = hgrn_forget(fb_raw, lb_b)
    o_f, s_f = hgrn_chunk_scan(q, k_f, v, logf_f, s0[:, 0])
    o_b, s_b = hgrn_chunk_scan(jnp.flip(q, 1), jnp.flip(k_b, 1), jnp.flip(v, 1), jnp.flip(logf_b, 1), s0[:, 1])
    o = rms_norm(o_f + jnp.flip(o_b, 1), onorm_w).reshape(bsz, seq, HG_WIDTH).astype(q_raw.dtype)
    return o * jax.nn.silu(g_raw), jnp.stack([s_f, s_b], axis=1)


def axial_rope_tables(seq):
    rows = seq // GRID_W
    row = jnp.repeat(jnp.arange(rows, dtype=jnp.float32), GRID_W)
    col = jnp.tile(jnp.arange(GRID_W, dtype=jnp.float32), rows)
    half = ROPE_AX // 2
    inv = ROPE_THETA ** (-jnp.arange(half, dtype=jnp.float32) / half)
    ang_r = row[:, None] * inv
    ang_c = col[:, None] * inv
    return (jnp.cos(ang_r), jnp.sin(ang_r), jnp.cos(ang_c), jnp.sin(ang_c))


def rope_rotate(x, cos, sin):
    half = x.shape[-1] // 2
    x1, x2 = x[..., :half], x[..., half:]
    cos = cos[None, :, None, :]
    sin = sin[None, :, None, :]
    return jnp.concatenate([x1 * cos - x2 * sin, x2 * cos + x1 * sin], axis=-1)


def apply_axial_rope(x, tables):
    cos_r, sin_r, cos_c, sin_c = tables
    xf = x.astype(jnp.float32)
    y = jnp.concatenate([rope_rotate(xf[..., :ROPE_AX], cos_r, sin_r),
                         rope_rotate(xf[..., ROPE_AX:], cos_c, sin_c)], axis=-1)
    return y.astype(x.dtype)


def block_attention(q, k, v):
    bsz, tq = q.shape[:2]
    nb = tq // Q_BLOCK
    qb = q.reshape(bsz, nb, Q_BLOCK, AT_KV_HEADS, AT_GROUP, AT_HD).transpose(1, 0, 2, 3, 4, 5)
    scale = AT_HD ** -0.5

    def one_block(qblk):
        s = jnp.einsum('bqkgd,bskd->bkgqs', qblk, k).astype(jnp.float32) * scale
        p = jax.nn.softmax(s, axis=-1).astype(v.dtype)
        return jnp.einsum('bkgqs,bskd->bqkgd', p, v)

    o = lax.map(one_block, qb)
    return o.transpose(1, 0, 2, 3, 4, 5).reshape(bsz, tq, AT_WIDTH)


def cmul(ar, ai, br, bi):
    return ar * br - ai * bi, ar * bi + ai * br


def ssm_combine(e1, e2):
    a1r, a1i, b1r, b1i = e1
    a2r, a2i, b2r, b2i = e2
    ar, ai = cmul(a2r, a2i, a1r, a1i)
    br, bi = cmul(a2r, a2i, b1r, b1i)
    return ar, ai, br + b2r, bi + b2i


def s5_discretise(a_re, a_im, log_dt, b_re, b_im):
    a_re = a_re.astype(jnp.float32)
    a_im = a_im.astype(jnp.float32)
    dt = jnp.exp(log_dt.astype(jnp.float32))[:, None]
    mag = jnp.exp(a_re * dt)
    abar_re = mag * jnp.cos(a_im * dt)
    abar_im = mag * jnp.sin(a_im * dt)
    den = a_re * a_re + a_im * a_im
    num_re = abar_re - 1.0
    coef_re = (num_re * a_re + abar_im * a_im) / den
    coef_im = (abar_im * a_re - num_re * a_im) / den
    bbar_re, bbar_im = cmul(coef_re[..., None], coef_im[..., None],
                            b_re.astype(jnp.float32), b_im.astype(jnp.float32))
    return abar_re, abar_im, bbar_re, bbar_im


def s5_scan(u, abar_re, abar_im, bbar_re, bbar_im, s0_re, s0_im):
    bu_re = jnp.einsum('btgc,gpc->btgp', u, bbar_re)
    bu_im = jnp.einsum('btgc,gpc->btgp', u, bbar_im)
    shape = (1, u.shape[1]) + abar_re.shape
    a_re = jnp.broadcast_to(abar_re, shape)
    a_im = jnp.broadcast_to(abar_im, shape)
    cum_re, cum_im, s_re, s_im = lax.associative_scan(ssm_combine, (a_re, a_im, bu_re, bu_im), axis=1)
    init_re, init_im = cmul(cum_re, cum_im, s0_re[:, None].astype(jnp.float32), s0_im[:, None].astype(jnp.float32))
    return s_re + init_re, s_im + init_im


def s5_branch(u_raw, gate_raw, a_re, a_im, log_dt, b_re, b_im, c_re, c_im, d, w_glu, s0_re, s0_im):
    bsz, seq, _ = u_raw.shape
    u = u_raw.astype(jnp.float32).reshape(bsz, seq, S5_GROUPS, S5_GROUP_CH)
    fwd = s5_discretise(a_re[0], a_im[0], log_dt[0], b_re[0], b_im[0])
    bwd = s5_discretise(a_re[1], a_im[1], log_dt[1], b_re[1], b_im[1])
    sf_re, sf_im = s5_scan(u, *fwd, s0_re[:, 0], s0_im[:, 0])
    sb_re, sb_im = s5_scan(jnp.flip(u, 1), *bwd, s0_re[:, 1], s0_im[:, 1])
    final_re = jnp.stack([sf_re[:, -1], sb_re[:, -1]], axis=1)
    final_im = jnp.stack([sf_im[:, -1], sb_im[:, -1]], axis=1)
    s_re = sf_re + jnp.flip(sb_re, 1)
    s_im = sf_im + jnp.flip(sb_im, 1)
    y = jnp.einsum('btgp,gcp->btgc', s_re, c_re.astype(jnp.float32)) - jnp.einsum(
        'btgp,gcp->btgc', s_im, c_im.astype(jnp.float32))
    y = (y.reshape(bsz, seq, S5_WIDTH) + d.astype(jnp.float32) * u_raw.astype(jnp.float32)).astype(u_raw.dtype)
    y = jax.nn.gelu(y)
    y = y * jax.nn.sigmoid(y @ w_glu)
    return y * jax.nn.silu(gate_raw), final_re, final_im


def trunk_layer(x, mod, l, p, lower_bounds, hg_s0, s5_s0_re, s5_s0_im, ctx_k, ctx_v, rope):
    bsz, seq, _ = x.shape
    shift, scale, gate = jnp.split(mod, 3, axis=-1)
    h = rms_norm(x, p['norm_w'][l]) * (1 + scale[:, None, :]) + shift[:, None, :]
    (hg_q, hg_i, hg_ff, hg_fb, hg_g, at_q, at_k, at_v, at_g,
     s5_u, s5_g, merge_raw) = split_columns(h @ p['w_in'][l])

    o_hg, hg_state = hgrn_branch(hg_q, hg_i, hg_ff, hg_fb, hg_g, lower_bounds[0, l], lower_bounds[1, l],
                                 p['hg_onorm'][l], hg_s0)

    q = rms_norm(at_q.reshape(bsz, seq, AT_Q_HEADS, AT_HD), p['at_q_norm'][l])
    k = rms_norm(at_k.reshape(bsz, seq, AT_KV_HEADS, AT_HD), p['at_k_norm'][l])
    v = at_v.reshape(bsz, seq, AT_KV_HEADS, AT_HD)
    if ctx_k is None:
        o_at = block_attention(q, k, v)
    else:
        q_r = apply_axial_rope(q, rope)
        k_r = apply_axial_rope(k, rope)
        k_all = jnp.concatenate([ctx_k.astype(k.dtype), k_r], axis=1)
        v_all = jnp.concatenate([ctx_v.astype(v.dtype), v], axis=1)
        o_at = block_attention(q_r, k_all, v_all)
    o_at = o_at * jax.nn.silu(at_g)

    o_s5, s5_re, s5_im = s5_branch(s5_u, s5_g, p['s5_a_re'][l], p['s5_a_im'][l], p['s5_log_dt'][l],
                                   p['s5_b_re'][l], p['s5_b_im'][l], p['s5_c_re'][l], p['s5_c_im'][l],
                                   p['s5_d'][l], p['s5_w_glu'][l], s5_s0_re, s5_s0_im)

    g_hg, g_at, g_s5 = jnp.split(jax.nn.sigmoid(merge_raw), 3, axis=-1)
    merged = (g_hg * (o_hg @ p['w_br_hg'][l]) + g_at * (o_at @ p['w_br_at'][l])
              + g_s5 * (o_s5 @ p['w_br_s5'][l]))
    x = x + gate[:, None, :] * (merged @ p['w_out'][l])
    return x, (k, v, hg_state, s5_re, s5_im)


def setup_inputs(seed: int = 0) -> dict:
    key = jax.random.key(seed)
    ks = iter(jax.random.split(key, 40))

    def nrm(shape, s):
        return s * jax.random.normal(next(ks), shape, jnp.float32)

    G, P, CH = S5_GROUPS, S5_STATE, S5_GROUP_CH
    return {
        'x_prompt': nrm((BATCH, SEQ, D_MODEL), 1.0),
        'x_sample': nrm((DEC_BATCH, DEC_SEQ, D_MODEL), 1.0),
        'cache_k': nrm((DEC_BATCH, DEPTH, PAST_LEN, AT_KV_HEADS, AT_HD), 1.0),
        'cache_v': nrm((DEC_BATCH, DEPTH, PAST_LEN, AT_KV_HEADS, AT_HD), 1.0),
        'state_hgrn': nrm((DEC_BATCH, DEPTH, 2, HG_HEADS, HG_DK, HG_DV), 0.5),
        'state_s5_re': nrm((DEC_BATCH, DEPTH, 2, G, P), 0.5),
        'state_s5_im': nrm((DEC_BATCH, DEPTH, 2, G, P), 0.5),
        'c': nrm((DEC_BATCH, D_MODEL), 1.0),
        'c_ctx': nrm((D_MODEL,), 1.0),
        'norm_w': 1.0 + nrm((DEPTH, D_MODEL), 0.02),
        'w_mod': nrm((DEPTH, D_MODEL, 3 * D_MODEL), 0.5 * D_MODEL ** -0.5),
        'b_mod': nrm((DEPTH, 3 * D_MODEL), 0.02),
        'w_in': nrm((DEPTH, D_MODEL, IN_COLS), D_MODEL ** -0.5),
        'hg_lb_logits': nrm((2, DEPTH, HG_WIDTH), 0.1),
        'hg_onorm': 1.0 + nrm((DEPTH, HG_DV), 0.02),
        'at_q_norm': 1.0 + nrm((DEPTH, AT_HD), 0.02),
        'at_k_norm': 1.0 + nrm((DEPTH, AT_HD), 0.02),
        's5_a_re': -0.5 * jnp.exp(nrm((DEPTH, 2, G, P), 0.05)),
        's5_a_im': math.pi * jnp.arange(P, dtype=jnp.float32) + nrm((DEPTH, 2, G, P), 0.01),
        's5_log_dt': jax.random.uniform(next(ks), (DEPTH, 2, G), jnp.float32,
                                        minval=math.log(0.001), maxval=math.log(0.1)),
        's5_b_re': nrm((DEPTH, 2, G, P, CH), (2 * CH) ** -0.5),
        's5_b_im': nrm((DEPTH, 2, G, P, CH), (2 * CH) ** -0.5),
        's5_c_re': nrm((DEPTH, G, CH, P), (2 * P) ** -0.5),
        's5_c_im': nrm((DEPTH, G, CH, P), (2 * P) ** -0.5),
        's5_d': nrm((DEPTH, S5_WIDTH), 0.5),
        's5_w_glu': nrm((DEPTH, S5_WIDTH, S5_WIDTH), S5_WIDTH ** -0.5),
        'w_br_hg': nrm((DEPTH, HG_WIDTH, D_MODEL), HG_WIDTH ** -0.5),
        'w_br_at': nrm((DEPTH, AT_WIDTH, D_MODEL), AT_WIDTH ** -0.5),
        'w_br_s5': nrm((DEPTH, S5_WIDTH, D_MODEL), S5_WIDTH ** -0.5),
        'w_out': nrm((DEPTH, D_MODEL, D_MODEL), D_MODEL ** -0.5),
        'final_norm': 1.0 + nrm((D_MODEL,), 0.02),
    }


def reference(x_prompt, x_sample, cache_k, cache_v, state_hgrn, state_s5_re, state_s5_im, c, c_ctx,
              norm_w, w_mod, b_mod, w_in, hg_lb_logits, hg_onorm, at_q_norm, at_k_norm,
              s5_a_re, s5_a_im, s5_log_dt, s5_b_re, s5_b_im, s5_c_re, s5_c_im, s5_d, s5_w_glu,
              w_br_hg, w_br_at, w_br_s5, w_out, final_norm):
    p = {'norm_w': norm_w, 'w_in': w_in, 'hg_onorm': hg_onorm, 'at_q_norm': at_q_norm,
         'at_k_norm': at_k_norm, 's5_a_re': s5_a_re, 's5_a_im': s5_a_im, 's5_log_dt': s5_log_dt,
         's5_b_re': s5_b_re, 's5_b_im': s5_b_im, 's5_c_re': s5_c_re, 's5_c_im': s5_c_im,
         's5_d': s5_d, 's5_w_glu': s5_w_glu, 'w_br_hg': w_br_hg, 'w_br_at': w_br_at,
         'w_br_s5': w_br_s5, 'w_out': w_out}
    lower_bounds = hgrn_lower_bounds(hg_lb_logits)

    bsz_p = x_prompt.shape[0]
    hg_zero = jnp.zeros((bsz_p, 2, HG_HEADS, HG_DK, HG_DV), jnp.float32)
    s5_zero = jnp.zeros((bsz_p, 2, S5_GROUPS, S5_STATE), jnp.float32)
    c_ctx_silu = jax.nn.silu(c_ctx)[None, :]
    xp = x_prompt
    ks_l, vs_l, hs_l, sre_l, sim_l = [], [], [], [], []
    for l in range(DEPTH):
        mod = c_ctx_silu @ w_mod[l] + b_mod[l]
        xp, (k_l, v_l, h_l, sr_l, si_l) = trunk_layer(xp, mod, l, p, lower_bounds, hg_zero, s5_zero, s5_zero,
                                                      None, None, None)
        ks_l.append(k_l)
        vs_l.append(v_l)
        hs_l.append(h_l)
        sre_l.append(sr_l)
        sim_l.append(si_l)
    y_prompt = rms_norm(xp, final_norm)
    new_cache_k = jnp.stack(ks_l, axis=1)
    new_cache_v = jnp.stack(vs_l, axis=1)
    new_state_hgrn = jnp.stack(hs_l, axis=1)
    new_state_s5_re = jnp.stack(sre_l, axis=1)
    new_state_s5_im = jnp.stack(sim_l, axis=1)

    rope = axial_rope_tables(x_sample.shape[1])
    c_silu = jax.nn.silu(c)
    xs = x_sample
    for l in range(DEPTH):
        mod = c_silu @ w_mod[l] + b_mod[l]
        xs, _ = trunk_layer(xs, mod, l, p, lower_bounds, state_hgrn[:, l], state_s5_re[:, l], state_s5_im[:, l],
                            cache_k[:, l], cache_v[:, l], rope)
    y_sample = rms_norm(xs, final_norm)

    return (y_prompt, y_sample, new_cache_k, new_cache_v, new_state_hgrn, new_state_s5_re, new_state_s5_im)
```

```python
import contextlib
import math
import numpy as np
import concourse.bass as bass
import concourse.mybir as mybir
from concourse.bass_utils import run_bass_kernel_spmd

F32 = mybir.dt.float32
BF16 = mybir.dt.bfloat16
I32 = mybir.dt.int32
AF = mybir.ActivationFunctionType
ALU = mybir.AluOpType
AX = mybir.AxisListType

ENGS = ("pe", "act", "dve", "pool", "sp")

D = 2048
KT = 16
T = 1024
DEPTH = 4
EPS = 1e-6
NCORES = 8
IN_COLS = 15872
NCT = IN_COLS // 128
CT_HQ, CT_HI, CT_HFF, CT_HFB, CT_HG = 0, 8, 16, 24, 32
CT_AQ, CT_AK, CT_AV, CT_AG = 40, 48, 50, 52
CT_SU, CT_SG = 60, 68
CT_MG = 76


class Region:
    __slots__ = ("w", "r")

    def __init__(self):
        self.w = None
        self.r = []


def regions(n):
    return [Region() for _ in range(n)]


class Builder:
    def __init__(self, nc, n_dma_sems=20, n_w_sems=6):
        self.nc = nc
        self.stack = contextlib.ExitStack()
        self.cnt = {e: 0 for e in ENGS}
        self.seen = {e: {} for e in ENGS}
        self.prog = {e: [] for e in ENGS}
        self.semobj = {}
        for e in ENGS:
            self.semobj[("c", e)] = self.stack.enter_context(nc.semaphore("c_" + e))
        self.dcnt = [0] * n_dma_sems
        for i in range(n_dma_sems):
            self.semobj[("d", i)] = self.stack.enter_context(nc.semaphore("d%d" % i))
        self.wcnt = [0] * n_w_sems
        for i in range(n_w_sems):
            self.semobj[("w", i)] = self.stack.enter_context(nc.semaphore("w%d" % i))
        self.drr = 0
        self.n_ops = 0

    def sbuf(self, name, shape, dt):
        return self.stack.enter_context(self.nc.sbuf_tensor(name, list(shape), dt))

    def psum(self, name, shape, dt):
        return self.stack.enter_context(self.nc.psum_tensor(name, list(shape), dt))

    def _waits(self, eng, reads, writes):
        need = {}
        for r in reads:
            if r.w is not None:
                k, v = r.w
                if need.get(k, 0) < v:
                    need[k] = v
        for w in writes:
            if w.w is not None:
                k, v = w.w
                if need.get(k, 0) < v:
                    need[k] = v
            for (k, v) in w.r:
                if need.get(k, 0) < v:
                    need[k] = v
        out = []
        seen = self.seen[eng]
        for k, v in need.items():
            if k == ("c", eng) and eng == "pe":
                continue
            if seen.get(k, 0) >= v:
                continue
            seen[k] = v
            out.append((k, v))
        return out

    def _commit(self, ev, reads, writes):
        for r in reads:
            r.r.append(ev)
            if len(r.r) > 64:
                mx = {}
                for (k, v) in r.r:
                    if mx.get(k, 0) < v:
                        mx[k] = v
                r.r = list(mx.items())
        for w in writes:
            w.w = ev
            w.r = []

    def op(self, eng, fn, reads=(), writes=()):
        waits = self._waits(eng, reads, writes)
        self.cnt[eng] += 1
        ev = (("c", eng), self.cnt[eng])
        self.prog[eng].append((waits, fn, ("c", eng), 1))
        self._commit(ev, reads, writes)
        self.n_ops += 1
        return ev

    def dma(self, q, out, in_, reads=(), writes=(), wslot=None, **kw):
        waits = self._waits(q, reads, writes)
        if wslot is None:
            i = self.drr
            self.drr = (self.drr + 1) % len(self.dcnt)
            self.dcnt[i] += 16
            ev = (("d", i), self.dcnt[i])
        else:
            self.wcnt[wslot] += 16
            ev = (("w", wslot), self.wcnt[wslot])

        def fn(e, out=out, in_=in_, kw=kw):
            return e.dma_start(out=out, in_=in_, **kw)

        self.prog[q].append((waits, fn, ev[0], 16))
        self._commit(ev, reads, writes)
        self.n_ops += 1
        return ev

    def fence(self):
        for e in ("pe", "act", "dve", "sp"):
            self.wait_all(e, engs=("pe", "act", "dve", "sp"))

    def wait_all(self, eng, engs=ENGS):
        waits = []
        for e in engs:
            if e == eng:
                continue
            if self.cnt[e] > self.seen[eng].get(("c", e), 0):
                self.seen[eng][("c", e)] = self.cnt[e]
                waits.append((("c", e), self.cnt[e]))
        for i, c in enumerate(self.dcnt):
            if c > self.seen[eng].get(("d", i), 0):
                self.seen[eng][("d", i)] = c
                waits.append((("d", i), c))
        self.prog[eng].append((waits, None, None, 0))

    def emit(self):
        nc = self.nc
        handles = {"pe": "tensor", "act": "scalar", "dve": "vector", "pool": "gpsimd", "sp": "sync"}
        semobj = self.semobj
        with nc.Block() as block:
            for e in ENGS:
                prog = self.prog[e]

                def body(h, prog=prog):
                    for waits, fn, inc, amt in prog:
                        for (k, v) in waits:
                            h.wait_ge(semobj[k], v)
                        if fn is not None:
                            fn(h).then_inc(semobj[inc], amt)

                getattr(block, handles[e])(body)
        self.stack.close()


class Prog:
    def __init__(self, cfg):
        self.cfg = cfg
        self.layers = cfg.get("layers", list(range(DEPTH)))
        self.passes = cfg.get("passes", [0, 1])
        self.nL = cfg.get("n_layers_alloc", DEPTH)
        nc = bass.Bass("TRN2", target_bir_lowering=False)
        self.nc = nc
        self.B = Builder(nc)
        self.din = {}
        self.dout = {}
        self.build()

    def inp(self, name, shape, dt=F32):
        t = self.nc.dram_tensor(name, list(shape), dt, kind="ExternalInput").ap()
        self.din[name] = t
        return t

    def outp(self, name, shape, dt=F32):
        t = self.nc.dram_tensor(name, list(shape), dt, kind="ExternalOutput").ap()
        self.dout[name] = t
        return t

    def scratch(self, name, shape, dt=F32):
        return self.nc.dram_tensor(name, list(shape), dt).ap()

    def ps(self):
        i = self.ps_rr
        self.ps_rr = (self.ps_rr + 1) % 8
        return self.PS[i], self.RPS[i]

    def wload(self, src, ncols):
        s = self.w_rr
        self.w_rr = (self.w_rr + 1) % self.NSLOT
        self.B.dma("pool", self.ring[:, s, 0:ncols], src, writes=[self.Rring[s]], wslot=s)
        return self.ring[:, s, :], self.Rring[s]

    def mm(self, out, lhsT, rhs, start, stop, reads, writes):
        self.B.op("pe", lambda e: e.matmul(out, lhsT, rhs, start=start, stop=stop), reads=reads, writes=writes)

    def act(self, out, in_, func, reads, writes, scale=1.0, bias=None):
        if bias is None:
            self.B.op("act", lambda e: e.activation(out=out, in_=in_, func=func, scale=scale), reads=reads, writes=writes)
        else:
            self.B.op("act", lambda e: e.activation(out=out, in_=in_, func=func, scale=scale, bias=bias), reads=reads, writes=writes)

    def tt(self, out, in0, in1, op, reads, writes, eng="dve"):
        self.B.op(eng, lambda e: e.tensor_tensor(out=out, in0=in0, in1=in1, op=op), reads=reads, writes=writes)

    def ts(self, out, in0, s1, s2, op0, op1, reads, writes, eng="dve"):
        if s2 is None:
            self.B.op(eng, lambda e: e.tensor_scalar(out=out, in0=in0, scalar1=s1, scalar2=None, op0=op0), reads=reads, writes=writes)
        else:
            self.B.op(eng, lambda e: e.tensor_scalar(out=out, in0=in0, scalar1=s1, scalar2=s2, op0=op0, op1=op1), reads=reads, writes=writes)

    def stt(self, out, in0, scalar, in1, op0, op1, reads, writes, eng="dve"):
        self.B.op(eng, lambda e: e.scalar_tensor_tensor(out=out, in0=in0, scalar=scalar, in1=in1, op0=op0, op1=op1), reads=reads, writes=writes)

    def cp(self, out, in_, reads, writes, eng="dve"):
        self.B.op(eng, lambda e: e.tensor_copy(out=out, in_=in_), reads=reads, writes=writes)

    def sig_from_exp(self, buf, R_):
        self.act(buf, buf, AF.Ln, reads=[R_, self.Rconst], writes=[R_], scale=1.0, bias=self.eps_t[:, 1:2])
        self.act(buf, buf, AF.Exp, reads=[R_], writes=[R_], scale=-1.0)

    def recip(self, out, in_, reads, writes):
        self.B.op("dve", lambda e: e.reciprocal(out=out, in_=in_), reads=reads, writes=writes)

    def proj_fm(self, w, Rw, kt_n, rhs_fn, rhs_regs, evac):
        banks = [self.ps() for _ in range(2)]
        for c in range(2):
            P_, R_ = banks[c]
            for kt in range(kt_n):
                self.mm(P_[:, :], w[:, kt * 128:(kt + 1) * 128], rhs_fn(kt, c), kt == 0, kt == kt_n - 1,
                        reads=[Rw] + rhs_regs, writes=[R_])
        for c in range(2):
            P_, R_ = banks[c]
            evac(c, P_, R_)

    def proj_tm(self, w, Rw, evac):
        for half in range(2):
            P_, R_ = self.ps()
            for q in range(4):
                tt = half * 4 + q
                for kt in range(KT):
                    self.mm(P_[:, q * 128:(q + 1) * 128], self.h[:, kt, tt * 128:(tt + 1) * 128],
                            w[:, kt * 128:(kt + 1) * 128], kt == 0, kt == KT - 1,
                            reads=[Rw, self.Rh], writes=[R_])
            for q in range(4):
                evac(half * 4 + q, P_[:, q * 128:(q + 1) * 128], R_)

    def rstd_from_sq(self, sq_fn, n_kt, sq_regs, denom, out, Rout, ncols):
        P_, R_ = self.ps()
        for kt in range(n_kt):
            self.mm(P_[:, 0:ncols], self.ones_bf[:, :], sq_fn(kt), kt == 0, kt == n_kt - 1,
                    reads=sq_regs + [self.Rconst], writes=[R_])
        self.act(out, P_[:, 0:ncols], AF.Ln, reads=[R_, self.Rconst], writes=[Rout], scale=1.0 / denom, bias=self.eps_t[:, 0:1])
        self.act(out, out, AF.Exp, reads=[Rout], writes=[Rout], scale=-0.5)

    def build(self):
        nc, B = self.nc, self.B
        nL = self.nL
        xin = self.inp("xin", [2, KT, 128, T])
        cond = self.inp("cond", [128, KT, 2])
        wmod = self.inp("wmod", [nL, 48, 128, KT * 128])
        bmod = self.inp("bmod", [128, nL, 48])
        normw = self.inp("normw", [128, nL, KT])
        finaln = self.inp("finaln", [128, KT])
        win = self.inp("win", [nL, NCT, 128, KT * 128])
        qnw = self.inp("qnw", [128, nL])
        knw = self.inp("knw", [128, nL])
        cmat = self.inp("cmat", [128, 8, 128])
        lblog = self.inp("lblog", [128, 2, 8, DEPTH])
        onw = self.inp("onw", [128, nL])
        if 0 in self.passes:
            self.hst = self.inp("hst", [nL, 2, 8, 128, 128])
        self.nhs = self.outp("nhs", [4, nL, 2, 8, 128, 128])
        self.s5p = self.inp("s5p", [nL, 128, 3, 64])
        self.s5b = self.inp("s5b", [nL, 2, 128, 64 * 16])
        self.s5c = self.inp("s5c", [nL, 2, 128, 64 * 16])
        s5dd = self.inp("s5dd", [128, nL, 64])
        sgnk_d = self.inp("sgnk", [128, 3, 9])
        self.wglu = self.inp("wglu", [nL, 8, 128, 8 * 128])
        self.wbr = self.inp("wbr", [nL, 3, KT, 128, 8 * 128])
        self.wout = self.inp("wout", [nL, KT, 128, KT * 128])
        if 0 in self.passes:
            self.s5s0 = self.inp("s5s0", [nL, 128, 64, 2])
        self.ns5 = self.outp("ns5", [nL, 128, 4, 64, 2])
        mk = self.outp if self.cfg.get("dbg_s5") else self.scratch
        self.toep_d = mk("toep_d", [nL, 64, 128, 128], BF16)
        self.wb_d = mk("wb_d", [nL, 64, 128, 2, 128], BF16)
        self.ca_d = mk("ca_d", [nL, 2, 2, 64, 64, 128], BF16)
        self.a8_d = mk("a8_d", [nL, 2, 128, 64], F32)
        self.ud_d = self.scratch("ud_d", [64, 16, 8, 128], BF16)
        self.yd_d = self.scratch("yd_d", [128, 64, 128], BF16)
        self.win = win
        if 0 in self.passes:
            self.ck = self.inp("ck", [nL, 2, 128, 256])
            self.cv = self.inp("cv", [nL, 256, 256])
            self.ropeC = self.inp("ropeC", [128, T])
            self.ropeS = self.inp("ropeS", [128, T])
        self.xs = self.scratch("xs", [2, KT, 128, T])
        nck = self.outp("nck", [nL, 2, 128, T])
        ncv = self.outp("ncv", [nL, T, 256])
        yout = self.outp("y", [2, KT, 128, T])
        self.nck, self.ncv = nck, ncv

        self.NSLOT = 6
        self.ring = B.sbuf("ring", [128, self.NSLOT, KT * 128], BF16)
        self.Rring = regions(self.NSLOT)
        self.w_rr = 0
        self.PS = [B.psum("ps%d" % i, [128, 512], F32) for i in range(8)]
        self.RPS = regions(8)
        self.ps_rr = 0
        self.h = B.sbuf("h", [128, KT, T], BF16)
        self.Rh = Region()
        cm = B.sbuf("cm", [128, 8, 128], F32)
        self.Rconst = Region()
        self.ones_bf = B.sbuf("ones_bf", [128, 128], BF16)
        self.ident_bf = B.sbuf("ident_bf", [128, 128], BF16)
        self.eps_t = B.sbuf("eps_t", [128, 2], F32)
        cond_s = B.sbuf("cond_s", [128, KT, 2], F32)
        csilu = B.sbuf("csilu", [128, KT, 2], BF16)
        modT = B.sbuf("modT", [128, nL, 48, 2], F32)
        bmod_s = B.sbuf("bmod_s", [128, nL, 48], F32)
        normw_s = B.sbuf("normw_s", [128, nL, KT], F32)
        finaln_s = B.sbuf("finaln_s", [128, KT], F32)
        amod = B.sbuf("amod", [128, nL, KT, 2], F32)
        qnw_s = B.sbuf("qnw_s", [128, nL], F32)
        knw_s = B.sbuf("knw_s", [128, nL], F32)
        Rsm = Region()
        self.Rsm = Rsm
        lbl_s = B.sbuf("lbl_s", [128, 2, 8, DEPTH], F32)
        self.lb = B.sbuf("lb", [128, 2, 8, DEPTH], F32)
        self.oml = B.sbuf("oml", [128, 2, 8, DEPTH], F32)
        lbsum = B.sbuf("lbsum", [128, 2, 8], F32)
        self.onw_s = B.sbuf("onw_s", [128, nL], F32)
        self.resetm = B.sbuf("resetm", [128, T], F32)
        self.cm = cm
        self.o_hg = B.sbuf("o_hg", [128, 8, T], BF16)
        self.Rohg = regions(8)
        self.o_at = B.sbuf("o_at", [128, 8, T], BF16)
        self.Roat = regions(8)
        self.qnw_s, self.knw_s = qnw_s, knw_s
        self.o_s5 = B.sbuf("o_s5", [128, 8, T], BF16)
        self.Ros5 = regions(8)
        self.sgnk = B.sbuf("sgnk_s", [128, 3, 9], F32)
        self.s5dd_s = B.sbuf("s5dd_s", [128, nL, 64], F32)
        self.Rs5d = Region()
        self.finaln_s = finaln_s
        self.yout = yout
        self.Rxs = [regions(KT), regions(KT)]
        self.S_f = [B.sbuf("S_f%d" % d, [128, 128], F32) for d in range(2)]
        self.S_b = [B.sbuf("S_b%d" % d, [128, 128], BF16) for d in range(2)]
        self.RS = regions(2)
        ARENA = 22 * 1024
        self.arena = B.sbuf("arena", [128, ARENA], F32)
        self.Rarena = Region()

        B.dma("sp", cm[:], cmat, writes=[self.Rconst])
        B.dma("sp", self.sgnk[:], sgnk_d, writes=[self.Rconst])
        B.op("dve", lambda e: e.memset(self.ones_bf[:], 1.0), writes=[self.Rconst])
        B.op("dve", lambda e: e.memset(self.eps_t[:, 0:1], EPS), writes=[self.Rconst])
        B.op("dve", lambda e: e.memset(self.eps_t[:, 1:2], 1.0), writes=[self.Rconst])
        self.cp(self.ident_bf[:], cm[:, 0, :], reads=[self.Rconst], writes=[self.Rconst])
        B.op("dve", lambda e: e.memset(self.resetm[:], 1.0), writes=[self.Rconst])
        B.op("dve", lambda e: e.memset(self.resetm[:].rearrange("p (c j) -> p c j", j=32)[:, :, 0:1], 0.0), reads=[self.Rconst], writes=[self.Rconst])
        for (dst, src) in ((cond_s, cond), (bmod_s, bmod), (normw_s, normw), (finaln_s, finaln), (qnw_s, qnw), (knw_s, knw),
                           (lbl_s, lblog), (self.onw_s, onw), (self.s5dd_s, s5dd)):
            B.dma("sp", dst[:], src, writes=[Rsm])

        self.act(lbl_s[:], lbl_s[:], AF.Exp, reads=[Rsm], writes=[Rsm])
        B.op("dve", lambda e: e.tensor_reduce(out=lbsum[:], in_=lbl_s[:], axis=AX.X, op=ALU.add), reads=[Rsm], writes=[Rsm])
        self.recip(lbsum[:], lbsum[:], reads=[Rsm], writes=[Rsm])
        self.tt(lbl_s[:], lbl_s[:], lbsum[:].unsqueeze(3).to_broadcast([128, 2, 8, DEPTH]), ALU.mult, reads=[Rsm], writes=[Rsm])
        B.op("dve", lambda e: e.memset(self.lb[:, :, :, 0:1], 0.0), writes=[Rsm])
        for li in range(1, DEPTH):
            self.tt(self.lb[:, :, :, li:li + 1], self.lb[:, :, :, li - 1:li], lbl_s[:, :, :, li:li + 1], ALU.add, reads=[Rsm], writes=[Rsm])
        self.ts(self.oml[:], self.lb[:], -1.0, 1.0, ALU.mult, ALU.add, reads=[Rsm], writes=[Rsm])

        tmpc = B.sbuf("tmpc", [128, KT, 2], F32)
        self.act(tmpc[:], cond_s[:], AF.Exp, reads=[Rsm], writes=[Rsm], scale=-1.0)
        self.sig_from_exp(tmpc[:], Rsm)
        self.tt(csilu[:], cond_s[:], tmpc[:], ALU.mult, reads=[Rsm], writes=[Rsm])
        Rmod = Region()
        for l in ([] if self.cfg.get("skip_mod") else self.layers):
            for j in range(48):
                w, Rw = self.wload(wmod[l, j], KT * 128)
                P_, R_ = self.ps()
                for kt in range(KT):
                    self.mm(P_[:, 0:2], w[:, kt * 128:(kt + 1) * 128], csilu[:, kt, :], kt == 0, kt == KT - 1,
                            reads=[Rw, Rsm], writes=[R_])
                self.ts(modT[:, l, j, :], P_[:, 0:2], bmod_s[:, l, j:j + 1], None, ALU.add, None,
                        reads=[R_, Rsm], writes=[Rmod])
            for r in range(2):
                self.stt(amod[:, l, :, r], modT[:, l, 16:32, r], 1.0, normw_s[:, l, :], ALU.add, ALU.mult,
                         reads=[Rmod, Rsm], writes=[Rmod])
        self.modT, self.amod, self.Rmod = modT, amod, Rmod

        if self.cfg.get("do_s5", True):
            for l in self.layers:
                self.s5_prep(l)
        for ps_ in self.passes:
            r = ps_
            for li, l in enumerate(self.layers):
                xsrc = xin if li == 0 else self.xs
                self.norm_phase(xsrc, ps_, l, r)
                if self.cfg.get("do_hgrn", True):
                    self.hgrn_phase(ps_, l)
                if self.cfg.get("do_attn", True):
                    self.attn_phase(ps_, l)
                if self.cfg.get("do_s5", True) and not self.cfg.get("s5_prep_only"):
                    self.s5_phase(ps_, l)
                if self.cfg.get("do_merge", True):
                    self.merge_phase(xsrc, ps_, l, r)
            if self.cfg.get("do_merge", True):
                self.final_phase(ps_)
        B.wait_all("sp")
        B.emit()

    def norm_phase(self, xsrc, ps_, l, r):
        B = self.B
        B.fence()
        xst = self.arena[:, 0:8192].rearrange("p (k t) -> p k t", k=KT)
        sq = self.arena[:, 8192:12288].bitcast(BF16).rearrange("p (k t) -> p k t", k=KT)
        rstd = self.arena[:, 12288:12800]
        tmp = self.arena[:, 12800:13312]
        Rx, Rsq, Rr, Rt = Region(), Region(), Region(), Region()
        for c in range(2):
            B.dma("sp", xst, xsrc[ps_, :, :, c * 512:(c + 1) * 512].rearrange("k p t -> p k t"),
                  reads=self.Rxs[ps_], writes=[Rx])
            self.act(sq, xst, AF.Square, reads=[Rx], writes=[Rsq])
            self.rstd_from_sq(lambda kt: sq[:, kt, :], KT, [Rsq], float(D), rstd, Rr, 512)
            for kt in range(KT):
                self.stt(tmp, xst[:, kt, :], self.amod[:, l, kt, r:r + 1], rstd, ALU.mult, ALU.mult,
                         reads=[Rx, Rr, self.Rmod], writes=[Rt])
                self.act(self.h[:, kt, c * 512:(c + 1) * 512], tmp, AF.Identity, reads=[Rt, self.Rmod], writes=[self.Rh],
                         scale=1.0, bias=self.modT[:, l, kt, r:r + 1])


    def s5_prep(self, l):
        B = self.B
        B.fence()
        A = self.arena
        off = [0]

        def alloc(n):
            a = A[:, off[0]:off[0] + n]
            off[0] += n
            assert off[0] <= 22 * 1024, off[0]
            return a

        TWO_PI = 2.0 * math.pi
        prm = alloc(192).rearrange("p (q g) -> p q g", q=3)
        dt = alloc(64); lre = alloc(64); lim = alloc(64)
        negpi = alloc(1)
        cre = alloc(64); cim = alloc(64); t64a = alloc(64); t64b = alloc(64); den = alloc(64)
        Fre = alloc(64); Fim = alloc(64); Gre = alloc(64); Gim = alloc(64)
        tre = alloc(3 * 576).rearrange("p (q k g) -> p q k g", q=3, k=9)
        tim = alloc(3 * 576).rearrange("p (q k g) -> p q k g", q=3, k=9)
        mark = off[0]
        ang = alloc(3 * 576).rearrange("p (q k g) -> p q k g", q=3, k=9)
        yv = alloc(3 * 576); fr = alloc(3 * 576); msk = alloc(3 * 576); mag = alloc(3 * 576)
        ki = msk.bitcast(I32)
        R = Region()

        B.dma("sp", prm, self.s5p[l], writes=[R])
        B.op("dve", lambda e: e.memset(negpi, -math.pi), writes=[R])
        self.act(dt, prm[:, 2, :], AF.Exp, reads=[R], writes=[R])
        self.tt(lre, prm[:, 0, :], dt, ALU.mult, reads=[R], writes=[R])
        self.tt(lim, prm[:, 1, :], dt, ALU.mult, reads=[R], writes=[R])
        for q in range(3):
            self.tt(ang[:, q], lim.unsqueeze(1).to_broadcast([128, 9, 64]),
                    self.sgnk[:, q, :].unsqueeze(2).to_broadcast([128, 9, 64]), ALU.mult, reads=[R, self.Rconst], writes=[R])
            self.tt(tre[:, q], lre.unsqueeze(1).to_broadcast([128, 9, 64]),
                    self.sgnk[:, q, :].unsqueeze(2).to_broadcast([128, 9, 64]), ALU.mult, reads=[R, self.Rconst], writes=[R])
        angf = ang.rearrange("p q k g -> p (q k g)")
        tref = tre.rearrange("p q k g -> p (q k g)")
        timf = tim.rearrange("p q k g -> p (q k g)")
        self.act(mag, tref, AF.Exp, reads=[R], writes=[R])

        def sin_of(dst, shift):
            self.ts(yv, angf, 1.0 / TWO_PI, 64.5 + shift, ALU.mult, ALU.add, reads=[R], writes=[R])
            self.cp(ki, yv, reads=[R], writes=[R])
            self.cp(fr, ki, reads=[R], writes=[R])
            self.tt(fr, yv, fr, ALU.subtract, reads=[R], writes=[R])
            B.op("dve", lambda e: e.tensor_single_scalar(out=msk, in_=fr, scalar=0.0, op=ALU.is_lt), reads=[R], writes=[R])
            self.tt(fr, fr, msk, ALU.add, reads=[R], writes=[R])
            self.act(dst, fr, AF.Sin, reads=[R], writes=[R], scale=TWO_PI, bias=negpi)

        if self.cfg.get("prep_stop", 9) <= 1:
            return
        sin_of(timf, 0.0)
        sin_of(tref, 0.25)
        if self.cfg.get("prep_stop", 9) <= 2:
            return
        self.tt(timf, timf, mag, ALU.mult, reads=[R], writes=[R])
        self.tt(tref, tref, mag, ALU.mult, reads=[R], writes=[R])
        are_, aim_ = prm[:, 0, :], prm[:, 1, :]
        a1r, a1i = tre[:, 2, 1, :], tim[:, 2, 1, :]
        self.tt(den, are_, are_, ALU.mult, reads=[R], writes=[R])
        self.tt(t64a, aim_, aim_, ALU.mult, reads=[R], writes=[R])
        self.tt(den, den, t64a, ALU.add, reads=[R], writes=[R])
        self.recip(den, den, reads=[R], writes=[R])
        self.ts(t64a, a1r, -1.0, None, ALU.add, None, reads=[R], writes=[R])
        self.tt(cre, t64a, are_, ALU.mult, reads=[R], writes=[R])
        self.tt(t64b, a1i, aim_, ALU.mult, reads=[R], writes=[R])
        self.tt(cre, cre, t64b, ALU.add, reads=[R], writes=[R])
        self.tt(cre, cre, den, ALU.mult, reads=[R], writes=[R])
        self.tt(cim, a1i, are_, ALU.mult, reads=[R], writes=[R])
        self.tt(t64b, t64a, aim_, ALU.mult, reads=[R], writes=[R])
        self.tt(cim, cim, t64b, ALU.subtract, reads=[R], writes=[R])
        self.tt(cim, cim, den, ALU.mult, reads=[R], writes=[R])
        a8st = yv[:, 0:128].rearrange("p (r m q) -> p r m q", r=2, m=2)
        for ri, tab in enumerate((tre, tim)):
            self.cp(a8st[:, ri], tab[:, 2, 8, :].rearrange("p (q m) -> p m q", m=2), reads=[R], writes=[R])
        for ri in range(2):
            for d in range(2):
                for m in range(2):
                    B.dma("sp", self.a8_d[l, ri, m * 64:(m + 1) * 64, d * 32:(d + 1) * 32],
                          a8st[d * 64:(d + 1) * 64, ri, m, :], reads=[R], writes=[self.Rs5d])
        B.op("dve", lambda e: e.memset(Fre[64:128, :], 1.0), writes=[R])
        B.op("dve", lambda e: e.memset(Fim[64:128, :], 0.0), writes=[R])
        self.cp(Fre[0:64, :], tre[0:64, 2, 7, :], reads=[R], writes=[R])
        self.cp(Fim[0:64, :], tim[0:64, 2, 7, :], reads=[R], writes=[R])
        self.cp(Gre[0:64, :], tre[0:64, 2, 1, :], reads=[R], writes=[R])
        self.cp(Gim[0:64, :], tim[0:64, 2, 1, :], reads=[R], writes=[R])
        self.cp(Gre[64:128, :], tre[64:128, 2, 8, :], reads=[R], writes=[R])
        self.cp(Gim[64:128, :], tim[64:128, 2, 8, :], reads=[R], writes=[R])
        if self.cfg.get("prep_stop", 9) <= 3:
            return
        B.fence()
        off[0] = mark
        Bre = alloc(1024).rearrange("p (g c) -> p g c", g=64); Bim = alloc(1024).rearrange("p (g c) -> p g c", g=64)
        Cre = alloc(1024).rearrange("p (g c) -> p g c", g=64); Cim = alloc(1024).rearrange("p (g c) -> p g c", g=64)
        bbr = alloc(1024).rearrange("p (g c) -> p g c", g=64); bbi = alloc(1024).rearrange("p (g c) -> p g c", g=64)
        X = [alloc(1024).rearrange("p (g k c) -> p g k c", g=8, k=8) for _ in range(6)]
        t1 = alloc(1024).rearrange("p (g k c) -> p g k c", g=8, k=8)
        t2 = alloc(1024).rearrange("p (g k c) -> p g k c", g=8, k=8)
        tA = alloc(128); tB = alloc(128)
        toep_st = alloc(512).bitcast(BF16).rearrange("p (g x) -> p g x", g=8)
        wb_st = alloc(1024).bitcast(BF16).rearrange("p (g r x) -> p g r x", g=8, r=2)
        ca_st = alloc(1024).bitcast(BF16).rearrange("p (r g x) -> p r g x", r=2, g=8)
        RX = regions(6); Rt1, Rt2, RtA, RtB, Rtoep, Rwb, Rca = [Region() for _ in range(7)]
        B.dma("sp", Bre.rearrange("p g c -> p (g c)"), self.s5b[l, 0], writes=[R])
        B.dma("sp", Bim.rearrange("p g c -> p (g c)"), self.s5b[l, 1], writes=[R])
        B.dma("sp", Cre.rearrange("p g c -> p (g c)"), self.s5c[l, 0], writes=[R])
        B.dma("sp", Cim.rearrange("p g c -> p (g c)"), self.s5c[l, 1], writes=[R])
        bc = lambda v: v.unsqueeze(2).to_broadcast([128, 64, 16])
        self.tt(bbr, Bre, bc(cre), ALU.mult, reads=[R], writes=[R])
        self.tt(bbi, Bim, bc(cim), ALU.mult, reads=[R], writes=[R])
        self.tt(bbr, bbr, bbi, ALU.subtract, reads=[R], writes=[R])
        self.tt(bbi, Bim, bc(cre), ALU.mult, reads=[R], writes=[R])
        self.tt(Bim, Bre, bc(cim), ALU.mult, reads=[R], writes=[R])
        self.tt(bbi, bbi, Bim, ALU.add, reads=[R], writes=[R])

        def cmul(dre, dim_, Rd, are, aim, bre, bim, rds, negate_im=False):
            self.tt(dre, are, bre, ALU.mult, reads=rds, writes=[Rd[0]])
            self.tt(t1, aim, bim, ALU.mult, reads=rds, writes=[Rt1])
            self.tt(dre, dre, t1, ALU.subtract, reads=[Rd[0], Rt1], writes=[Rd[0]])
            self.tt(dim_, are, bim, ALU.mult, reads=rds, writes=[Rd[1]])
            self.tt(t2, aim, bre, ALU.mult, reads=rds, writes=[Rt2])
            if negate_im:
                self.stt(dim_, dim_, -1.0, t2, ALU.mult, ALU.subtract, reads=[Rd[1], Rt2], writes=[Rd[1]])
            else:
                self.tt(dim_, dim_, t2, ALU.add, reads=[Rd[1], Rt2], writes=[Rd[1]])

        if self.cfg.get("prep_stop", 9) <= 4:
            return
        for gb in range(8):
            g0 = gb * 8
            tabk = lambda tab, q: tab[:, q, 0:8, g0:g0 + 8].rearrange("p k g -> p g k").unsqueeze(3).to_broadcast([128, 8, 8, 16])
            gk = lambda v: v[:, g0:g0 + 8, :].unsqueeze(2).to_broadcast([128, 8, 8, 16])
            fg = lambda v: v[:, g0:g0 + 8].unsqueeze(2).unsqueeze(3).to_broadcast([128, 8, 8, 16])
            Lre, Lim, Rre, Rim, CAre, CAim = X
            cmul(Lre, Lim, (RX[0], RX[1]), tabk(tre, 1), tabk(tim, 1), gk(bbr), gk(bbi), [R])
            cmul(Rre, Rim, (RX[2], RX[3]), tabk(tre, 0), tabk(tim, 0), gk(Cre), gk(Cim), [R])
            cmul(CAre, CAim, (RX[4], RX[5]), fg(Gre), fg(Gim), Rre, Rim, [R, RX[2], RX[3]], negate_im=True)
            self.ts(Rim, Rim, -1.0, None, ALU.mult, None, reads=[RX[3]], writes=[RX[3]])
            if self.cfg.get("prep_stop", 9) <= 5:
                return
            for ri, src in enumerate((CAre, CAim)):
                self.act(ca_st[:, ri], src.rearrange("p g k c -> p g (k c)"), AF.Copy, reads=[RX[4 + ri]], writes=[Rca])
            for ri in range(2):
                for d in range(2):
                    B.dma("sp", self.ca_d[l, d, ri, g0:g0 + 8].rearrange("g p x -> p g x"),
                          ca_st[d * 64:(d + 1) * 64, ri], reads=[Rca], writes=[self.Rs5d])
            f2 = lambda x, gl: x[:, gl].rearrange("p k c -> p (k c)")
            for gl in range(8):
                g = g0 + gl
                Pfs = [self.ps(), self.ps()]
                for d in range(2):
                    sl = slice(d * 64, (d + 1) * 64)
                    Pf, Rpf = Pfs[d]
                    self.mm(Pf[:, 0:128], f2(Lre, gl)[sl, :], f2(Rre, gl)[sl, :], True, False, reads=[RX[0], RX[2]], writes=[Rpf])
                    self.mm(Pf[:, 0:128], f2(Lim, gl)[sl, :], f2(Rim, gl)[sl, :], False, True, reads=[RX[1], RX[3]], writes=[Rpf])
                self.tt(tA, Pfs[0][0][:, 0:128], self.cm[:, 5, :], ALU.mult, reads=[Pfs[0][1], self.Rconst], writes=[RtA])
                self.tt(tB, Pfs[1][0][:, 0:128], self.cm[:, 6, :], ALU.mult, reads=[Pfs[1][1], self.Rconst], writes=[RtB])
                self.tt(tA, tA, tB, ALU.add, reads=[RtA, RtB], writes=[RtA])
                self.stt(toep_st[:, gl, :], self.cm[:, 0, :], self.s5dd_s[:, l, g:g + 1], tA, ALU.mult, ALU.add,
                         reads=[RtA, self.Rconst, self.Rsm], writes=[Rtoep])
            B.dma("sp", self.toep_d[l, g0:g0 + 8].rearrange("g p x -> p g x"), toep_st, reads=[Rtoep], writes=[self.Rs5d])
            if self.cfg.get("prep_stop", 9) <= 6:
                return
            cmul(Rre, Rim, (RX[2], RX[3]), fg(Fre), fg(Fim), Lre, Lim, [R, RX[0], RX[1]])
            for gl in range(8):
                Pf, Rpf = self.ps()
                for ri, src in enumerate((Rre, Rim)):
                    B.op("pe", lambda e, src=src, ri=ri, Pf=Pf, gl=gl: e.transpose(Pf[:, ri * 128:(ri + 1) * 128],
                                                                                  src[:, gl].rearrange("p k c -> p (k c)"), self.cm[:, 0, :]),
                         reads=[RX[2 + ri], self.Rconst], writes=[Rpf])
                self.act(wb_st[:, gl].rearrange("p r x -> p (r x)"), Pf[:, 0:256], AF.Copy, reads=[Rpf], writes=[Rwb])
            B.dma("sp", self.wb_d[l, g0:g0 + 8].rearrange("g p r x -> p g r x"), wb_st, reads=[Rwb], writes=[self.Rs5d])

    def s5_phase(self, ps_, l):
        B = self.B
        B.fence()
        A = self.arena
        win = self.win
        NS, L = (1, 1024) if ps_ == 0 else (4, 256)
        NL = L // 8
        off = [0]

        def alloc(n):
            a = A[:, off[0]:off[0] + n]
            off[0] += n
            return a

        bufA = alloc(4096).bitcast(BF16)
        bufB = alloc(4096).bitcast(BF16)
        u_dl = bufA.rearrange("p (t j n) -> p t j n", t=8, j=8)
        y_unf = bufA.rearrange("p (g n) -> p g n", g=64)
        U_unf = bufB.rearrange("p (g n) -> p g n", g=64)
        y_dl = bufB.rearrange("p (t x) -> p t x", t=8)
        WE = alloc(NS * (NL + 1) * 64).bitcast(BF16).rearrange("p (s n j r) -> p s n j r", s=NS, n=NL + 1, j=64)
        st = alloc(NS * 128).rearrange("p (s j r) -> p s j r", s=NS, j=64)
        t1 = alloc(NS * 128).rearrange("p (s j r) -> p s j r", s=NS, j=64)
        t2 = alloc(NS * 128).rearrange("p (s j r) -> p s j r", s=NS, j=64)
        nw = alloc(NS * 128).rearrange("p (s j r) -> p s j r", s=NS, j=64)
        a8 = alloc(128).rearrange("p (r j) -> p r j", r=2)
        NR = 2
        s5r = [alloc(320 * 2).bitcast(BF16) for _ in range(NR)]
        tmpf = alloc(1024)
        tmpg = alloc(1024)
        assert off[0] <= 22 * 1024, off[0]
        Ru, RU, Ry, Ryd, RWE, Rst, Rt1, Rt2, Rnw, Ra8, Rtf, Rtg = [Region() for _ in range(12)]
        Rs5r = regions(NR)
        hfn = lambda kt, c: self.h[:, kt, c * 512:(c + 1) * 512]
        sgate = self.o_s5

        for pt in range(8):
            w, Rw = self.wload(win[l, CT_SU + pt], KT * 128)

            def evu(c, P_, R_, pt=pt):
                self.act(u_dl[:, pt, :, c * 64:(c + 1) * 64].rearrange("p j n -> p n j"),
                         P_[:, :].rearrange("p (n j) -> p n j", j=8), AF.Copy, reads=[R_], writes=[Ru])

            self.proj_fm(w, Rw, KT, hfn, [self.Rh], evu)
            w, Rw = self.wload(win[l, CT_SG + pt], KT * 128)

            def evg(c, P_, R_, pt=pt):
                sl = slice(c * 512, (c + 1) * 512)
                self.act(tmpf[:, 0:512], P_[:, :], AF.Exp, reads=[R_], writes=[Rtf], scale=-1.0)
                self.sig_from_exp(tmpf[:, 0:512], Rtf)
                self.tt(sgate[:, pt, sl], P_[:, :], tmpf[:, 0:512], ALU.mult, reads=[R_, Rtf], writes=[self.Ros5[pt]])

            self.proj_fm(w, Rw, KT, hfn, [self.Rh], evg)
        Rud = Region()
        for pt in range(8):
            B.dma("sp", self.ud_d[pt * 8:(pt + 1) * 8].rearrange("g c i n -> (g c) i n"), u_dl[:, pt], reads=[Ru], writes=[Rud])
        for i in range(8):
            B.dma("sp", U_unf[i * 16:(i + 1) * 16, :, :], self.ud_d[:, :, i, :].rearrange("g c n -> c g n"), reads=[Rud], writes=[RU])
        B.dma("sp", a8, self.a8_d[l].rearrange("r p j -> p r j"), reads=[self.Rs5d], writes=[Ra8])
        if ps_ == 0:
            B.dma("sp", st[:, 0], self.s5s0[l], writes=[Rst])
        else:
            B.op("dve", lambda e: e.memset(st, 0.0), writes=[Rst])
        self.act(WE[:, :, 0, :, :], st, AF.Copy, reads=[Rst], writes=[RWE])

        def load_pair(q, what):
            k = self.s5_rr
            self.s5_rr = (self.s5_rr + 1) % NR
            buf, Rb = s5r[k], Rs5r[k]
            if what == "wb":
                v = buf[:, 0:512].rearrange("p (m r x) -> p m r x", m=2, r=2)
                B.dma("sp", v, self.wb_d[l, 2 * q:2 * q + 2].rearrange("m p r x -> p m r x"), reads=[self.Rs5d], writes=[Rb])
                return v, Rb
            vt = buf[:, 0:256].rearrange("p (m x) -> p m x", m=2)
            B.dma("sp", vt, self.toep_d[l, 2 * q:2 * q + 2].rearrange("m p x -> p m x"), reads=[self.Rs5d], writes=[Rb])
            vc = buf[:, 256:768].rearrange("p (d r x) -> p d r x", d=2, r=2)
            B.dma("sp", vc, self.ca_d[l, :, :, 2 * q:2 * q + 2].rearrange("d r m p x -> (m p) d r x"), reads=[self.Rs5d], writes=[Rb])
            return (vt, vc), Rb

        self.s5_rr = 0
        for q in range(32):
            wbv, Rb = load_pair(q, "wb")
            Pw, Rpw = self.ps()
            for m in range(2):
                g = 2 * q + m
                for d in range(2):
                    for ri in range(2):
                        blk = (d * 2 + ri) * 128
                        self.mm(Pw[m * 64:(m + 1) * 64, blk:blk + 128], wbv[:, m, ri, d * 64:(d + 1) * 64], U_unf[:, g, :], True, True,
                                reads=[Rb, RU], writes=[Rpw])
            for d in range(2):
                src = Pw[:, d * 256:(d + 1) * 256].rearrange("p (r s n) -> p s n r", r=2, s=NS)
                if d == 0:
                    dst = WE[:, :, 1:NL + 1, q, :]
                else:
                    dst = WE[:, :, NL:0:-1, 32 + q, :]
                self.cp(dst, src, reads=[Rpw], writes=[RWE])
        ar = a8[:, 0, :].unsqueeze(1).unsqueeze(3).to_broadcast([128, NS, 64, 2])
        ai = a8[:, 1, :].unsqueeze(1).unsqueeze(3).to_broadcast([128, NS, 64, 2])
        for k in range(NL):
            self.tt(t1, st, ar, ALU.mult, reads=[Rst, Ra8], writes=[Rt1])
            self.tt(t2, st, ai, ALU.mult, reads=[Rst, Ra8], writes=[Rt2])
            self.tt(nw[:, :, :, 0], t1[:, :, :, 0], t2[:, :, :, 1], ALU.subtract, reads=[Rt1, Rt2], writes=[Rnw])
            self.tt(nw[:, :, :, 1], t1[:, :, :, 1], t2[:, :, :, 0], ALU.add, reads=[Rt1, Rt2], writes=[Rnw])
            self.tt(st, nw, WE[:, :, k + 1, :, :], ALU.add, reads=[Rnw, RWE], writes=[Rst])
            self.act(WE[:, :, k + 1, :, :], st, AF.Copy, reads=[Rst], writes=[RWE])
        if ps_ == 1:
            B.dma("sp", self.ns5[l], st, reads=[Rst])
        for q in range(32):
            (tv, cv_), Rb = load_pair(q, "tc")
            if q % 4 == 0:
                Pys = [self.ps(), self.ps()]
            for m in range(2):
                g = 2 * q + m
                Py, Rpy = Pys[m]
                blk = (q % 4) * 128
                self.mm(Py[:, blk:blk + 128], tv[:, m, :], U_unf[:, g, :], True, False, reads=[Rb, RU], writes=[Rpy])
                msl = slice(m * 64, (m + 1) * 64)
                for d in range(2):
                    for ri in range(2):
                        if d == 0:
                            rhs = WE[msl, :, 0:NL, q, ri]
                        else:
                            rhs = WE[msl, :, 0:NL, 32 + q, ri][:, :, ::-1]
                        self.mm(Py[:, blk:blk + 128], cv_[msl, d, ri, :], rhs, False, d == 1 and ri == 1, reads=[Rb, RWE], writes=[Rpy])
            if q % 4 == 3:
                g0 = 2 * (q - 3)
                for m in range(2):
                    Py, Rpy = Pys[m]
                    self.act(y_unf[:, g0 + m:g0 + 8:2, :], Py[:, :].rearrange("p (g n) -> p g n", g=4), AF.Copy,
                             reads=[Rpy, Rud], writes=[Ry, Ru])
        B.dma("sp", self.yd_d, y_unf, reads=[Ry], writes=[Ryd])
        Ryl = Region()
        for gl in range(8):
            B.dma("sp", y_dl[gl * 16:(gl + 1) * 16].rearrange("c t (j n) -> c t j n", j=8),
                  self.yd_d.rearrange("(j c) (t g) n -> g c t j n", c=16, g=8)[gl], reads=[Ryd], writes=[Ryl, RU])
        C2 = 2.0 * math.sqrt(2.0 / math.pi)
        a_, b_ = tmpf[:, 0:1024], tmpg[:, 0:1024]
        for pt in range(8):
            y = y_dl[:, pt, :]
            self.tt(a_, y, y, ALU.mult, reads=[Ryl], writes=[Rtf])
            self.ts(a_, a_, 0.044715, 1.0, ALU.mult, ALU.add, reads=[Rtf], writes=[Rtf])
            self.tt(a_, a_, y, ALU.mult, reads=[Rtf, Ryl], writes=[Rtf])
            self.ts(a_, a_, -25.0, None, ALU.max, None, reads=[Rtf], writes=[Rtf])
            self.act(b_, a_, AF.Exp, reads=[Rtf], writes=[Rtg], scale=-C2)
            self.sig_from_exp(b_, Rtg)
            self.tt(y, y, b_, ALU.mult, reads=[Ryl, Rtg], writes=[Ryl])
        for f in range(8):
            w, Rw = self.wload(self.wglu[l, f], 8 * 128)

            def evglu(c, P_, R_, f=f):
                self.act(tmpf[:, 0:512], P_[:, :], AF.Exp, reads=[R_], writes=[Rtf], scale=-1.0)
                self.sig_from_exp(tmpf[:, 0:512], Rtf)
                self.tt(tmpg[:, 0:512], y_dl[:, f, c * 512:(c + 1) * 512], tmpf[:, 0:512], ALU.mult, reads=[Ryl, Rtf], writes=[Rtg])
                ov = self.o_s5[:, f, :].rearrange("p (n j) -> p j n", j=8)[:, 4 * c:4 * c + 4, :]
                self.tt(ov, tmpg[:, 0:512].rearrange("p (j n) -> p j n", j=4), ov, ALU.mult, reads=[Rtg, self.Ros5[f]], writes=[self.Ros5[f]])

            self.proj_fm(w, Rw, 8, lambda kt, c: y_dl[:, kt, c * 512:(c + 1) * 512], [Ryl], evglu)

    def merge_phase(self, xsrc, ps_, l, r):
        B = self.B
        B.fence()
        A = self.arena
        merged = A[:, 0:8192].bitcast(BF16).rearrange("p (k t) -> p k t", k=KT)
        acc = A[:, 8192:9216]
        tmp = A[:, 9216:9728]
        tmp2 = A[:, 9728:10240]
        xt = [A[:, 10240:11264], A[:, 11264:12288]]
        Rm, Racc, Rt, Rt2 = Region(), Region(), Region(), Region()
        Rxt = regions(2)
        branches = ((self.o_hg, self.Rohg), (self.o_at, self.Roat), (self.o_s5, self.Ros5))
        hfn = lambda kt, c: self.h[:, kt, c * 512:(c + 1) * 512]
        for f in range(KT):
            for b, (ob, Rob) in enumerate(branches):
                w, Rw = self.wload(self.wbr[l, b, f], 8 * 128)
                Pb = [self.ps() for _ in range(2)]
                for c in range(2):
                    for kt in range(8):
                        self.mm(Pb[c][0][:, :], w[:, kt * 128:(kt + 1) * 128], ob[:, kt, c * 512:(c + 1) * 512], kt == 0, kt == 7,
                                reads=[Rw, Rob[kt]], writes=[Pb[c][1]])
                w2, Rw2 = self.wload(self.win[l, CT_MG + b * 16 + f], KT * 128)

                def evm(c, P_, R_, b=b, Pb=Pb):
                    sl = slice(c * 512, (c + 1) * 512)
                    self.act(tmp, P_[:, :], AF.Exp, reads=[R_], writes=[Rt], scale=-1.0)
                    self.sig_from_exp(tmp, Rt)
                    if b == 0:
                        self.tt(acc[:, sl], Pb[c][0][:, :], tmp, ALU.mult, reads=[Pb[c][1], Rt], writes=[Racc])
                    else:
                        self.tt(tmp2, Pb[c][0][:, :], tmp, ALU.mult, reads=[Pb[c][1], Rt], writes=[Rt2])
                        self.tt(acc[:, sl], acc[:, sl], tmp2, ALU.add, reads=[Racc, Rt2], writes=[Racc])

                self.proj_fm(w2, Rw2, KT, hfn, [self.Rh], evm)
            self.act(merged[:, f, :], acc, AF.Copy, reads=[Racc], writes=[Rm])
        for f in range(KT):
            w, Rw = self.wload(self.wout[l, f], KT * 128)
            k = f % 2
            B.dma("sp", xt[k], xsrc[ps_, f], reads=[self.Rxs[ps_][f]], writes=[Rxt[k]])

            def evo(c, P_, R_, f=f, k=k):
                sl = slice(c * 512, (c + 1) * 512)
                self.stt(xt[k][:, sl], P_[:, :], self.modT[:, l, 32 + f, r:r + 1], xt[k][:, sl], ALU.mult, ALU.add,
                         reads=[R_, self.Rmod, Rxt[k]], writes=[Rxt[k]])

            self.proj_fm(w, Rw, KT, lambda kt, c: merged[:, kt, c * 512:(c + 1) * 512], [Rm], evo)
            B.dma("sp", self.xs[ps_, f], xt[k], reads=[Rxt[k]], writes=[self.Rxs[ps_][f]])

    def final_phase(self, ps_):
        B = self.B
        B.fence()
        xst = self.arena[:, 0:8192].rearrange("p (k t) -> p k t", k=KT)
        sq = self.arena[:, 8192:12288].bitcast(BF16).rearrange("p (k t) -> p k t", k=KT)
        rstd = self.arena[:, 12288:12800]
        Rx, Rsq, Rr = Region(), Region(), Region()
        for c in range(2):
            B.dma("sp", xst, self.xs[ps_, :, :, c * 512:(c + 1) * 512].rearrange("k p t -> p k t"),
                  reads=self.Rxs[ps_], writes=[Rx])
            self.act(sq, xst, AF.Square, reads=[Rx], writes=[Rsq])
            self.rstd_from_sq(lambda kt: sq[:, kt, :], KT, [Rsq], float(D), rstd, Rr, 512)
            for kt in range(KT):
                self.stt(xst[:, kt, :], xst[:, kt, :], self.finaln_s[:, kt:kt + 1], rstd, ALU.mult, ALU.mult,
                         reads=[Rx, Rr, self.Rsm], writes=[Rx])
            B.dma("sp", self.yout[ps_, :, :, c * 512:(c + 1) * 512].rearrange("k p t -> p k t"), xst, reads=[Rx])

    def ps_grp(self, grp):
        lo, n = grp
        i = lo + self.ps_grr.get(grp, 0)
        self.ps_grr[grp] = (self.ps_grr.get(grp, 0) + 1) % n
        return self.PS[i], self.RPS[i]

    def hgrn_phase(self, ps_, l):
        B = self.B
        B.fence()
        self.ps_grr = {}
        A = self.arena
        NS, L = (1, 1024) if ps_ == 0 else (4, 256)
        off = [0]

        def alloc(n):
            a = A[:, off[0]:off[0] + n]
            off[0] += n
            assert off[0] <= 22 * 1024, off[0]
            return a

        X1 = self.o_at[:, :, :].rearrange("p k t -> p (k t)").bitcast(F32)
        X2 = self.o_s5[:, :, :].rearrange("p k t -> p (k t)").bitcast(F32)
        bf = lambda a: a.bitcast(BF16)
        k4 = lambda a: a.bitcast(BF16).rearrange("p (t c d) -> p t c d", t=8, c=4)
        S = []
        S.append(dict(qt=[bf(alloc(512)), bf(alloc(512))], ktl=[bf(alloc(512)), bf(alloc(512))],
                      khat4=[k4(alloc(2048)), k4(alloc(2048))],
                      vtok=bf(alloc(512)).rearrange("p (t c) -> p t c", t=8), sg=alloc(1024), etot=[alloc(32), alloc(32)]))
        S.append(dict(qt=[bf(X2[:, 0:512]), bf(X2[:, 512:1024])], ktl=[bf(X2[:, 1024:1536]), bf(X2[:, 1536:2048])],
                      khat4=[k4(X1[:, 0:2048]), k4(X1[:, 2048:4096])],
                      vtok=bf(X2[:, 2048:2560]).rearrange("p (t c) -> p t c", t=8), sg=X2[:, 2560:3584], etot=[alloc(32), alloc(32)]))
        for st_ in S:
            st_["R"] = dict(qt=regions(2), ktl=regions(2), khat4=regions(2), vtok=Region(), sg=Region(), etot=regions(2))
        q_f = alloc(1024); tmp = alloc(1024); tmp2 = alloc(1024)
        tmpD = [tmp, alloc(1024)]; tmp2D = [tmp2, alloc(1024)]
        RtD = [Region(), Region()]; Rt2D = [Region(), Region()]
        bb = [alloc(1024), alloc(1024)]; kk = [alloc(1024), alloc(1024)]
        khat = [bf(alloc(512)), bf(alloc(512))]
        Rq, Rt, Rt2 = Region(), Region(), Region()
        Rb, Rk, Rkh = regions(2), regions(2), regions(2)
        o_d = [alloc(1024), alloc(1024)]
        masked = [bf(alloc(64)), bf(alloc(64))]
        sqo = bf(X2[:, 3584:4096]); rstd = alloc(1024); ntmp = alloc(1024)
        Ro, Rms = regions(2), regions(2)
        Rsq, Rr, Rnt = Region(), Region(), Region()
        win = self.win
        hfn = lambda kt, c: self.h[:, kt, c * 512:(c + 1) * 512]
        G_PREP, G_REC = (4, 4), (2, 2)

        def proj_fm(w, Rw, evac):
            banks = [self.ps_grp(G_PREP) for _ in range(2)]
            for c in range(2):
                P_, R_ = banks[c]
                for kt in range(KT):
                    self.mm(P_[:, :], w[:, kt * 128:(kt + 1) * 128], hfn(kt, c), kt == 0, kt == KT - 1, reads=[Rw, self.Rh], writes=[R_])
            for c in range(2):
                evac(c, *banks[c])

        def prep(hd):
            D_ = S[hd % 2]
            RD = D_["R"]
            qt, ktl, khat4, vtok, sg, etot = D_["qt"], D_["ktl"], D_["khat4"], D_["vtok"], D_["sg"], D_["etot"]
            w, Rw = self.wload(win[l, CT_HQ + hd], KT * 128)
            proj_fm(w, Rw, lambda c, P_, R_: self.act(q_f[:, c * 512:(c + 1) * 512], P_[:, :], AF.Copy, reads=[R_], writes=[Rq]))
            yield
            w, Rw = self.wload(win[l, CT_HI + hd], KT * 128)
            for half in range(2):
                P_, R_ = self.ps_grp(G_PREP)
                for q in range(4):
                    tt_ = half * 4 + q
                    for kt in range(KT):
                        self.mm(P_[:, q * 128:(q + 1) * 128], self.h[:, kt, tt_ * 128:(tt_ + 1) * 128], w[:, kt * 128:(kt + 1) * 128],
                                kt == 0, kt == KT - 1, reads=[Rw, self.Rh], writes=[R_])
                for q in range(4):
                    self.act(vtok[:, half * 4 + q, :], P_[:, q * 128:(q + 1) * 128], AF.Copy, reads=[R_], writes=[RD["vtok"]])
                yield
            def dchain(d):
                tmp, tmp2, Rt, Rt2 = tmpD[d], tmp2D[d], RtD[d], Rt2D[d]
                w, Rw = self.wload(win[l, (CT_HFF if d == 0 else CT_HFB) + hd], KT * 128)
                proj_fm(w, Rw, lambda c, P_, R_: self.act(tmp[:, c * 512:(c + 1) * 512], P_[:, :], AF.Exp, reads=[R_], writes=[Rt], scale=-1.0))
                yield
                self.act(tmp, tmp, AF.Ln, reads=[Rt, self.Rconst], writes=[Rt], scale=1.0, bias=self.eps_t[:, 1:2])
                yield
                self.act(tmp, tmp, AF.Exp, reads=[Rt], writes=[Rt], scale=-1.0)
                yield
                self.ts(tmp, tmp, self.oml[:, d, hd, l:l + 1], self.lb[:, d, hd, l:l + 1], ALU.mult, ALU.add, reads=[Rt, self.Rsm], writes=[Rt])
                yield
                self.ts(kk[d], tmp, -1.0, 1.0, ALU.mult, ALU.add, reads=[Rt], writes=[Rk[d]])
                self.act(tmp, tmp, AF.Ln, reads=[Rt], writes=[Rt])
                yield
                B.op("dve", lambda e, d=d: e.tensor_tensor_scan(out=bb[d], data0=self.resetm[:], data1=tmp, initial=0.0, op0=ALU.mult, op1=ALU.add),
                     reads=[Rt, self.Rconst], writes=[Rb[d]])
                yield
                b3 = bb[d].rearrange("p (c j) -> p c j", j=32)
                if d == 1:
                    self.tt(tmp, tmp, bb[d], ALU.subtract, reads=[Rt, Rb[d]], writes=[Rt])
                    yield
                    self.tt(tmp2.rearrange("p (c j) -> p c j", j=32), tmp.rearrange("p (c j) -> p c j", j=32),
                            b3[:, :, 31:32].to_broadcast([128, 32, 32]), ALU.add, reads=[Rt, Rb[d]], writes=[Rt2])
                    yield
                    self.cp(bb[d], tmp2, reads=[Rt2], writes=[Rb[d]])
                    yield
                    tot = b3[:, :, 0:1]
                else:
                    tot = b3[:, :, 31:32]
                self.act(etot[d].unsqueeze(2), tot, AF.Exp, reads=[Rb[d]], writes=[RD["etot"][d]])
                self.act(tmp, bb[d], AF.Exp, reads=[Rb[d]], writes=[Rt])
                yield
                self.tt(qt[d], q_f, tmp, ALU.mult, reads=[Rq, Rt], writes=[RD["qt"][d]])
                yield
                self.act(tmp, bb[d], AF.Exp, reads=[Rb[d]], writes=[Rt], scale=-1.0)
                yield
                self.tt(tmp2, kk[d], tmp, ALU.mult, reads=[Rk[d], Rt], writes=[Rt2])
                yield
                self.act(ktl[d], tmp2, AF.Copy, reads=[Rt2], writes=[RD["ktl"][d]])
                self.tt(khat[d].rearrange("p (c j) -> p c j", j=32), tmp2.rearrange("p (c j) -> p c j", j=32),
                        etot[d].unsqueeze(2).to_broadcast([128, 32, 32]), ALU.mult, reads=[Rt2, RD["etot"][d]], writes=[Rkh[d]])
                yield
                for half in range(2):
                    P_, R_ = self.ps_grp(G_PREP)
                    Pb = P_[:, :].bitcast(BF16)
                    for qd in range(4):
                        tt_ = half * 4 + qd
                        B.op("pe", lambda e, Pb=Pb, qd=qd, tt_=tt_, d=d: e.transpose(Pb[:, qd * 128:(qd + 1) * 128], khat[d][:, tt_ * 128:(tt_ + 1) * 128], self.ident_bf[:, :]),
                             reads=[Rkh[d], self.Rconst], writes=[R_])
                    for qd in range(4):
                        tt_ = half * 4 + qd
                        self.tt(khat4[d][:, tt_, :, :], Pb[:, qd * 128:(qd + 1) * 128].unsqueeze(1).to_broadcast([128, 4, 128]),
                                self.cm[:, 3, 0:4].unsqueeze(2).to_broadcast([128, 4, 128]), ALU.mult,
                                reads=[R_, self.Rconst], writes=[RD["khat4"][d]])
                        if qd % 2 == 1:
                            yield

            gens = [dchain(0), dchain(1)]
            alive = [True, True]
            while any(alive):
                for gi in range(2):
                    if alive[gi]:
                        try:
                            next(gens[gi])
                        except StopIteration:
                            alive[gi] = False
                yield
            Rt = RtD[0]
            w, Rw = self.wload(win[l, CT_HG + hd], KT * 128)

            def evg(c, P_, R_):
                sl = slice(c * 512, (c + 1) * 512)
                self.act(tmp[:, sl], P_[:, :], AF.Exp, reads=[R_], writes=[Rt], scale=-1.0)
                self.sig_from_exp(tmp[:, sl], Rt)
                self.tt(sg[:, sl], P_[:, :], tmp[:, sl], ALU.mult, reads=[R_, Rt], writes=[RD["sg"]])

            proj_fm(w, Rw, evg)
            yield

        def rec(hd):
            D_ = S[hd % 2]
            RD = D_["R"]
            qt, ktl, khat4, vtok, sg, etot = D_["qt"], D_["ktl"], D_["khat4"], D_["vtok"], D_["sg"], D_["etot"]
            ntile = L // 128
            for sq_ in range(NS):
                for d in range(2):
                    if ps_ == 0:
                        B.dma("sp", self.S_f[d][:, :], self.hst[l, d, hd], writes=[self.RS[d]])
                        self.act(self.S_b[d][:, :], self.S_f[d][:, :], AF.Copy, reads=[self.RS[d]], writes=[self.RS[d]])
                    else:
                        B.op("dve", lambda e, d=d: e.memset(self.S_f[d][:, :], 0.0), writes=[self.RS[d]])
                        B.op("dve", lambda e, d=d: e.memset(self.S_b[d][:, :], 0.0), writes=[self.RS[d]])
                for it in range(ntile):
                    for d in range(2):
                        tl = sq_ * ntile + (it if d == 0 else ntile - 1 - it)
                        ts_ = slice(tl * 128, (tl + 1) * 128)
                        Psc, Rsc = self.ps_grp(G_REC)
                        self.mm(Psc[:, 0:128], ktl[d][:, ts_], qt[d][:, ts_], True, True, reads=[RD["ktl"][d], RD["qt"][d]], writes=[Rsc])
                        self.tt(masked[d], Psc[:, 0:128], self.cm[:, 1 + d, :], ALU.mult, reads=[Rsc, self.Rconst], writes=[Rms[d]])
                        Po, Rpo = self.PS[d], self.RPS[d]
                        self.mm(Po[:, 0:128], vtok[:, tl, :], masked[d], True, False, reads=[RD["vtok"], Rms[d]], writes=[Rpo])
                        corder = range(4) if d == 0 else range(3, -1, -1)
                        for ci, c in enumerate(corder):
                            gc = tl * 4 + c
                            cs = slice(tl * 128 + c * 32, tl * 128 + (c + 1) * 32)
                            self.mm(Po[:, c * 32:(c + 1) * 32], self.S_b[d][:, :], qt[d][:, cs], False, ci == 3,
                                    reads=[self.RS[d], RD["qt"][d]], writes=[Rpo])
                            Pu, Rpu = self.ps_grp(G_REC)
                            self.mm(Pu[:, 0:128], khat4[d][:, tl, c, :], vtok[:, tl, :], True, True, reads=[RD["khat4"][d], RD["vtok"]], writes=[Rpu])
                            self.stt(self.S_f[d][:, :], self.S_f[d][:, :], etot[d][:, gc:gc + 1], Pu[:, 0:128], ALU.mult, ALU.add,
                                     reads=[self.RS[d], RD["etot"][d], Rpu], writes=[self.RS[d]])
                            self.act(self.S_b[d][:, :], self.S_f[d][:, :], AF.Copy, reads=[self.RS[d]], writes=[self.RS[d]])
                            yield
                        self.act(o_d[d][:, ts_], Po[:, 0:128], AF.Copy, reads=[Rpo], writes=[Ro[d]])
                if ps_ == 1:
                    for d in range(2):
                        B.dma("sp", self.nhs[sq_, l, d, hd], self.S_f[d][:, :], reads=[self.RS[d]])
            self.tt(o_d[0], o_d[0], o_d[1], ALU.add, reads=[Ro[0], Ro[1]], writes=[Ro[0]])
            self.act(sqo, o_d[0], AF.Square, reads=[Ro[0]], writes=[Rsq])
            yield
            for c in range(2):
                P_, R_ = self.ps_grp(G_REC)
                self.mm(P_[:, 0:512], self.ones_bf[:, :], sqo[:, c * 512:(c + 1) * 512], True, True, reads=[Rsq, self.Rconst], writes=[R_])
                self.act(rstd[:, c * 512:(c + 1) * 512], P_[:, 0:512], AF.Ln, reads=[R_, self.Rconst], writes=[Rr], scale=1.0 / 128.0, bias=self.eps_t[:, 0:1])
            self.act(rstd, rstd, AF.Exp, reads=[Rr], writes=[Rr], scale=-0.5)
            yield
            self.stt(ntmp, o_d[0], self.onw_s[:, l:l + 1], rstd, ALU.mult, ALU.mult, reads=[Ro[0], Rr, self.Rsm], writes=[Rnt])
            self.tt(self.o_hg[:, hd, :], ntmp, sg, ALU.mult, reads=[Rnt, RD["sg"]], writes=[self.Rohg[hd]])
            yield

        heads = list(self.cfg.get("heads", range(8)))
        for _ in prep(heads[0]):
            pass
        for i, hd in enumerate(heads):
            gr = rec(hd)
            gp = prep(heads[i + 1]) if i + 1 < len(heads) else iter(())
            done_r = done_p = False
            while not (done_r and done_p):
                if not done_r:
                    try:
                        next(gr)
                    except StopIteration:
                        done_r = True
                if not done_p:
                    try:
                        next(gp)
                    except StopIteration:
                        done_p = True
        B.fence()

    def attn_phase(self, ps_, l):
        B = self.B
        B.fence()
        A = self.arena
        win = self.win
        NS, L = (1, 1024) if ps_ == 0 else (4, 256)
        sample = ps_ == 0
        koff = 256 if sample else 0
        NK = koff + 1024
        off = [0]

        def alloc(n):
            a = A[:, off[0]:off[0] + n]
            off[0] += n
            return a

        k_use = alloc(NK).bitcast(BF16).rearrange("p (h n) -> p h n", h=2)
        vtok = alloc(1280).bitcast(BF16).rearrange("p (t h d) -> p t h d", t=10, h=2)
        kraw = alloc(1024); rstd = alloc(1024); kn = alloc(1024); sg = alloc(1024); tmp = alloc(1024); tmp2 = alloc(1024)
        sqk = alloc(512).bitcast(BF16)
        q_use = alloc(512).bitcast(BF16)
        vtf = alloc(2048).rearrange("p (t c) -> p t c", t=8)
        Eb = [alloc(256).bitcast(BF16) for _ in range(4)]
        rden = alloc(512)
        if sample:
            rC = alloc(1024); rS = alloc(1024)
        assert off[0] <= 20 * 1024, off[0]
        Rk, Rsq, Rr, Rkn, Rv, Rvt, Rku, Rsg, Rt, Rt2, Rqu, Rrd, Rrope = [Region() for _ in range(13)]
        RE = regions(4)
        hfn = lambda kt, c: self.h[:, kt, c * 512:(c + 1) * 512]
        if sample:
            B.dma("sp", rC, self.ropeC, writes=[Rrope])
            B.dma("sp", rS, self.ropeS, writes=[Rrope])

        def normed(ct, nw_ap):
            w, Rw = self.wload(win[l, ct], KT * 128)

            def evac(c, P_, R_):
                self.act(kraw[:, c * 512:(c + 1) * 512], P_[:, :], AF.Copy, reads=[R_], writes=[Rk])
                self.act(sqk[:, c * 512:(c + 1) * 512], P_[:, :], AF.Square, reads=[R_], writes=[Rsq])

            self.proj_fm(w, Rw, KT, hfn, [self.Rh], evac)
            for c in range(2):
                self.rstd_from_sq(lambda kt, c=c: sqk[:, c * 512:(c + 1) * 512], 1, [Rsq], 128.0,
                                  rstd[:, c * 512:(c + 1) * 512], Rr, 512)
            self.stt(kn, kraw, nw_ap, rstd, ALU.mult, ALU.mult, reads=[Rk, Rr, self.Rsm], writes=[Rkn])

        def rope_to(dst, Rdst):
            for c in range(2):
                sl = slice(c * 512, (c + 1) * 512)
                P_, R_ = self.ps()
                self.mm(P_[:, :], self.cm[:, 4, :], kn[:, sl], True, True, reads=[Rkn, self.Rconst], writes=[R_])
                self.tt(tmp[:, sl], kn[:, sl], rC[:, sl], ALU.mult, reads=[Rkn, Rrope], writes=[Rt])
                self.tt(tmp2[:, sl], P_[:, :], rS[:, sl], ALU.mult, reads=[R_, Rrope], writes=[Rt2])
                self.tt(dst[:, sl], tmp[:, sl], tmp2[:, sl], ALU.add, reads=[Rt, Rt2], writes=[Rdst])

        for kvh in range(2):
            normed(CT_AK + kvh, self.knw_s[:, l:l + 1])
            if sample:
                rope_to(k_use[:, kvh, koff:koff + 1024], Rku)
                B.dma("sp", tmp[:, 0:256], self.ck[l, kvh], reads=[Rt], writes=[Rt])
                self.act(k_use[:, kvh, 0:256], tmp[:, 0:256], AF.Copy, reads=[Rt], writes=[Rku])
            else:
                B.dma("sp", self.nck[l, kvh], kn, reads=[Rkn])
                self.act(k_use[:, kvh, 0:1024], kn, AF.Copy, reads=[Rkn], writes=[Rku])
        kt0 = 2 if sample else 0
        for kvh in range(2):
            w, Rw = self.wload(win[l, CT_AV + kvh], KT * 128)

            def evacv(tt, P_, R_, kvh=kvh):
                self.act(vtok[:, kt0 + tt, kvh, :], P_, AF.Copy, reads=[R_], writes=[Rvt])
                if not sample:
                    self.act(vtf[:, tt, kvh * 128:(kvh + 1) * 128], P_, AF.Copy, reads=[R_], writes=[Rv])

            self.proj_tm(w, Rw, evacv)
        if sample:
            B.dma("sp", vtf[:, 0:2, :], self.cv[l].rearrange("(t p) c -> p t c", p=128), writes=[Rv])
            self.act(vtok[:, 0:2, :, :], vtf[:, 0:2, :].rearrange("p t (h d) -> p t h d", h=2), AF.Copy, reads=[Rv], writes=[Rvt])
        else:
            B.dma("sp", self.ncv[l].rearrange("(t p) c -> p t c", p=128), vtf, reads=[Rv])
        scale = 128.0 ** -0.5
        e_rr = 0
        a_rr = 0
        s_rr = 0
        for hq in range(self.cfg.get("n_qheads", 8)):
            kvh = hq // 4
            normed(CT_AQ + hq, self.qnw_s[:, l:l + 1])
            if sample:
                rope_to(q_use, Rqu)
            else:
                self.act(q_use, kn, AF.Copy, reads=[Rkn], writes=[Rqu])
            w, Rw = self.wload(win[l, CT_AG + hq], KT * 128)

            def evg(c, P_, R_):
                sl = slice(c * 512, (c + 1) * 512)
                self.act(tmp[:, sl], P_[:, :], AF.Exp, reads=[R_], writes=[Rt], scale=-1.0)
                self.sig_from_exp(tmp[:, sl], Rt)
                self.tt(sg[:, sl], P_[:, :], tmp[:, sl], ALU.mult, reads=[R_, Rt], writes=[Rsg])

            self.proj_fm(w, Rw, KT, hfn, [self.Rh], evg)
            for sq_ in range(NS):
                if sample:
                    ktiles = list(range(10))
                    qchunks = [(0, 512), (512, 512)]
                else:
                    ktiles = [2 * sq_, 2 * sq_ + 1]
                    qchunks = [(sq_ * 256, 256)]
                for (q0, qn_) in qchunks:
                    Po, Rpo = self.PS[a_rr], self.RPS[a_rr]
                    Pd, Rpd = self.PS[2 + a_rr], self.RPS[2 + a_rr]
                    a_rr = 1 - a_rr
                    for ji, j in enumerate(ktiles):
                        Psc, Rsc = self.PS[4 + s_rr], self.RPS[4 + s_rr]
                        s_rr = (s_rr + 1) % 4
                        self.mm(Psc[:, 0:qn_], k_use[:, kvh, j * 128:(j + 1) * 128], q_use[:, q0:q0 + qn_], True, True,
                                reads=[Rku, Rqu], writes=[Rsc])
                        E, RE_ = Eb[e_rr], RE[e_rr]
                        e_rr = (e_rr + 1) % 4
                        self.act(E[:, 0:qn_], Psc[:, 0:qn_], AF.Exp, reads=[Rsc], writes=[RE_], scale=scale)
                        first, last = ji == 0, ji == len(ktiles) - 1
                        self.mm(Po[:, 0:qn_], vtok[:, j, kvh, :], E[:, 0:qn_], first, last, reads=[Rvt, RE_], writes=[Rpo])
                        self.mm(Pd[:, 0:qn_], self.ones_bf[:, :], E[:, 0:qn_], first, last, reads=[self.Rconst, RE_], writes=[Rpd])
                    self.act(rden[:, 0:qn_], Pd[:, 0:qn_], AF.Ln, reads=[Rpd], writes=[Rrd])
                    self.act(rden[:, 0:qn_], rden[:, 0:qn_], AF.Exp, reads=[Rrd], writes=[Rrd], scale=-1.0)
                    self.tt(tmp2[:, q0:q0 + qn_], Po[:, 0:qn_], rden[:, 0:qn_], ALU.mult, reads=[Rpo, Rrd], writes=[Rt2])
                    self.tt(self.o_at[:, hq, q0:q0 + qn_], tmp2[:, q0:q0 + qn_], sg[:, q0:q0 + qn_], ALU.mult,
                            reads=[Rt2, Rsg], writes=[self.Roat[hq]])


def _const_mats():
    cm = np.zeros((128, 8, 128), np.float32)
    cm[:, 0, :] = np.eye(128, dtype=np.float32)
    idx = np.arange(128)
    same = (idx[:, None] // 32) == (idx[None, :] // 32)
    cm[:, 1, :] = (same & (idx[:, None] <= idx[None, :])).astype(np.float32)
    cm[:, 2, :] = (same & (idx[:, None] >= idx[None, :])).astype(np.float32)
    cm[:, 3, 0:4] = (idx[:, None] // 32 == np.arange(4)[None, :]).astype(np.float32)
    partner = np.where((idx % 64) < 32, idx + 32, idx - 32)
    cm[partner, 4, idx] = 1.0
    blk = idx // 16
    cm[:, 5, :] = (blk[:, None] <= blk[None, :]).astype(np.float32)
    cm[:, 6, :] = (blk[:, None] >= blk[None, :]).astype(np.float32)
    return cm


def _rope_tables():
    f = np.float32
    t = np.arange(T)
    row = (t // 64).astype(f)
    col = (t % 64).astype(f)
    inv = (f(10000.0) ** (-(np.arange(32, dtype=f) / f(32)))).astype(f)
    ar = (row[None, :] * inv[:, None]).astype(f)
    ac = (col[None, :] * inv[:, None]).astype(f)
    C = np.concatenate([np.cos(ar), np.cos(ar), np.cos(ac), np.cos(ac)], 0).astype(f)
    S = np.concatenate([-np.sin(ar), np.sin(ar), -np.sin(ac), np.sin(ac)], 0).astype(f)
    return np.ascontiguousarray(C), np.ascontiguousarray(S)


def host_inputs(inp, nL=DEPTH):
    f = np.float32
    shared = {}
    wm = inp["w_mod"][:nL].reshape(nL, KT, 128, 48, 128).transpose(0, 3, 2, 1, 4)
    shared["wmod"] = np.ascontiguousarray(wm).reshape(nL, 48, 128, KT * 128)
    shared["bmod"] = np.ascontiguousarray(inp["b_mod"][:nL].reshape(nL, 48, 128).transpose(2, 0, 1))
    shared["normw"] = np.ascontiguousarray(inp["norm_w"][:nL].reshape(nL, KT, 128).transpose(2, 0, 1))
    shared["finaln"] = np.ascontiguousarray(inp["final_norm"].reshape(KT, 128).T)
    wi = inp["w_in"][:nL].reshape(nL, KT, 128, NCT, 128).transpose(0, 3, 2, 1, 4)
    shared["win"] = np.ascontiguousarray(wi).reshape(nL, NCT, 128, KT * 128)
    shared["qnw"] = np.ascontiguousarray(inp["at_q_norm"][:nL].T)
    shared["knw"] = np.ascontiguousarray(inp["at_k_norm"][:nL].T)
    shared["cmat"] = _const_mats()
    shared["ropeC"], shared["ropeS"] = _rope_tables()
    are = inp["s5_a_re"][:nL].transpose(0, 1, 3, 2).reshape(nL, 128, 64)
    aim = inp["s5_a_im"][:nL].transpose(0, 1, 3, 2).reshape(nL, 128, 64)
    ldt = np.broadcast_to(inp["s5_log_dt"][:nL][:, :, None, :], (nL, 2, 64, 64)).reshape(nL, 128, 64)
    shared["s5p"] = np.ascontiguousarray(np.stack([are, aim, ldt], 2))
    bre = inp["s5_b_re"][:nL].transpose(0, 1, 3, 2, 4).reshape(nL, 128, 1024)
    bim = inp["s5_b_im"][:nL].transpose(0, 1, 3, 2, 4).reshape(nL, 128, 1024)
    shared["s5b"] = np.ascontiguousarray(np.stack([bre, bim], 1))
    cre = inp["s5_c_re"][:nL].transpose(0, 3, 1, 2).reshape(nL, 1, 64, 1024)
    cim = inp["s5_c_im"][:nL].transpose(0, 3, 1, 2).reshape(nL, 1, 64, 1024)
    cre = np.broadcast_to(cre, (nL, 2, 64, 1024)).reshape(nL, 128, 1024)
    cim = np.broadcast_to(cim, (nL, 2, 64, 1024)).reshape(nL, 128, 1024)
    shared["s5c"] = np.ascontiguousarray(np.stack([cre, cim], 1))
    dd = inp["s5_d"][:nL].reshape(nL, 64, 16)
    shared["s5dd"] = np.ascontiguousarray(np.broadcast_to(dd.transpose(2, 0, 1)[None], (8, 16, nL, 64)).reshape(128, nL, 64))
    sg = np.where(np.arange(128) < 64, 1.0, -1.0).astype(f)[:, None] * np.arange(9, dtype=f)[None, :]
    shared["sgnk"] = np.ascontiguousarray(np.stack([sg, -sg, np.broadcast_to(np.arange(9, dtype=f), (128, 9))], 1))
    wg = inp["s5_w_glu"][:nL].reshape(nL, 8, 128, 8, 128).transpose(0, 3, 2, 1, 4)
    shared["wglu"] = np.ascontiguousarray(wg).reshape(nL, 8, 128, 1024)
    wbr = np.stack([inp[k][:nL].reshape(nL, 8, 128, KT, 128).transpose(0, 3, 2, 1, 4) for k in ("w_br_hg", "w_br_at", "w_br_s5")], 1)
    shared["wbr"] = np.ascontiguousarray(wbr).reshape(nL, 3, KT, 128, 1024)
    wo = inp["w_out"][:nL].reshape(nL, KT, 128, KT, 128).transpose(0, 3, 2, 1, 4)
    shared["wout"] = np.ascontiguousarray(wo).reshape(nL, KT, 128, KT * 128)
    shared["lblog"] = np.ascontiguousarray(inp["hg_lb_logits"].reshape(2, DEPTH, 8, 128).transpose(3, 0, 2, 1))
    shared["onw"] = np.ascontiguousarray(inp["hg_onorm"][:nL].T)
    maps = []
    for i in range(NCORES):
        m = dict(shared)
        xs = inp["x_sample"][i]
        xp = inp["x_prompt"][4 * i:4 * i + 4].reshape(T, D)
        xin = np.stack([xs.T.reshape(KT, 128, T), xp.T.reshape(KT, 128, T)], 0)
        m["xin"] = np.ascontiguousarray(xin)
        cond = np.stack([inp["c"][i].reshape(KT, 128).T, inp["c_ctx"].reshape(KT, 128).T], -1)
        m["cond"] = np.ascontiguousarray(cond)
        m["hst"] = np.ascontiguousarray(inp["state_hgrn"][i, :nL])
        s0 = np.stack([inp["state_s5_re"][i, :nL], inp["state_s5_im"][i, :nL]], -1)
        s0 = s0.reshape(nL, 2, 32, 2, 64, 2).transpose(0, 3, 4, 1, 2, 5)
        m["s5s0"] = np.ascontiguousarray(s0).reshape(nL, 128, 64, 2)
        m["ck"] = np.ascontiguousarray(inp["cache_k"][i, :nL].transpose(0, 2, 3, 1))
        m["cv"] = np.ascontiguousarray(inp["cache_v"][i, :nL].reshape(nL, 256, 256))
        maps.append(m)
    return maps


_PROG_CACHE = {}


def run(inp, cfg):
    key = repr(sorted(cfg.items()))
    if key not in _PROG_CACHE:
        _PROG_CACHE[key] = Prog(cfg)
    prog = _PROG_CACHE[key]
    maps = host_inputs(inp, cfg.get("n_layers_alloc", DEPTH))
    maps = [{k: v for k, v in m.items() if k in prog.din} for m in maps]
    res = run_bass_kernel_spmd(prog.nc, maps, core_ids=list(range(NCORES)))
    return res.results


def kernel(**inputs):
    inp = {k: np.asarray(v) for k, v in inputs.items()}
    res = run(inp, {})
    L = DEPTH
    y = np.stack([r["y"] for r in res], 0)
    y_sample = np.ascontiguousarray(y[:, 0].reshape(NCORES, D, T).transpose(0, 2, 1))
    y_prompt = np.ascontiguousarray(y[:, 1].reshape(NCORES, D, T).transpose(0, 2, 1)).reshape(32, 256, D)
    nck = np.stack([r["nck"] for r in res], 0)
    new_k = np.ascontiguousarray(nck.reshape(NCORES, L, 2, 128, 4, 256).transpose(0, 4, 1, 5, 2, 3)).reshape(32, L, 256, 2, 128)
    ncv = np.stack([r["ncv"] for r in res], 0)
    new_v = np.ascontiguousarray(ncv.reshape(NCORES, L, 4, 256, 2, 128).transpose(0, 2, 1, 3, 4, 5)).reshape(32, L, 256, 2, 128)
    nhs = np.stack([r["nhs"] for r in res], 0).reshape(32, L, 2, 8, 128, 128)
    ns5 = np.stack([r["ns5"] for r in res], 0)
    a = ns5.reshape(NCORES, L, 2, 64, 4, 2, 32, 2).transpose(0, 4, 1, 5, 6, 2, 3, 7).reshape(32, L, 2, 64, 64, 2)
    f = np.float32
    return (y_prompt.astype(f), y_sample.astype(f), new_k.astype(f), new_v.astype(f), np.ascontiguousarray(nhs).astype(f),
            np.ascontiguousarray(a[..., 0]).astype(f), np.ascontiguousarray(a[..., 1]).astype(f))
```

```python
import contextlib
import math
import numpy as np
import concourse.bass as bass
import concourse.mybir as mybir
from concourse.bass_utils import run_bass_kernel_spmd

F32 = mybir.dt.float32
BF16 = mybir.dt.bfloat16
I32 = mybir.dt.int32
AF = mybir.ActivationFunctionType
ALU = mybir.AluOpType
AX = mybir.AxisListType

ENGS = ("pe", "act", "dve", "pool", "sp")

D = 2048
KT = 16
T = 1024
DEPTH = 4
EPS = 1e-6
NCORES = 8
IN_COLS = 15872
NCT = IN_COLS // 128
CT_HQ, CT_HI, CT_HFF, CT_HFB, CT_HG = 0, 8, 16, 24, 32
CT_AQ, CT_AK, CT_AV, CT_AG = 40, 48, 50, 52
CT_SU, CT_SG = 60, 68
CT_MG = 76


class Region:
    __slots__ = ("w", "r")

    def __init__(self):
        self.w = None
        self.r = []


def regions(n):
    return [Region() for _ in range(n)]


class Builder:
    def __init__(self, nc, n_dma_sems=20, n_w_sems=6):
        self.nc = nc
        self.stack = contextlib.ExitStack()
        self.cnt = {e: 0 for e in ENGS}
        self.seen = {e: {} for e in ENGS}
        self.prog = {e: [] for e in ENGS}
        self.semobj = {}
        for e in ENGS:
            self.semobj[("c", e)] = self.stack.enter_context(nc.semaphore("c_" + e))
        self.dcnt = [0] * n_dma_sems
        for i in range(n_dma_sems):
            self.semobj[("d", i)] = self.stack.enter_context(nc.semaphore("d%d" % i))
        self.wcnt = [0] * n_w_sems
        for i in range(n_w_sems):
            self.semobj[("w", i)] = self.stack.enter_context(nc.semaphore("w%d" % i))
        self.drr = 0
        self.n_ops = 0

    def sbuf(self, name, shape, dt):
        return self.stack.enter_context(self.nc.sbuf_tensor(name, list(shape), dt))

    def psum(self, name, shape, dt):
        return self.stack.enter_context(self.nc.psum_tensor(name, list(shape), dt))

    def _waits(self, eng, reads, writes):
        need = {}
        for r in reads:
            if r.w is not None:
                k, v = r.w
                if need.get(k, 0) < v:
                    need[k] = v
        for w in writes:
            if w.w is not None:
                k, v = w.w
                if need.get(k, 0) < v:
                    need[k] = v
            for (k, v) in w.r:
                if need.get(k, 0) < v:
                    need[k] = v
        out = []
        seen = self.seen[eng]
        for k, v in need.items():
            if k == ("c", eng) and eng == "pe":
                continue
            if seen.get(k, 0) >= v:
                continue
            seen[k] = v
            out.append((k, v))
        return out

    def _commit(self, ev, reads, writes):
        for r in reads:
            r.r.append(ev)
            if len(r.r) > 64:
                mx = {}
                for (k, v) in r.r:
                    if mx.get(k, 0) < v:
                        mx[k] = v
                r.r = list(mx.items())
        for w in writes:
            w.w = ev
            w.r = []

    def op(self, eng, fn, reads=(), writes=()):
        waits = self._waits(eng, reads, writes)
        self.cnt[eng] += 1
        ev = (("c", eng), self.cnt[eng])
        self.prog[eng].append((waits, fn, ("c", eng), 1))
        self._commit(ev, reads, writes)
        self.n_ops += 1
        return ev

    def dma(self, q, out, in_, reads=(), writes=(), wslot=None, **kw):
        waits = self._waits(q, reads, writes)
        if wslot is None:
            i = self.drr
            self.drr = (self.drr + 1) % len(self.dcnt)
            self.dcnt[i] += 16
            ev = (("d", i), self.dcnt[i])
        else:
            self.wcnt[wslot] += 16
            ev = (("w", wslot), self.wcnt[wslot])

        def fn(e, out=out, in_=in_, kw=kw):
            return e.dma_start(out=out, in_=in_, **kw)

        self.prog[q].append((waits, fn, ev[0], 16))
        self._commit(ev, reads, writes)
        self.n_ops += 1
        return ev

    def fence(self):
        for e in ("pe", "act", "dve", "sp"):
            self.wait_all(e, engs=("pe", "act", "dve", "sp"))

    def wait_all(self, eng, engs=ENGS):
        waits = []
        for e in engs:
            if e == eng:
                continue
            if self.cnt[e] > self.seen[eng].get(("c", e), 0):
                self.seen[eng][("c", e)] = self.cnt[e]
                waits.append((("c", e), self.cnt[e]))
        for i, c in enumerate(self.dcnt):
            if c > self.seen[eng].get(("d", i), 0):
                self.seen[eng][("d", i)] = c
                waits.append((("d", i), c))
        self.prog[eng].append((waits, None, None, 0))

    def emit(self):
        nc = self.nc
        handles = {"pe": "tensor", "act": "scalar", "dve": "vector", "pool": "gpsimd", "sp": "sync"}
        semobj = self.semobj
        with nc.Block() as block:
            for e in ENGS:
                prog = self.prog[e]

                def body(h, prog=prog):
                    for waits, fn, inc, amt in prog:
                        for (k, v) in waits:
                            h.wait_ge(semobj[k], v)
                        if fn is not None:
                            fn(h).then_inc(semobj[inc], amt)

                getattr(block, handles[e])(body)
        self.stack.close()


class Prog:
    def __init__(self, cfg):
        self.cfg = cfg
        self.layers = cfg.get("layers", list(range(DEPTH)))
        self.passes = cfg.get("passes", [0, 1])
        self.nL = cfg.get("n_layers_alloc", DEPTH)
        nc = bass.Bass("TRN2", target_bir_lowering=False)
        self.nc = nc
        self.B = Builder(nc)
        self.din = {}
        self.dout = {}
        self.build()

    def inp(self, name, shape, dt=F32):
        t = self.nc.dram_tensor(name, list(shape), dt, kind="ExternalInput").ap()
        self.din[name] = t
        return t

    def outp(self, name, shape, dt=F32):
        t = self.nc.dram_tensor(name, list(shape), dt, kind="ExternalOutput").ap()
        self.dout[name] = t
        return t

    def scratch(self, name, shape, dt=F32):
        return self.nc.dram_tensor(name, list(shape), dt).ap()

    def ps(self):
        i = self.ps_rr
        self.ps_rr = (self.ps_rr + 1) % 8
        return self.PS[i], self.RPS[i]

    def wload(self, src, ncols):
        s = self.w_rr
        self.w_rr = (self.w_rr + 1) % self.NSLOT
        self.B.dma("pool", self.ring[:, s, 0:ncols], src, writes=[self.Rring[s]], wslot=s)
        return self.ring[:, s, :], self.Rring[s]

    def mm(self, out, lhsT, rhs, start, stop, reads, writes):
        self.B.op("pe", lambda e: e.matmul(out, lhsT, rhs, start=start, stop=stop), reads=reads, writes=writes)

    def act(self, out, in_, func, reads, writes, scale=1.0, bias=None):
        if bias is None:
            self.B.op("act", lambda e: e.activation(out=out, in_=in_, func=func, scale=scale), reads=reads, writes=writes)
        else:
            self.B.op("act", lambda e: e.activation(out=out, in_=in_, func=func, scale=scale, bias=bias), reads=reads, writes=writes)

    def tt(self, out, in0, in1, op, reads, writes, eng="dve"):
        self.B.op(eng, lambda e: e.tensor_tensor(out=out, in0=in0, in1=in1, op=op), reads=reads, writes=writes)

    def ts(self, out, in0, s1, s2, op0, op1, reads, writes, eng="dve"):
        if s2 is None:
            self.B.op(eng, lambda e: e.tensor_scalar(out=out, in0=in0, scalar1=s1, scalar2=None, op0=op0), reads=reads, writes=writes)
        else:
            self.B.op(eng, lambda e: e.tensor_scalar(out=out, in0=in0, scalar1=s1, scalar2=s2, op0=op0, op1=op1), reads=reads, writes=writes)

    def stt(self, out, in0, scalar, in1, op0, op1, reads, writes, eng="dve"):
        self.B.op(eng, lambda e: e.scalar_tensor_tensor(out=out, in0=in0, scalar=scalar, in1=in1, op0=op0, op1=op1), reads=reads, writes=writes)

    def cp(self, out, in_, reads, writes, eng="dve"):
        self.B.op(eng, lambda e: e.tensor_copy(out=out, in_=in_), reads=reads, writes=writes)

    def sig_from_exp(self, buf, R_):
        self.act(buf, buf, AF.Ln, reads=[R_, self.Rconst], writes=[R_], scale=1.0, bias=self.eps_t[:, 1:2])
        self.act(buf, buf, AF.Exp, reads=[R_], writes=[R_], scale=-1.0)

    def recip(self, out, in_, reads, writes):
        self.B.op("dve", lambda e: e.reciprocal(out=out, in_=in_), reads=reads, writes=writes)

    def proj_fm(self, w, Rw, kt_n, rhs_fn, rhs_regs, evac):
        banks = [self.ps() for _ in range(2)]
        for c in range(2):
            P_, R_ = banks[c]
            for kt in range(kt_n):
                self.mm(P_[:, :], w[:, kt * 128:(kt + 1) * 128], rhs_fn(kt, c), kt == 0, kt == kt_n - 1,
                        reads=[Rw] + rhs_regs, writes=[R_])
        for c in range(2):
            P_, R_ = banks[c]
            evac(c, P_, R_)

    def proj_tm(self, w, Rw, evac):
        for half in range(2):
            P_, R_ = self.ps()
            for q in range(4):
                tt = half * 4 + q
                for kt in range(KT):
                    self.mm(P_[:, q * 128:(q + 1) * 128], self.h[:, kt, tt * 128:(tt + 1) * 128],
                            w[:, kt * 128:(kt + 1) * 128], kt == 0, kt == KT - 1,
                            reads=[Rw, self.Rh], writes=[R_])
            for q in range(4):
                evac(half * 4 + q, P_[:, q * 128:(q + 1) * 128], R_)

    def rstd_from_sq(self, sq_fn, n_kt, sq_regs, denom, out, Rout, ncols):
        P_, R_ = self.ps()
        for kt in range(n_kt):
            self.mm(P_[:, 0:ncols], self.ones_bf[:, :], sq_fn(kt), kt == 0, kt == n_kt - 1,
                    reads=sq_regs + [self.Rconst], writes=[R_])
        self.act(out, P_[:, 0:ncols], AF.Ln, reads=[R_, self.Rconst], writes=[Rout], scale=1.0 / denom, bias=self.eps_t[:, 0:1])
        self.act(out, out, AF.Exp, reads=[Rout], writes=[Rout], scale=-0.5)

    def build(self):
        nc, B = self.nc, self.B
        nL = self.nL
        xin = self.inp("xin", [2, KT, 128, T])
        cond = self.inp("cond", [128, KT, 2])
        wmod = self.inp("wmod", [nL, 48, 128, KT * 128])
        bmod = self.inp("bmod", [128, nL, 48])
        normw = self.inp("normw", [128, nL, KT])
        finaln = self.inp("finaln", [128, KT])
        win = self.inp("win", [nL, NCT, 128, KT * 128])
        qnw = self.inp("qnw", [128, nL])
        knw = self.inp("knw", [128, nL])
        cmat = self.inp("cmat", [128, 8, 128])
        lblog = self.inp("lblog", [128, 2, 8, DEPTH])
        onw = self.inp("onw", [128, nL])
        if 0 in self.passes:
            self.hst = self.inp("hst", [nL, 2, 8, 128, 128])
        self.nhs = self.outp("nhs", [4, nL, 2, 8, 128, 128])
        self.s5p = self.inp("s5p", [nL, 128, 3, 64])
        self.s5b = self.inp("s5b", [nL, 2, 128, 64 * 16])
        self.s5c = self.inp("s5c", [nL, 2, 128, 64 * 16])
        s5dd = self.inp("s5dd", [128, nL, 64])
        sgnk_d = self.inp("sgnk", [128, 3, 9])
        self.wglu = self.inp("wglu", [nL, 8, 128, 8 * 128])
        self.wbr = self.inp("wbr", [nL, 3, KT, 128, 8 * 128])
        self.wout = self.inp("wout", [nL, KT, 128, KT * 128])
        if 0 in self.passes:
            self.s5s0 = self.inp("s5s0", [nL, 128, 64, 2])
        self.ns5 = self.outp("ns5", [nL, 128, 4, 64, 2])
        mk = self.outp if self.cfg.get("dbg_s5") else self.scratch
        self.toep_d = mk("toep_d", [nL, 64, 128, 128], BF16)
        self.wb_d = mk("wb_d", [nL, 64, 128, 2, 128], BF16)
        self.ca_d = mk("ca_d", [nL, 2, 2, 64, 64, 128], BF16)
        self.a8_d = mk("a8_d", [nL, 2, 128, 64], F32)
        self.ud_d = self.scratch("ud_d", [64, 16, 8, 128], BF16)
        self.yd_d = self.scratch("yd_d", [128, 64, 128], BF16)
        self.win = win
        if 0 in self.passes:
            self.ck = self.inp("ck", [nL, 2, 128, 256])
            self.cv = self.inp("cv", [nL, 256, 256])
            self.ropeC = self.inp("ropeC", [128, T])
            self.ropeS = self.inp("ropeS", [128, T])
        self.xs = self.scratch("xs", [2, KT, 128, T])
        nck = self.outp("nck", [nL, 2, 128, T])
        ncv = self.outp("ncv", [nL, T, 256])
        yout = self.outp("y", [2, KT, 128, T])
        self.nck, self.ncv = nck, ncv

        self.NSLOT = 6
        self.ring = B.sbuf("ring", [128, self.NSLOT, KT * 128], BF16)
        self.Rring = regions(self.NSLOT)
        self.w_rr = 0
        self.PS = [B.psum("ps%d" % i, [128, 512], F32) for i in range(8)]
        self.RPS = regions(8)
        self.ps_rr = 0
        self.h = B.sbuf("h", [128, KT, T], BF16)
        self.Rh = Region()
        cm = B.sbuf("cm", [128, 8, 128], F32)
        self.Rconst = Region()
        self.ones_bf = B.sbuf("ones_bf", [128, 128], BF16)
        self.ident_bf = B.sbuf("ident_bf", [128, 128], BF16)
        self.eps_t = B.sbuf("eps_t", [128, 2], F32)
        cond_s = B.sbuf("cond_s", [128, KT, 2], F32)
        csilu = B.sbuf("csilu", [128, KT, 2], BF16)
        modT = B.sbuf("modT", [128, nL, 48, 2], F32)
        bmod_s = B.sbuf("bmod_s", [128, nL, 48], F32)
        normw_s = B.sbuf("normw_s", [128, nL, KT], F32)
        finaln_s = B.sbuf("finaln_s", [128, KT], F32)
        amod = B.sbuf("amod", [128, nL, KT, 2], F32)
        qnw_s = B.sbuf("qnw_s", [128, nL], F32)
        knw_s = B.sbuf("knw_s", [128, nL], F32)
        Rsm = Region()
        self.Rsm = Rsm
        lbl_s = B.sbuf("lbl_s", [128, 2, 8, DEPTH], F32)
        self.lb = B.sbuf("lb", [128, 2, 8, DEPTH], F32)
        self.oml = B.sbuf("oml", [128, 2, 8, DEPTH], F32)
        lbsum = B.sbuf("lbsum", [128, 2, 8], F32)
        self.onw_s = B.sbuf("onw_s", [128, nL], F32)
        self.resetm = B.sbuf("resetm", [128, T], F32)
        self.cm = cm
        self.o_hg = B.sbuf("o_hg", [128, 8, T], BF16)
        self.Rohg = regions(8)
        self.o_at = B.sbuf("o_at", [128, 8, T], BF16)
        self.Roat = regions(8)
        self.qnw_s, self.knw_s = qnw_s, knw_s
        self.o_s5 = B.sbuf("o_s5", [128, 8, T], BF16)
        self.Ros5 = regions(8)
        self.sgnk = B.sbuf("sgnk_s", [128, 3, 9], F32)
        self.s5dd_s = B.sbuf("s5dd_s", [128, nL, 64], F32)
        self.Rs5d = Region()
        self.finaln_s = finaln_s
        self.yout = yout
        self.Rxs = [regions(KT), regions(KT)]
        self.S_f = [B.sbuf("S_f%d" % d, [128, 128], F32) for d in range(2)]
        self.S_b = [B.sbuf("S_b%d" % d, [128, 128], BF16) for d in range(2)]
        self.RS = regions(2)
        ARENA = 22 * 1024
        self.arena = B.sbuf("arena", [128, ARENA], F32)
        self.Rarena = Region()

        B.dma("sp", cm[:], cmat, writes=[self.Rconst])
        B.dma("sp", self.sgnk[:], sgnk_d, writes=[self.Rconst])
        B.op("dve", lambda e: e.memset(self.ones_bf[:], 1.0), writes=[self.Rconst])
        B.op("dve", lambda e: e.memset(self.eps_t[:, 0:1], EPS), writes=[self.Rconst])
        B.op("dve", lambda e: e.memset(self.eps_t[:, 1:2], 1.0), writes=[self.Rconst])
        self.cp(self.ident_bf[:], cm[:, 0, :], reads=[self.Rconst], writes=[self.Rconst])
        B.op("dve", lambda e: e.memset(self.resetm[:], 1.0), writes=[self.Rconst])
        B.op("dve", lambda e: e.memset(self.resetm[:].rearrange("p (c j) -> p c j", j=32)[:, :, 0:1], 0.0), reads=[self.Rconst], writes=[self.Rconst])
        for (dst, src) in ((cond_s, cond), (bmod_s, bmod), (normw_s, normw), (finaln_s, finaln), (qnw_s, qnw), (knw_s, knw),
                           (lbl_s, lblog), (self.onw_s, onw), (self.s5dd_s, s5dd)):
            B.dma("sp", dst[:], src, writes=[Rsm])

        self.act(lbl_s[:], lbl_s[:], AF.Exp, reads=[Rsm], writes=[Rsm])
        B.op("dve", lambda e: e.tensor_reduce(out=lbsum[:], in_=lbl_s[:], axis=AX.X, op=ALU.add), reads=[Rsm], writes=[Rsm])
        self.recip(lbsum[:], lbsum[:], reads=[Rsm], writes=[Rsm])
        self.tt(lbl_s[:], lbl_s[:], lbsum[:].unsqueeze(3).to_broadcast([128, 2, 8, DEPTH]), ALU.mult, reads=[Rsm], writes=[Rsm])
        B.op("dve", lambda e: e.memset(self.lb[:, :, :, 0:1], 0.0), writes=[Rsm])
        for li in range(1, DEPTH):
            self.tt(self.lb[:, :, :, li:li + 1], self.lb[:, :, :, li - 1:li], lbl_s[:, :, :, li:li + 1], ALU.add, reads=[Rsm], writes=[Rsm])
        self.ts(self.oml[:], self.lb[:], -1.0, 1.0, ALU.mult, ALU.add, reads=[Rsm], writes=[Rsm])

        tmpc = B.sbuf("tmpc", [128, KT, 2], F32)
        self.act(tmpc[:], cond_s[:], AF.Exp, reads=[Rsm], writes=[Rsm], scale=-1.0)
        self.sig_from_exp(tmpc[:], Rsm)
        self.tt(csilu[:], cond_s[:], tmpc[:], ALU.mult, reads=[Rsm], writes=[Rsm])
        Rmod = Region()

        def mod_gen(l):
            for j in range(48):
                w, Rw = self.wload(wmod[l, j], KT * 128)
                P_, R_ = self.ps()
                for kt in range(KT):
                    self.mm(P_[:, 0:2], w[:, kt * 128:(kt + 1) * 128], csilu[:, kt, :], kt == 0, kt == KT - 1,
                            reads=[Rw, Rsm], writes=[R_])
                self.ts(modT[:, l, j, :], P_[:, 0:2], bmod_s[:, l, j:j + 1], None, ALU.add, None,
                        reads=[R_, Rsm], writes=[Rmod])
                yield
            for r in range(2):
                self.stt(amod[:, l, :, r], modT[:, l, 16:32, r], 1.0, normw_s[:, l, :], ALU.add, ALU.mult,
                         reads=[Rmod, Rsm], writes=[Rmod])

        self.mod_gen = mod_gen
        self.modT, self.amod, self.Rmod = modT, amod, Rmod

        for l in self.layers:
            gm = ([] if self.cfg.get("skip_mod") else self.mod_gen(l))
            gm = iter(gm)
            if self.cfg.get("do_s5", True):
                for _ in self.s5_prep(l):
                    for _k in range(6):
                        next(gm, None)
            for _ in gm:
                pass
        for ps_ in self.passes:
            r = ps_
            for li, l in enumerate(self.layers):
                xsrc = xin if li == 0 else self.xs
                self.norm_phase(xsrc, ps_, l, r)
                if self.cfg.get("do_hgrn", True):
                    self.hgrn_phase(ps_, l)
                if self.cfg.get("do_attn", True):
                    self.attn_phase(ps_, l)
                if self.cfg.get("do_s5", True) and not self.cfg.get("s5_prep_only"):
                    self.s5_phase(ps_, l)
                if self.cfg.get("do_merge", True):
                    self.merge_phase(xsrc, ps_, l, r)
            if self.cfg.get("do_merge", True):
                self.final_phase(ps_)
        B.wait_all("sp")
        B.emit()

    def norm_phase(self, xsrc, ps_, l, r):
        B = self.B
        B.fence()
        xst = self.arena[:, 0:8192].rearrange("p (k t) -> p k t", k=KT)
        sq = self.arena[:, 8192:12288].bitcast(BF16).rearrange("p (k t) -> p k t", k=KT)
        rstd = self.arena[:, 12288:12800]
        tmp = self.arena[:, 12800:13312]
        Rx, Rsq, Rr, Rt = Region(), Region(), Region(), Region()
        for c in range(2):
            B.dma("sp", xst, xsrc[ps_, :, :, c * 512:(c + 1) * 512].rearrange("k p t -> p k t"),
                  reads=self.Rxs[ps_], writes=[Rx])
            self.act(sq, xst, AF.Square, reads=[Rx], writes=[Rsq])
            self.rstd_from_sq(lambda kt: sq[:, kt, :], KT, [Rsq], float(D), rstd, Rr, 512)
            for kt in range(KT):
                self.stt(tmp, xst[:, kt, :], self.amod[:, l, kt, r:r + 1], rstd, ALU.mult, ALU.mult,
                         reads=[Rx, Rr, self.Rmod], writes=[Rt])
                self.act(self.h[:, kt, c * 512:(c + 1) * 512], tmp, AF.Identity, reads=[Rt, self.Rmod], writes=[self.Rh],
                         scale=1.0, bias=self.modT[:, l, kt, r:r + 1])


    def s5_prep(self, l):
        B = self.B
        B.fence()
        A = self.arena
        off = [0]

        def alloc(n):
            a = A[:, off[0]:off[0] + n]
            off[0] += n
            assert off[0] <= 22 * 1024, off[0]
            return a

        TWO_PI = 2.0 * math.pi
        prm = alloc(192).rearrange("p (q g) -> p q g", q=3)
        dt = alloc(64); lre = alloc(64); lim = alloc(64)
        negpi = alloc(1)
        cre = alloc(64); cim = alloc(64); t64a = alloc(64); t64b = alloc(64); den = alloc(64)
        Fre = alloc(64); Fim = alloc(64); Gre = alloc(64); Gim = alloc(64)
        tre = alloc(3 * 576).rearrange("p (q k g) -> p q k g", q=3, k=9)
        tim = alloc(3 * 576).rearrange("p (q k g) -> p q k g", q=3, k=9)
        mark = off[0]
        ang = alloc(3 * 576).rearrange("p (q k g) -> p q k g", q=3, k=9)
        yv = alloc(3 * 576); fr = alloc(3 * 576); msk = alloc(3 * 576); mag = alloc(3 * 576)
        ki = msk.bitcast(I32)
        R = Region()

        B.dma("sp", prm, self.s5p[l], writes=[R])
        B.op("dve", lambda e: e.memset(negpi, -math.pi), writes=[R])
        self.act(dt, prm[:, 2, :], AF.Exp, reads=[R], writes=[R])
        self.tt(lre, prm[:, 0, :], dt, ALU.mult, reads=[R], writes=[R])
        self.tt(lim, prm[:, 1, :], dt, ALU.mult, reads=[R], writes=[R])
        for q in range(3):
            self.tt(ang[:, q], lim.unsqueeze(1).to_broadcast([128, 9, 64]),
                    self.sgnk[:, q, :].unsqueeze(2).to_broadcast([128, 9, 64]), ALU.mult, reads=[R, self.Rconst], writes=[R])
            self.tt(tre[:, q], lre.unsqueeze(1).to_broadcast([128, 9, 64]),
                    self.sgnk[:, q, :].unsqueeze(2).to_broadcast([128, 9, 64]), ALU.mult, reads=[R, self.Rconst], writes=[R])
        angf = ang.rearrange("p q k g -> p (q k g)")
        tref = tre.rearrange("p q k g -> p (q k g)")
        timf = tim.rearrange("p q k g -> p (q k g)")
        self.act(mag, tref, AF.Exp, reads=[R], writes=[R])

        def sin_of(dst, shift):
            self.ts(yv, angf, 1.0 / TWO_PI, 64.5 + shift, ALU.mult, ALU.add, reads=[R], writes=[R])
            self.cp(ki, yv, reads=[R], writes=[R])
            self.cp(fr, ki, reads=[R], writes=[R])
            self.tt(fr, yv, fr, ALU.subtract, reads=[R], writes=[R])
            B.op("dve", lambda e: e.tensor_single_scalar(out=msk, in_=fr, scalar=0.0, op=ALU.is_lt), reads=[R], writes=[R])
            self.tt(fr, fr, msk, ALU.add, reads=[R], writes=[R])
            self.act(dst, fr, AF.Sin, reads=[R], writes=[R], scale=TWO_PI, bias=negpi)

        if self.cfg.get("prep_stop", 9) <= 1:
            return
        sin_of(timf, 0.0)
        sin_of(tref, 0.25)
        if self.cfg.get("prep_stop", 9) <= 2:
            return
        self.tt(timf, timf, mag, ALU.mult, reads=[R], writes=[R])
        self.tt(tref, tref, mag, ALU.mult, reads=[R], writes=[R])
        are_, aim_ = prm[:, 0, :], prm[:, 1, :]
        a1r, a1i = tre[:, 2, 1, :], tim[:, 2, 1, :]
        self.tt(den, are_, are_, ALU.mult, reads=[R], writes=[R])
        self.tt(t64a, aim_, aim_, ALU.mult, reads=[R], writes=[R])
        self.tt(den, den, t64a, ALU.add, reads=[R], writes=[R])
        self.recip(den, den, reads=[R], writes=[R])
        self.ts(t64a, a1r, -1.0, None, ALU.add, None, reads=[R], writes=[R])
        self.tt(cre, t64a, are_, ALU.mult, reads=[R], writes=[R])
        self.tt(t64b, a1i, aim_, ALU.mult, reads=[R], writes=[R])
        self.tt(cre, cre, t64b, ALU.add, reads=[R], writes=[R])
        self.tt(cre, cre, den, ALU.mult, reads=[R], writes=[R])
        self.tt(cim, a1i, are_, ALU.mult, reads=[R], writes=[R])
        self.tt(t64b, t64a, aim_, ALU.mult, reads=[R], writes=[R])
        self.tt(cim, cim, t64b, ALU.subtract, reads=[R], writes=[R])
        self.tt(cim, cim, den, ALU.mult, reads=[R], writes=[R])
        a8st = yv[:, 0:128].rearrange("p (r m q) -> p r m q", r=2, m=2)
        for ri, tab in enumerate((tre, tim)):
            self.cp(a8st[:, ri], tab[:, 2, 8, :].rearrange("p (q m) -> p m q", m=2), reads=[R], writes=[R])
        for ri in range(2):
            for d in range(2):
                for m in range(2):
                    B.dma("sp", self.a8_d[l, ri, m * 64:(m + 1) * 64, d * 32:(d + 1) * 32],
                          a8st[d * 64:(d + 1) * 64, ri, m, :], reads=[R], writes=[self.Rs5d])
        B.op("dve", lambda e: e.memset(Fre[64:128, :], 1.0), writes=[R])
        B.op("dve", lambda e: e.memset(Fim[64:128, :], 0.0), writes=[R])
        self.cp(Fre[0:64, :], tre[0:64, 2, 7, :], reads=[R], writes=[R])
        self.cp(Fim[0:64, :], tim[0:64, 2, 7, :], reads=[R], writes=[R])
        self.cp(Gre[0:64, :], tre[0:64, 2, 1, :], reads=[R], writes=[R])
        self.cp(Gim[0:64, :], tim[0:64, 2, 1, :], reads=[R], writes=[R])
        self.cp(Gre[64:128, :], tre[64:128, 2, 8, :], reads=[R], writes=[R])
        self.cp(Gim[64:128, :], tim[64:128, 2, 8, :], reads=[R], writes=[R])
        if self.cfg.get("prep_stop", 9) <= 3:
            return
        B.fence()
        off[0] = mark
        Bre = alloc(1024).rearrange("p (g c) -> p g c", g=64); Bim = alloc(1024).rearrange("p (g c) -> p g c", g=64)
        Cre = alloc(1024).rearrange("p (g c) -> p g c", g=64); Cim = alloc(1024).rearrange("p (g c) -> p g c", g=64)
        bbr = alloc(1024).rearrange("p (g c) -> p g c", g=64); bbi = alloc(1024).rearrange("p (g c) -> p g c", g=64)
        X = [alloc(1024).rearrange("p (g k c) -> p g k c", g=8, k=8) for _ in range(6)]
        t1 = alloc(1024).rearrange("p (g k c) -> p g k c", g=8, k=8)
        t2 = alloc(1024).rearrange("p (g k c) -> p g k c", g=8, k=8)
        tA = alloc(128); tB = alloc(128)
        toep_st = alloc(512).bitcast(BF16).rearrange("p (g x) -> p g x", g=8)
        wb_st = alloc(1024).bitcast(BF16).rearrange("p (g r x) -> p g r x", g=8, r=2)
        ca_st = alloc(1024).bitcast(BF16).rearrange("p (r g x) -> p r g x", r=2, g=8)
        RX = regions(6); Rt1, Rt2, RtA, RtB, Rtoep, Rwb, Rca = [Region() for _ in range(7)]
        B.dma("sp", Bre.rearrange("p g c -> p (g c)"), self.s5b[l, 0], writes=[R])
        B.dma("sp", Bim.rearrange("p g c -> p (g c)"), self.s5b[l, 1], writes=[R])
        B.dma("sp", Cre.rearrange("p g c -> p (g c)"), self.s5c[l, 0], writes=[R])
        B.dma("sp", Cim.rearrange("p g c -> p (g c)"), self.s5c[l, 1], writes=[R])
        bc = lambda v: v.unsqueeze(2).to_broadcast([128, 64, 16])
        self.tt(bbr, Bre, bc(cre), ALU.mult, reads=[R], writes=[R])
        self.tt(bbi, Bim, bc(cim), ALU.mult, reads=[R], writes=[R])
        self.tt(bbr, bbr, bbi, ALU.subtract, reads=[R], writes=[R])
        self.tt(bbi, Bim, bc(cre), ALU.mult, reads=[R], writes=[R])
        self.tt(Bim, Bre, bc(cim), ALU.mult, reads=[R], writes=[R])
        self.tt(bbi, bbi, Bim, ALU.add, reads=[R], writes=[R])

        def cmul(dre, dim_, Rd, are, aim, bre, bim, rds, negate_im=False):
            self.tt(dre, are, bre, ALU.mult, reads=rds, writes=[Rd[0]])
            self.tt(t1, aim, bim, ALU.mult, reads=rds, writes=[Rt1])
            self.tt(dre, dre, t1, ALU.subtract, reads=[Rd[0], Rt1], writes=[Rd[0]])
            self.tt(dim_, are, bim, ALU.mult, reads=rds, writes=[Rd[1]])
            self.tt(t2, aim, bre, ALU.mult, reads=rds, writes=[Rt2])
            if negate_im:
                self.stt(dim_, dim_, -1.0, t2, ALU.mult, ALU.subtract, reads=[Rd[1], Rt2], writes=[Rd[1]])
            else:
                self.tt(dim_, dim_, t2, ALU.add, reads=[Rd[1], Rt2], writes=[Rd[1]])

        if self.cfg.get("prep_stop", 9) <= 4:
            return
        for gb in range(8):
            g0 = gb * 8
            tabk = lambda tab, q: tab[:, q, 0:8, g0:g0 + 8].rearrange("p k g -> p g k").unsqueeze(3).to_broadcast([128, 8, 8, 16])
            gk = lambda v: v[:, g0:g0 + 8, :].unsqueeze(2).to_broadcast([128, 8, 8, 16])
            fg = lambda v: v[:, g0:g0 + 8].unsqueeze(2).unsqueeze(3).to_broadcast([128, 8, 8, 16])
            Lre, Lim, Rre, Rim, CAre, CAim = X
            cmul(Lre, Lim, (RX[0], RX[1]), tabk(tre, 1), tabk(tim, 1), gk(bbr), gk(bbi), [R])
            cmul(Rre, Rim, (RX[2], RX[3]), tabk(tre, 0), tabk(tim, 0), gk(Cre), gk(Cim), [R])
            cmul(CAre, CAim, (RX[4], RX[5]), fg(Gre), fg(Gim), Rre, Rim, [R, RX[2], RX[3]], negate_im=True)
            self.ts(Rim, Rim, -1.0, None, ALU.mult, None, reads=[RX[3]], writes=[RX[3]])
            if self.cfg.get("prep_stop", 9) <= 5:
                return
            for ri, src in enumerate((CAre, CAim)):
                self.act(ca_st[:, ri], src.rearrange("p g k c -> p g (k c)"), AF.Copy, reads=[RX[4 + ri]], writes=[Rca])
            for ri in range(2):
                for d in range(2):
                    B.dma("sp", self.ca_d[l, d, ri, g0:g0 + 8].rearrange("g p x -> p g x"),
                          ca_st[d * 64:(d + 1) * 64, ri], reads=[Rca], writes=[self.Rs5d])
            f2 = lambda x, gl: x[:, gl].rearrange("p k c -> p (k c)")
            for gl in range(8):
                g = g0 + gl
                Pfs = [self.ps(), self.ps()]
                for d in range(2):
                    sl = slice(d * 64, (d + 1) * 64)
                    Pf, Rpf = Pfs[d]
                    self.mm(Pf[:, 0:128], f2(Lre, gl)[sl, :], f2(Rre, gl)[sl, :], True, False, reads=[RX[0], RX[2]], writes=[Rpf])
                    self.mm(Pf[:, 0:128], f2(Lim, gl)[sl, :], f2(Rim, gl)[sl, :], False, True, reads=[RX[1], RX[3]], writes=[Rpf])
                self.tt(tA, Pfs[0][0][:, 0:128], self.cm[:, 5, :], ALU.mult, reads=[Pfs[0][1], self.Rconst], writes=[RtA])
                self.tt(tB, Pfs[1][0][:, 0:128], self.cm[:, 6, :], ALU.mult, reads=[Pfs[1][1], self.Rconst], writes=[RtB])
                self.tt(tA, tA, tB, ALU.add, reads=[RtA, RtB], writes=[RtA])
                self.stt(toep_st[:, gl, :], self.cm[:, 0, :], self.s5dd_s[:, l, g:g + 1], tA, ALU.mult, ALU.add,
                         reads=[RtA, self.Rconst, self.Rsm], writes=[Rtoep])
            B.dma("sp", self.toep_d[l, g0:g0 + 8].rearrange("g p x -> p g x"), toep_st, reads=[Rtoep], writes=[self.Rs5d])
            if self.cfg.get("prep_stop", 9) <= 6:
                return
            cmul(Rre, Rim, (RX[2], RX[3]), fg(Fre), fg(Fim), Lre, Lim, [R, RX[0], RX[1]])
            for gl in range(8):
                Pf, Rpf = self.ps()
                for ri, src in enumerate((Rre, Rim)):
                    B.op("pe", lambda e, src=src, ri=ri, Pf=Pf, gl=gl: e.transpose(Pf[:, ri * 128:(ri + 1) * 128],
                                                                                  src[:, gl].rearrange("p k c -> p (k c)"), self.cm[:, 0, :]),
                         reads=[RX[2 + ri], self.Rconst], writes=[Rpf])
                self.act(wb_st[:, gl].rearrange("p r x -> p (r x)"), Pf[:, 0:256], AF.Copy, reads=[Rpf], writes=[Rwb])
            B.dma("sp", self.wb_d[l, g0:g0 + 8].rearrange("g p r x -> p g r x"), wb_st, reads=[Rwb], writes=[self.Rs5d])
            yield

    def s5_phase(self, ps_, l):
        B = self.B
        B.fence()
        A = self.arena
        win = self.win
        NS, L = (1, 1024) if ps_ == 0 else (4, 256)
        NL = L // 8
        off = [0]

        def alloc(n):
            a = A[:, off[0]:off[0] + n]
            off[0] += n
            return a

        bufA = alloc(4096).bitcast(BF16)
        bufB = alloc(4096).bitcast(BF16)
        u_dl = bufA.rearrange("p (t j n) -> p t j n", t=8, j=8)
        y_unf = bufA.rearrange("p (g n) -> p g n", g=64)
        U_unf = bufB.rearrange("p (g n) -> p g n", g=64)
        y_dl = bufB.rearrange("p (t x) -> p t x", t=8)
        WE = alloc(NS * (NL + 1) * 64).bitcast(BF16).rearrange("p (s n j r) -> p s n j r", s=NS, n=NL + 1, j=64)
        st = alloc(NS * 128).rearrange("p (s j r) -> p s j r", s=NS, j=64)
        t1 = alloc(NS * 128).rearrange("p (s j r) -> p s j r", s=NS, j=64)
        t2 = alloc(NS * 128).rearrange("p (s j r) -> p s j r", s=NS, j=64)
        nw = alloc(NS * 128).rearrange("p (s j r) -> p s j r", s=NS, j=64)
        a8 = alloc(128).rearrange("p (r j) -> p r j", r=2)
        NR = 2
        s5r = [alloc(320 * 2).bitcast(BF16) for _ in range(NR)]
        tmpf = alloc(1024)
        tmpg = alloc(1024)
        assert off[0] <= 22 * 1024, off[0]
        Ru, RU, Ry, Ryd, RWE, Rst, Rt1, Rt2, Rnw, Ra8, Rtf, Rtg = [Region() for _ in range(12)]
        Rs5r = regions(NR)
        hfn = lambda kt, c: self.h[:, kt, c * 512:(c + 1) * 512]
        sgate = self.o_s5

        for pt in range(8):
            w, Rw = self.wload(win[l, CT_SU + pt], KT * 128)

            def evu(c, P_, R_, pt=pt):
                self.act(u_dl[:, pt, :, c * 64:(c + 1) * 64].rearrange("p j n -> p n j"),
                         P_[:, :].rearrange("p (n j) -> p n j", j=8), AF.Copy, reads=[R_], writes=[Ru])

            self.proj_fm(w, Rw, KT, hfn, [self.Rh], evu)
            w, Rw = self.wload(win[l, CT_SG + pt], KT * 128)

            def evg(c, P_, R_, pt=pt):
                sl = slice(c * 512, (c + 1) * 512)
                self.act(tmpf[:, 0:512], P_[:, :], AF.Exp, reads=[R_], writes=[Rtf], scale=-1.0)
                self.sig_from_exp(tmpf[:, 0:512], Rtf)
                self.tt(sgate[:, pt, sl], P_[:, :], tmpf[:, 0:512], ALU.mult, reads=[R_, Rtf], writes=[self.Ros5[pt]])

            self.proj_fm(w, Rw, KT, hfn, [self.Rh], evg)
        Rud = Region()
        for pt in range(8):
            B.dma("sp", self.ud_d[pt * 8:(pt + 1) * 8].rearrange("g c i n -> (g c) i n"), u_dl[:, pt], reads=[Ru], writes=[Rud])
        for i in range(8):
            B.dma("sp", U_unf[i * 16:(i + 1) * 16, :, :], self.ud_d[:, :, i, :].rearrange("g c n -> c g n"), reads=[Rud], writes=[RU])
        B.dma("sp", a8, self.a8_d[l].rearrange("r p j -> p r j"), reads=[self.Rs5d], writes=[Ra8])
        if ps_ == 0:
            B.dma("sp", st[:, 0], self.s5s0[l], writes=[Rst])
        else:
            B.op("dve", lambda e: e.memset(st, 0.0), writes=[Rst])
        self.act(WE[:, :, 0, :, :], st, AF.Copy, reads=[Rst], writes=[RWE])

        def load_pair(q, what):
            k = self.s5_rr
            self.s5_rr = (self.s5_rr + 1) % NR
            buf, Rb = s5r[k], Rs5r[k]
            if what == "wb":
                v = buf[:, 0:512].rearrange("p (m r x) -> p m r x", m=2, r=2)
                B.dma("sp", v, self.wb_d[l, 2 * q:2 * q + 2].rearrange("m p r x -> p m r x"), reads=[self.Rs5d], writes=[Rb])
                return v, Rb
            vt = buf[:, 0:256].rearrange("p (m x) -> p m x", m=2)
            B.dma("sp", vt, self.toep_d[l, 2 * q:2 * q + 2].rearrange("m p x -> p m x"), reads=[self.Rs5d], writes=[Rb])
            vc = buf[:, 256:768].rearrange("p (d r x) -> p d r x", d=2, r=2)
            B.dma("sp", vc, self.ca_d[l, :, :, 2 * q:2 * q + 2].rearrange("d r m p x -> (m p) d r x"), reads=[self.Rs5d], writes=[Rb])
            return (vt, vc), Rb

        self.s5_rr = 0
        for q in range(32):
            wbv, Rb = load_pair(q, "wb")
            Pw, Rpw = self.ps()
            for m in range(2):
                g = 2 * q + m
                for d in range(2):
                    for ri in range(2):
                        blk = (d * 2 + ri) * 128
                        self.mm(Pw[m * 64:(m + 1) * 64, blk:blk + 128], wbv[:, m, ri, d * 64:(d + 1) * 64], U_unf[:, g, :], True, True,
                                reads=[Rb, RU], writes=[Rpw])
            for d in range(2):
                src = Pw[:, d * 256:(d + 1) * 256].rearrange("p (r s n) -> p s n r", r=2, s=NS)
                if d == 0:
                    dst = WE[:, :, 1:NL + 1, q, :]
                else:
                    dst = WE[:, :, NL:0:-1, 32 + q, :]
                self.cp(dst, src, reads=[Rpw], writes=[RWE])
        ar = a8[:, 0, :].unsqueeze(1).unsqueeze(3).to_broadcast([128, NS, 64, 2])
        ai = a8[:, 1, :].unsqueeze(1).unsqueeze(3).to_broadcast([128, NS, 64, 2])
        for k in range(NL):
            self.tt(t1, st, ar, ALU.mult, reads=[Rst, Ra8], writes=[Rt1])
            self.tt(t2, st, ai, ALU.mult, reads=[Rst, Ra8], writes=[Rt2])
            self.tt(nw[:, :, :, 0], t1[:, :, :, 0], t2[:, :, :, 1], ALU.subtract, reads=[Rt1, Rt2], writes=[Rnw])
            self.tt(nw[:, :, :, 1], t1[:, :, :, 1], t2[:, :, :, 0], ALU.add, reads=[Rt1, Rt2], writes=[Rnw])
            self.tt(st, nw, WE[:, :, k + 1, :, :], ALU.add, reads=[Rnw, RWE], writes=[Rst])
            self.act(WE[:, :, k + 1, :, :], st, AF.Copy, reads=[Rst], writes=[RWE])
        if ps_ == 1:
            B.dma("sp", self.ns5[l], st, reads=[Rst])
        for q in range(32):
            (tv, cv_), Rb = load_pair(q, "tc")
            if q % 4 == 0:
                Pys = [self.ps(), self.ps()]
            for m in range(2):
                g = 2 * q + m
                Py, Rpy = Pys[m]
                blk = (q % 4) * 128
                self.mm(Py[:, blk:blk + 128], tv[:, m, :], U_unf[:, g, :], True, False, reads=[Rb, RU], writes=[Rpy])
                msl = slice(m * 64, (m + 1) * 64)
                for d in range(2):
                    for ri in range(2):
                        if d == 0:
                            rhs = WE[msl, :, 0:NL, q, ri]
                        else:
                            rhs = WE[msl, :, 0:NL, 32 + q, ri][:, :, ::-1]
                        self.mm(Py[:, blk:blk + 128], cv_[msl, d, ri, :], rhs, False, d == 1 and ri == 1, reads=[Rb, RWE], writes=[Rpy])
            if q % 4 == 3:
                g0 = 2 * (q - 3)
                for m in range(2):
                    Py, Rpy = Pys[m]
                    self.act(y_unf[:, g0 + m:g0 + 8:2, :], Py[:, :].rearrange("p (g n) -> p g n", g=4), AF.Copy,
                             reads=[Rpy, Rud], writes=[Ry, Ru])
        B.dma("sp", self.yd_d, y_unf, reads=[Ry], writes=[Ryd])
        Ryl = Region()
        for gl in range(8):
            B.dma("sp", y_dl[gl * 16:(gl + 1) * 16].rearrange("c t (j n) -> c t j n", j=8),
                  self.yd_d.rearrange("(j c) (t g) n -> g c t j n", c=16, g=8)[gl], reads=[Ryd], writes=[Ryl, RU])
        C2 = 2.0 * math.sqrt(2.0 / math.pi)
        a_, b_ = tmpf[:, 0:1024], tmpg[:, 0:1024]
        for pt in range(8):
            y = y_dl[:, pt, :]
            self.tt(a_, y, y, ALU.mult, reads=[Ryl], writes=[Rtf])
            self.ts(a_, a_, 0.044715, 1.0, ALU.mult, ALU.add, reads=[Rtf], writes=[Rtf])
            self.tt(a_, a_, y, ALU.mult, reads=[Rtf, Ryl], writes=[Rtf])
            self.ts(a_, a_, -25.0, None, ALU.max, None, reads=[Rtf], writes=[Rtf])
            self.act(b_, a_, AF.Exp, reads=[Rtf], writes=[Rtg], scale=-C2)
            self.sig_from_exp(b_, Rtg)
            self.tt(y, y, b_, ALU.mult, reads=[Ryl, Rtg], writes=[Ryl])
        for f in range(8):
            w, Rw = self.wload(self.wglu[l, f], 8 * 128)

            def evglu(c, P_, R_, f=f):
                self.act(tmpf[:, 0:512], P_[:, :], AF.Exp, reads=[R_], writes=[Rtf], scale=-1.0)
                self.sig_from_exp(tmpf[:, 0:512], Rtf)
                self.tt(tmpg[:, 0:512], y_dl[:, f, c * 512:(c + 1) * 512], tmpf[:, 0:512], ALU.mult, reads=[Ryl, Rtf], writes=[Rtg])
                ov = self.o_s5[:, f, :].rearrange("p (n j) -> p j n", j=8)[:, 4 * c:4 * c + 4, :]
                self.tt(ov, tmpg[:, 0:512].rearrange("p (j n) -> p j n", j=4), ov, ALU.mult, reads=[Rtg, self.Ros5[f]], writes=[self.Ros5[f]])

            self.proj_fm(w, Rw, 8, lambda kt, c: y_dl[:, kt, c * 512:(c + 1) * 512], [Ryl], evglu)

    def merge_phase(self, xsrc, ps_, l, r):
        B = self.B
        B.fence()
        A = self.arena
        merged = A[:, 0:8192].bitcast(BF16).rearrange("p (k t) -> p k t", k=KT)
        acc = A[:, 8192:9216]
        tmp = A[:, 9216:9728]
        tmp2 = A[:, 9728:10240]
        xt = [A[:, 10240:11264], A[:, 11264:12288]]
        Rm, Racc, Rt, Rt2 = Region(), Region(), Region(), Region()
        Rxt = regions(2)
        branches = ((self.o_hg, self.Rohg), (self.o_at, self.Roat), (self.o_s5, self.Ros5))
        hfn = lambda kt, c: self.h[:, kt, c * 512:(c + 1) * 512]
        for f in range(KT):
            for b, (ob, Rob) in enumerate(branches):
                w, Rw = self.wload(self.wbr[l, b, f], 8 * 128)
                Pb = [self.ps() for _ in range(2)]
                for c in range(2):
                    for kt in range(8):
                        self.mm(Pb[c][0][:, :], w[:, kt * 128:(kt + 1) * 128], ob[:, kt, c * 512:(c + 1) * 512], kt == 0, kt == 7,
                                reads=[Rw, Rob[kt]], writes=[Pb[c][1]])
                w2, Rw2 = self.wload(self.win[l, CT_MG + b * 16 + f], KT * 128)

                def evm(c, P_, R_, b=b, Pb=Pb):
                    sl = slice(c * 512, (c + 1) * 512)
                    self.act(tmp, P_[:, :], AF.Exp, reads=[R_], writes=[Rt], scale=-1.0)
                    self.sig_from_exp(tmp, Rt)
                    if b == 0:
                        self.tt(acc[:, sl], Pb[c][0][:, :], tmp, ALU.mult, reads=[Pb[c][1], Rt], writes=[Racc])
                    else:
                        self.tt(tmp2, Pb[c][0][:, :], tmp, ALU.mult, reads=[Pb[c][1], Rt], writes=[Rt2])
                        self.tt(acc[:, sl], acc[:, sl], tmp2, ALU.add, reads=[Racc, Rt2], writes=[Racc])

                self.proj_fm(w2, Rw2, KT, hfn, [self.Rh], evm)
            self.act(merged[:, f, :], acc, AF.Copy, reads=[Racc], writes=[Rm])
        for f in range(KT):
            w, Rw = self.wload(self.wout[l, f], KT * 128)
            k = f % 2
            B.dma("sp", xt[k], xsrc[ps_, f], reads=[self.Rxs[ps_][f]], writes=[Rxt[k]])

            def evo(c, P_, R_, f=f, k=k):
                sl = slice(c * 512, (c + 1) * 512)
                self.stt(xt[k][:, sl], P_[:, :], self.modT[:, l, 32 + f, r:r + 1], xt[k][:, sl], ALU.mult, ALU.add,
                         reads=[R_, self.Rmod, Rxt[k]], writes=[Rxt[k]])

            self.proj_fm(w, Rw, KT, lambda kt, c: merged[:, kt, c * 512:(c + 1) * 512], [Rm], evo)
            B.dma("sp", self.xs[ps_, f], xt[k], reads=[Rxt[k]], writes=[self.Rxs[ps_][f]])

    def final_phase(self, ps_):
        B = self.B
        B.fence()
        xst = self.arena[:, 0:8192].rearrange("p (k t) -> p k t", k=KT)
        sq = self.arena[:, 8192:12288].bitcast(BF16).rearrange("p (k t) -> p k t", k=KT)
        rstd = self.arena[:, 12288:12800]
        Rx, Rsq, Rr = Region(), Region(), Region()
        for c in range(2):
            B.dma("sp", xst, self.xs[ps_, :, :, c * 512:(c + 1) * 512].rearrange("k p t -> p k t"),
                  reads=self.Rxs[ps_], writes=[Rx])
            self.act(sq, xst, AF.Square, reads=[Rx], writes=[Rsq])
            self.rstd_from_sq(lambda kt: sq[:, kt, :], KT, [Rsq], float(D), rstd, Rr, 512)
            for kt in range(KT):
                self.stt(xst[:, kt, :], xst[:, kt, :], self.finaln_s[:, kt:kt + 1], rstd, ALU.mult, ALU.mult,
                         reads=[Rx, Rr, self.Rsm], writes=[Rx])
            B.dma("sp", self.yout[ps_, :, :, c * 512:(c + 1) * 512].rearrange("k p t -> p k t"), xst, reads=[Rx])

    def ps_grp(self, grp):
        lo, n = grp
        i = lo + self.ps_grr.get(grp, 0)
        self.ps_grr[grp] = (self.ps_grr.get(grp, 0) + 1) % n
        return self.PS[i], self.RPS[i]

    def hgrn_phase(self, ps_, l):
        B = self.B
        B.fence()
        self.ps_grr = {}
        A = self.arena
        NS, L = (1, 1024) if ps_ == 0 else (4, 256)
        off = [0]

        def alloc(n):
            a = A[:, off[0]:off[0] + n]
            off[0] += n
            assert off[0] <= 22 * 1024, off[0]
            return a

        X1 = self.o_at[:, :, :].rearrange("p k t -> p (k t)").bitcast(F32)
        X2 = self.o_s5[:, :, :].rearrange("p k t -> p (k t)").bitcast(F32)
        bf = lambda a: a.bitcast(BF16)
        k4 = lambda a: a.bitcast(BF16).rearrange("p (t c d) -> p t c d", t=8, c=4)
        S = []
        S.append(dict(qt=[bf(alloc(512)), bf(alloc(512))], ktl=[bf(alloc(512)), bf(alloc(512))],
                      khat4=[k4(alloc(2048)), k4(alloc(2048))],
                      vtok=bf(alloc(512)).rearrange("p (t c) -> p t c", t=8), sg=alloc(1024), etot=[alloc(32), alloc(32)]))
        S.append(dict(qt=[bf(X2[:, 0:512]), bf(X2[:, 512:1024])], ktl=[bf(X2[:, 1024:1536]), bf(X2[:, 1536:2048])],
                      khat4=[k4(X1[:, 0:2048]), k4(X1[:, 2048:4096])],
                      vtok=bf(X2[:, 2048:2560]).rearrange("p (t c) -> p t c", t=8), sg=X2[:, 2560:3584], etot=[alloc(32), alloc(32)]))
        for st_ in S:
            st_["R"] = dict(qt=regions(2), ktl=regions(2), khat4=regions(2), vtok=Region(), sg=Region(), etot=regions(2))
        q_f = alloc(1024); tmp = alloc(1024); tmp2 = alloc(1024)
        tmpD = [tmp, alloc(1024)]; tmp2D = [tmp2, alloc(1024)]
        RtD = [Region(), Region()]; Rt2D = [Region(), Region()]
        bb = [alloc(1024), alloc(1024)]; kk = [alloc(1024), alloc(1024)]
        khat = [bf(alloc(512)), bf(alloc(512))]
        Rq, Rt, Rt2 = Region(), Region(), Region()
        Rb, Rk, Rkh = regions(2), regions(2), regions(2)
        o_d = [alloc(1024), alloc(1024)]
        masked = [bf(alloc(64)), bf(alloc(64))]
        sqo = bf(X2[:, 3584:4096]); rstd = alloc(1024); ntmp = alloc(1024)
        Ro, Rms = regions(2), regions(2)
        Rsq, Rr, Rnt = Region(), Region(), Region()
        win = self.win
        hfn = lambda kt, c: self.h[:, kt, c * 512:(c + 1) * 512]
        G_PREP, G_REC = (4, 4), (2, 2)

        def proj_fm(w, Rw, evac):
            banks = [self.ps_grp(G_PREP) for _ in range(2)]
            for c in range(2):
                P_, R_ = banks[c]
                for kt in range(KT):
                    self.mm(P_[:, :], w[:, kt * 128:(kt + 1) * 128], hfn(kt, c), kt == 0, kt == KT - 1, reads=[Rw, self.Rh], writes=[R_])
            for c in range(2):
                evac(c, *banks[c])

        def prep(hd):
            D_ = S[hd % 2]
            RD = D_["R"]
            qt, ktl, khat4, vtok, sg, etot = D_["qt"], D_["ktl"], D_["khat4"], D_["vtok"], D_["sg"], D_["etot"]
            w, Rw = self.wload(win[l, CT_HQ + hd], KT * 128)
            proj_fm(w, Rw, lambda c, P_, R_: self.act(q_f[:, c * 512:(c + 1) * 512], P_[:, :], AF.Copy, reads=[R_], writes=[Rq]))
            yield
            w, Rw = self.wload(win[l, CT_HI + hd], KT * 128)
            for half in range(2):
                P_, R_ = self.ps_grp(G_PREP)
                for q in range(4):
                    tt_ = half * 4 + q
                    for kt in range(KT):
                        self.mm(P_[:, q * 128:(q + 1) * 128], self.h[:, kt, tt_ * 128:(tt_ + 1) * 128], w[:, kt * 128:(kt + 1) * 128],
                                kt == 0, kt == KT - 1, reads=[Rw, self.Rh], writes=[R_])
                for q in range(4):
                    self.act(vtok[:, half * 4 + q, :], P_[:, q * 128:(q + 1) * 128], AF.Copy, reads=[R_], writes=[RD["vtok"]])
                yield
            def dchain(d):
                tmp, tmp2, Rt, Rt2 = tmpD[d], tmp2D[d], RtD[d], Rt2D[d]
                w, Rw = self.wload(win[l, (CT_HFF if d == 0 else CT_HFB) + hd], KT * 128)
                proj_fm(w, Rw, lambda c, P_, R_: self.act(tmp[:, c * 512:(c + 1) * 512], P_[:, :], AF.Exp, reads=[R_], writes=[Rt], scale=-1.0))
                yield
                self.act(tmp, tmp, AF.Ln, reads=[Rt, self.Rconst], writes=[Rt], scale=1.0, bias=self.eps_t[:, 1:2])
                yield
                self.act(tmp, tmp, AF.Exp, reads=[Rt], writes=[Rt], scale=-1.0)
                yield
                self.ts(tmp, tmp, self.oml[:, d, hd, l:l + 1], self.lb[:, d, hd, l:l + 1], ALU.mult, ALU.add, reads=[Rt, self.Rsm], writes=[Rt])
                yield
                self.ts(kk[d], tmp, -1.0, 1.0, ALU.mult, ALU.add, reads=[Rt], writes=[Rk[d]])
                self.act(tmp, tmp, AF.Ln, reads=[Rt], writes=[Rt])
                yield
                B.op("dve", lambda e, d=d: e.tensor_tensor_scan(out=bb[d], data0=self.resetm[:], data1=tmp, initial=0.0, op0=ALU.mult, op1=ALU.add),
                     reads=[Rt, self.Rconst], writes=[Rb[d]])
                yield
                b3 = bb[d].rearrange("p (c j) -> p c j", j=32)
                if d == 1:
                    self.tt(tmp, tmp, bb[d], ALU.subtract, reads=[Rt, Rb[d]], writes=[Rt])
                    yield
                    self.tt(tmp2.rearrange("p (c j) -> p c j", j=32), tmp.rearrange("p (c j) -> p c j", j=32),
                            b3[:, :, 31:32].to_broadcast([128, 32, 32]), ALU.add, reads=[Rt, Rb[d]], writes=[Rt2])
                    yield
                    self.cp(bb[d], tmp2, reads=[Rt2], writes=[Rb[d]])
                    yield
                    tot = b3[:, :, 0:1]
                else:
                    tot = b3[:, :, 31:32]
                self.act(etot[d].unsqueeze(2), tot, AF.Exp, reads=[Rb[d]], writes=[RD["etot"][d]])
                self.act(tmp, bb[d], AF.Exp, reads=[Rb[d]], writes=[Rt])
                yield
                self.tt(qt[d], q_f, tmp, ALU.mult, reads=[Rq, Rt], writes=[RD["qt"][d]])
                yield
                self.act(tmp, bb[d], AF.Exp, reads=[Rb[d]], writes=[Rt], scale=-1.0)
                yield
                self.tt(tmp2, kk[d], tmp, ALU.mult, reads=[Rk[d], Rt], writes=[Rt2])
                yield
                self.act(ktl[d], tmp2, AF.Copy, reads=[Rt2], writes=[RD["ktl"][d]])
                self.tt(khat[d].rearrange("p (c j) -> p c j", j=32), tmp2.rearrange("p (c j) -> p c j", j=32),
                        etot[d].unsqueeze(2).to_broadcast([128, 32, 32]), ALU.mult, reads=[Rt2, RD["etot"][d]], writes=[Rkh[d]])
                yield
                for half in range(2):
                    P_, R_ = self.ps_grp(G_PREP)
                    Pb = P_[:, :].bitcast(BF16)
                    for qd in range(4):
                        tt_ = half * 4 + qd
                        B.op("pe", lambda e, Pb=Pb, qd=qd, tt_=tt_, d=d: e.transpose(Pb[:, qd * 128:(qd + 1) * 128], khat[d][:, tt_ * 128:(tt_ + 1) * 128], self.ident_bf[:, :]),
                             reads=[Rkh[d], self.Rconst], writes=[R_])
                    for qd in range(4):
                        tt_ = half * 4 + qd
                        self.tt(khat4[d][:, tt_, :, :], Pb[:, qd * 128:(qd + 1) * 128].unsqueeze(1).to_broadcast([128, 4, 128]),
                                self.cm[:, 3, 0:4].unsqueeze(2).to_broadcast([128, 4, 128]), ALU.mult,
                                reads=[R_, self.Rconst], writes=[RD["khat4"][d]])
                        if qd % 2 == 1:
                            yield

            gens = [dchain(0), dchain(1)]
            alive = [True, True]
            while any(alive):
                for gi in range(2):
                    if alive[gi]:
                        try:
                            next(gens[gi])
                        except StopIteration:
                            alive[gi] = False
                yield
            Rt = RtD[0]
            w, Rw = self.wload(win[l, CT_HG + hd], KT * 128)

            def evg(c, P_, R_):
                sl = slice(c * 512, (c + 1) * 512)
                self.act(tmp[:, sl], P_[:, :], AF.Exp, reads=[R_], writes=[Rt], scale=-1.0)
                self.sig_from_exp(tmp[:, sl], Rt)
                self.tt(sg[:, sl], P_[:, :], tmp[:, sl], ALU.mult, reads=[R_, Rt], writes=[RD["sg"]])

            proj_fm(w, Rw, evg)
            yield

        def rec(hd):
            D_ = S[hd % 2]
            RD = D_["R"]
            qt, ktl, khat4, vtok, sg, etot = D_["qt"], D_["ktl"], D_["khat4"], D_["vtok"], D_["sg"], D_["etot"]
            ntile = L // 128
            for sq_ in range(NS):
                for d in range(2):
                    if ps_ == 0:
                        B.dma("sp", self.S_f[d][:, :], self.hst[l, d, hd], writes=[self.RS[d]])
                        self.act(self.S_b[d][:, :], self.S_f[d][:, :], AF.Copy, reads=[self.RS[d]], writes=[self.RS[d]])
                    else:
                        B.op("dve", lambda e, d=d: e.memset(self.S_f[d][:, :], 0.0), writes=[self.RS[d]])
                        B.op("dve", lambda e, d=d: e.memset(self.S_b[d][:, :], 0.0), writes=[self.RS[d]])
                for it in range(ntile):
                    for d in range(2):
                        tl = sq_ * ntile + (it if d == 0 else ntile - 1 - it)
                        ts_ = slice(tl * 128, (tl + 1) * 128)
                        Psc, Rsc = self.ps_grp(G_REC)
                        self.mm(Psc[:, 0:128], ktl[d][:, ts_], qt[d][:, ts_], True, True, reads=[RD["ktl"][d], RD["qt"][d]], writes=[Rsc])
                        self.tt(masked[d], Psc[:, 0:128], self.cm[:, 1 + d, :], ALU.mult, reads=[Rsc, self.Rconst], writes=[Rms[d]])
                        Po, Rpo = self.PS[d], self.RPS[d]
                        self.mm(Po[:, 0:128], vtok[:, tl, :], masked[d], True, False, reads=[RD["vtok"], Rms[d]], writes=[Rpo])
                        corder = range(4) if d == 0 else range(3, -1, -1)
                        for ci, c in enumerate(corder):
                            gc = tl * 4 + c
                            cs = slice(tl * 128 + c * 32, tl * 128 + (c + 1) * 32)
                            self.mm(Po[:, c * 32:(c + 1) * 32], self.S_b[d][:, :], qt[d][:, cs], False, ci == 3,
                                    reads=[self.RS[d], RD["qt"][d]], writes=[Rpo])
                            Pu, Rpu = self.ps_grp(G_REC)
                            self.mm(Pu[:, 0:128], khat4[d][:, tl, c, :], vtok[:, tl, :], True, True, reads=[RD["khat4"][d], RD["vtok"]], writes=[Rpu])
                            self.stt(self.S_f[d][:, :], self.S_f[d][:, :], etot[d][:, gc:gc + 1], Pu[:, 0:128], ALU.mult, ALU.add,
                                     reads=[self.RS[d], RD["etot"][d], Rpu], writes=[self.RS[d]])
                            self.act(self.S_b[d][:, :], self.S_f[d][:, :], AF.Copy, reads=[self.RS[d]], writes=[self.RS[d]])
                            yield
                        self.act(o_d[d][:, ts_], Po[:, 0:128], AF.Copy, reads=[Rpo], writes=[Ro[d]])
                if ps_ == 1:
                    for d in range(2):
                        B.dma("sp", self.nhs[sq_, l, d, hd], self.S_f[d][:, :], reads=[self.RS[d]])
            self.tt(o_d[0], o_d[0], o_d[1], ALU.add, reads=[Ro[0], Ro[1]], writes=[Ro[0]])
            self.act(sqo, o_d[0], AF.Square, reads=[Ro[0]], writes=[Rsq])
            yield
            for c in range(2):
                P_, R_ = self.ps_grp(G_REC)
                self.mm(P_[:, 0:512], self.ones_bf[:, :], sqo[:, c * 512:(c + 1) * 512], True, True, reads=[Rsq, self.Rconst], writes=[R_])
                self.act(rstd[:, c * 512:(c + 1) * 512], P_[:, 0:512], AF.Ln, reads=[R_, self.Rconst], writes=[Rr], scale=1.0 / 128.0, bias=self.eps_t[:, 0:1])
            self.act(rstd, rstd, AF.Exp, reads=[Rr], writes=[Rr], scale=-0.5)
            yield
            self.stt(ntmp, o_d[0], self.onw_s[:, l:l + 1], rstd, ALU.mult, ALU.mult, reads=[Ro[0], Rr, self.Rsm], writes=[Rnt])
            self.tt(self.o_hg[:, hd, :], ntmp, sg, ALU.mult, reads=[Rnt, RD["sg"]], writes=[self.Rohg[hd]])
            yield

        heads = list(self.cfg.get("heads", range(8)))
        for _ in prep(heads[0]):
            pass
        for i, hd in enumerate(heads):
            gr = rec(hd)
            gp = prep(heads[i + 1]) if i + 1 < len(heads) else iter(())
            done_r = done_p = False
            while not (done_r and done_p):
                if not done_r:
                    try:
                        next(gr)
                    except StopIteration:
                        done_r = True
                if not done_p:
                    try:
                        next(gp)
                    except StopIteration:
                        done_p = True
        B.fence()

    def attn_phase(self, ps_, l):
        B = self.B
        B.fence()
        A = self.arena
        win = self.win
        NS, L = (1, 1024) if ps_ == 0 else (4, 256)
        sample = ps_ == 0
        koff = 256 if sample else 0
        NK = koff + 1024
        off = [0]

        def alloc(n):
            a = A[:, off[0]:off[0] + n]
            off[0] += n
            return a

        k_use = alloc(NK).bitcast(BF16).rearrange("p (h n) -> p h n", h=2)
        vtok = alloc(1280).bitcast(BF16).rearrange("p (t h d) -> p t h d", t=10, h=2)
        kraw = alloc(1024); rstd = alloc(1024); kn = alloc(1024); sg = alloc(1024); tmp = alloc(1024); tmp2 = alloc(1024)
        sqk = alloc(512).bitcast(BF16)
        q_use = alloc(512).bitcast(BF16)
        vtf = alloc(2048).rearrange("p (t c) -> p t c", t=8)
        Eb = [alloc(256).bitcast(BF16) for _ in range(4)]
        rden = alloc(512)
        if sample:
            rC = alloc(1024); rS = alloc(1024)
        assert off[0] <= 20 * 1024, off[0]
        Rk, Rsq, Rr, Rkn, Rv, Rvt, Rku, Rsg, Rt, Rt2, Rqu, Rrd, Rrope = [Region() for _ in range(13)]
        RE = regions(4)
        hfn = lambda kt, c: self.h[:, kt, c * 512:(c + 1) * 512]
        if sample:
            B.dma("sp", rC, self.ropeC, writes=[Rrope])
            B.dma("sp", rS, self.ropeS, writes=[Rrope])

        def normed(ct, nw_ap):
            w, Rw = self.wload(win[l, ct], KT * 128)

            def evac(c, P_, R_):
                self.act(kraw[:, c * 512:(c + 1) * 512], P_[:, :], AF.Copy, reads=[R_], writes=[Rk])
                self.act(sqk[:, c * 512:(c + 1) * 512], P_[:, :], AF.Square, reads=[R_], writes=[Rsq])

            self.proj_fm(w, Rw, KT, hfn, [self.Rh], evac)
            for c in range(2):
                self.rstd_from_sq(lambda kt, c=c: sqk[:, c * 512:(c + 1) * 512], 1, [Rsq], 128.0,
                                  rstd[:, c * 512:(c + 1) * 512], Rr, 512)
            self.stt(kn, kraw, nw_ap, rstd, ALU.mult, ALU.mult, reads=[Rk, Rr, self.Rsm], writes=[Rkn])

        def rope_to(dst, Rdst):
            for c in range(2):
                sl = slice(c * 512, (c + 1) * 512)
                P_, R_ = self.ps()
                self.mm(P_[:, :], self.cm[:, 4, :], kn[:, sl], True, True, reads=[Rkn, self.Rconst], writes=[R_])
                self.tt(tmp[:, sl], kn[:, sl], rC[:, sl], ALU.mult, reads=[Rkn, Rrope], writes=[Rt])
                self.tt(tmp2[:, sl], P_[:, :], rS[:, sl], ALU.mult, reads=[R_, Rrope], writes=[Rt2])
                self.tt(dst[:, sl], tmp[:, sl], tmp2[:, sl], ALU.add, reads=[Rt, Rt2], writes=[Rdst])

        for kvh in range(2):
            normed(CT_AK + kvh, self.knw_s[:, l:l + 1])
            if sample:
                rope_to(k_use[:, kvh, koff:koff + 1024], Rku)
                B.dma("sp", tmp[:, 0:256], self.ck[l, kvh], reads=[Rt], writes=[Rt])
                self.act(k_use[:, kvh, 0:256], tmp[:, 0:256], AF.Copy, reads=[Rt], writes=[Rku])
            else:
                B.dma("sp", self.nck[l, kvh], kn, reads=[Rkn])
                self.act(k_use[:, kvh, 0:1024], kn, AF.Copy, reads=[Rkn], writes=[Rku])
        kt0 = 2 if sample else 0
        for kvh in range(2):
            w, Rw = self.wload(win[l, CT_AV + kvh], KT * 128)

            def evacv(tt, P_, R_, kvh=kvh):
                self.act(vtok[:, kt0 + tt, kvh, :], P_, AF.Copy, reads=[R_], writes=[Rvt])
                if not sample:
                    self.act(vtf[:, tt, kvh * 128:(kvh + 1) * 128], P_, AF.Copy, reads=[R_], writes=[Rv])

            self.proj_tm(w, Rw, evacv)
        if sample:
            B.dma("sp", vtf[:, 0:2, :], self.cv[l].rearrange("(t p) c -> p t c", p=128), writes=[Rv])
            self.act(vtok[:, 0:2, :, :], vtf[:, 0:2, :].rearrange("p t (h d) -> p t h d", h=2), AF.Copy, reads=[Rv], writes=[Rvt])
        else:
            B.dma("sp", self.ncv[l].rearrange("(t p) c -> p t c", p=128), vtf, reads=[Rv])
        scale = 128.0 ** -0.5
        e_rr = 0
        a_rr = 0
        s_rr = 0
        for hq in range(self.cfg.get("n_qheads", 8)):
            kvh = hq // 4
            normed(CT_AQ + hq, self.qnw_s[:, l:l + 1])
            if sample:
                rope_to(q_use, Rqu)
            else:
                self.act(q_use, kn, AF.Copy, reads=[Rkn], writes=[Rqu])
            w, Rw = self.wload(win[l, CT_AG + hq], KT * 128)

            def evg(c, P_, R_):
                sl = slice(c * 512, (c + 1) * 512)
                self.act(tmp[:, sl], P_[:, :], AF.Exp, reads=[R_], writes=[Rt], scale=-1.0)
                self.sig_from_exp(tmp[:, sl], Rt)
                self.tt(sg[:, sl], P_[:, :], tmp[:, sl], ALU.mult, reads=[R_, Rt], writes=[Rsg])

            self.proj_fm(w, Rw, KT, hfn, [self.Rh], evg)
            for sq_ in range(NS):
                if sample:
                    ktiles = list(range(10))
                    qchunks = [(0, 512), (512, 512)]
                else:
                    ktiles = [2 * sq_, 2 * sq_ + 1]
                    qchunks = [(sq_ * 256, 256)]
                for (q0, qn_) in qchunks:
                    Po, Rpo = self.PS[a_rr], self.RPS[a_rr]
                    Pd, Rpd = self.PS[2 + a_rr], self.RPS[2 + a_rr]
                    a_rr = 1 - a_rr
                    for ji, j in enumerate(ktiles):
                        Psc, Rsc = self.PS[4 + s_rr], self.RPS[4 + s_rr]
                        s_rr = (s_rr + 1) % 4
                        self.mm(Psc[:, 0:qn_], k_use[:, kvh, j * 128:(j + 1) * 128], q_use[:, q0:q0 + qn_], True, True,
                                reads=[Rku, Rqu], writes=[Rsc])
                        E, RE_ = Eb[e_rr], RE[e_rr]
                        e_rr = (e_rr + 1) % 4
                        self.act(E[:, 0:qn_], Psc[:, 0:qn_], AF.Exp, reads=[Rsc], writes=[RE_], scale=scale)
                        first, last = ji == 0, ji == len(ktiles) - 1
                        self.mm(Po[:, 0:qn_], vtok[:, j, kvh, :], E[:, 0:qn_], first, last, reads=[Rvt, RE_], writes=[Rpo])
                        self.mm(Pd[:, 0:qn_], self.ones_bf[:, :], E[:, 0:qn_], first, last, reads=[self.Rconst, RE_], writes=[Rpd])
                    self.act(rden[:, 0:qn_], Pd[:, 0:qn_], AF.Ln, reads=[Rpd], writes=[Rrd])
                    self.act(rden[:, 0:qn_], rden[:, 0:qn_], AF.Exp, reads=[Rrd], writes=[Rrd], scale=-1.0)
                    self.tt(tmp2[:, q0:q0 + qn_], Po[:, 0:qn_], rden[:, 0:qn_], ALU.mult, reads=[Rpo, Rrd], writes=[Rt2])
                    self.tt(self.o_at[:, hq, q0:q0 + qn_], tmp2[:, q0:q0 + qn_], sg[:, q0:q0 + qn_], ALU.mult,
                            reads=[Rt2, Rsg], writes=[self.Roat[hq]])


def _const_mats():
    cm = np.zeros((128, 8, 128), np.float32)
    cm[:, 0, :] = np.eye(128, dtype=np.float32)
    idx = np.arange(128)
    same = (idx[:, None] // 32) == (idx[None, :] // 32)
    cm[:, 1, :] = (same & (idx[:, None] <= idx[None, :])).astype(np.float32)
    cm[:, 2, :] = (same & (idx[:, None] >= idx[None, :])).astype(np.float32)
    cm[:, 3, 0:4] = (idx[:, None] // 32 == np.arange(4)[None, :]).astype(np.float32)
    partner = np.where((idx % 64) < 32, idx + 32, idx - 32)
    cm[partner, 4, idx] = 1.0
    blk = idx // 16
    cm[:, 5, :] = (blk[:, None] <= blk[None, :]).astype(np.float32)
    cm[:, 6, :] = (blk[:, None] >= blk[None, :]).astype(np.float32)
    return cm


def _rope_tables():
    f = np.float32
    t = np.arange(T)
    row = (t // 64).astype(f)
    col = (t % 64).astype(f)
    inv = (f(10000.0) ** (-(np.arange(32, dtype=f) / f(32)))).astype(f)
    ar = (row[None, :] * inv[:, None]).astype(f)
    ac = (col[None, :] * inv[:, None]).astype(f)
    C = np.concatenate([np.cos(ar), np.cos(ar), np.cos(ac), np.cos(ac)], 0).astype(f)
    S = np.concatenate([-np.sin(ar), np.sin(ar), -np.sin(ac), np.sin(ac)], 0).astype(f)
    return np.ascontiguousarray(C), np.ascontiguousarray(S)


def host_inputs(inp, nL=DEPTH):
    f = np.float32
    shared = {}
    wm = inp["w_mod"][:nL].reshape(nL, KT, 128, 48, 128).transpose(0, 3, 2, 1, 4)
    shared["wmod"] = np.ascontiguousarray(wm).reshape(nL, 48, 128, KT * 128)
    shared["bmod"] = np.ascontiguousarray(inp["b_mod"][:nL].reshape(nL, 48, 128).transpose(2, 0, 1))
    shared["normw"] = np.ascontiguousarray(inp["norm_w"][:nL].reshape(nL, KT, 128).transpose(2, 0, 1))
    shared["finaln"] = np.ascontiguousarray(inp["final_norm"].reshape(KT, 128).T)
    wi = inp["w_in"][:nL].reshape(nL, KT, 128, NCT, 128).transpose(0, 3, 2, 1, 4)
    shared["win"] = np.ascontiguousarray(wi).reshape(nL, NCT, 128, KT * 128)
    shared["qnw"] = np.ascontiguousarray(inp["at_q_norm"][:nL].T)
    shared["knw"] = np.ascontiguousarray(inp["at_k_norm"][:nL].T)
    shared["cmat"] = _const_mats()
    shared["ropeC"], shared["ropeS"] = _rope_tables()
    are = inp["s5_a_re"][:nL].transpose(0, 1, 3, 2).reshape(nL, 128, 64)
    aim = inp["s5_a_im"][:nL].transpose(0, 1, 3, 2).reshape(nL, 128, 64)
    ldt = np.broadcast_to(inp["s5_log_dt"][:nL][:, :, None, :], (nL, 2, 64, 64)).reshape(nL, 128, 64)
    shared["s5p"] = np.ascontiguousarray(np.stack([are, aim, ldt], 2))
    bre = inp["s5_b_re"][:nL].transpose(0, 1, 3, 2, 4).reshape(nL, 128, 1024)
    bim = inp["s5_b_im"][:nL].transpose(0, 1, 3, 2, 4).reshape(nL, 128, 1024)
    shared["s5b"] = np.ascontiguousarray(np.stack([bre, bim], 1))
    cre = inp["s5_c_re"][:nL].transpose(0, 3, 1, 2).reshape(nL, 1, 64, 1024)
    cim = inp["s5_c_im"][:nL].transpose(0, 3, 1, 2).reshape(nL, 1, 64, 1024)
    cre = np.broadcast_to(cre, (nL, 2, 64, 1024)).reshape(nL, 128, 1024)
    cim = np.broadcast_to(cim, (nL, 2, 64, 1024)).reshape(nL, 128, 1024)
    shared["s5c"] = np.ascontiguousarray(np.stack([cre, cim], 1))
    dd = inp["s5_d"][:nL].reshape(nL, 64, 16)
    shared["s5dd"] = np.ascontiguousarray(np.broadcast_to(dd.transpose(2, 0, 1)[None], (8, 16, nL, 64)).reshape(128, nL, 64))
    sg = np.where(np.arange(128) < 64, 1.0, -1.0).astype(f)[:, None] * np.arange(9, dtype=f)[None, :]
    shared["sgnk"] = np.ascontiguousarray(np.stack([sg, -sg, np.broadcast_to(np.arange(9, dtype=f), (128, 9))], 1))
    wg = inp["s5_w_glu"][:nL].reshape(nL, 8, 128, 8, 128).transpose(0, 3, 2, 1, 4)
    shared["wglu"] = np.ascontiguousarray(wg).reshape(nL, 8, 128, 1024)
    wbr = np.stack([inp[k][:nL].reshape(nL, 8, 128, KT, 128).transpose(0, 3, 2, 1, 4) for k in ("w_br_hg", "w_br_at", "w_br_s5")], 1)
    shared["wbr"] = np.ascontiguousarray(wbr).reshape(nL, 3, KT, 128, 1024)
    wo = inp["w_out"][:nL].reshape(nL, KT, 128, KT, 128).transpose(0, 3, 2, 1, 4)
    shared["wout"] = np.ascontiguousarray(wo).reshape(nL, KT, 128, KT * 128)
    shared["lblog"] = np.ascontiguousarray(inp["hg_lb_logits"].reshape(2, DEPTH, 8, 128).transpose(3, 0, 2, 1))
    shared["onw"] = np.ascontiguousarray(inp["hg_onorm"][:nL].T)
    maps = []
    for i in range(NCORES):
        m = dict(shared)
        xs = inp["x_sample"][i]
        xp = inp["x_prompt"][4 * i:4 * i + 4].reshape(T, D)
        xin = np.stack([xs.T.reshape(KT, 128, T), xp.T.reshape(KT, 128, T)], 0)
        m["xin"] = np.ascontiguousarray(xin)
        cond = np.stack([inp["c"][i].reshape(KT, 128).T, inp["c_ctx"].reshape(KT, 128).T], -1)
        m["cond"] = np.ascontiguousarray(cond)
        m["hst"] = np.ascontiguousarray(inp["state_hgrn"][i, :nL])
        s0 = np.stack([inp["state_s5_re"][i, :nL], inp["state_s5_im"][i, :nL]], -1)
        s0 = s0.reshape(nL, 2, 32, 2, 64, 2).transpose(0, 3, 4, 1, 2, 5)
        m["s5s0"] = np.ascontiguousarray(s0).reshape(nL, 128, 64, 2)
        m["ck"] = np.ascontiguousarray(inp["cache_k"][i, :nL].transpose(0, 2, 3, 1))
        m["cv"] = np.ascontiguousarray(inp["cache_v"][i, :nL].reshape(nL, 256, 256))
        maps.append(m)
    return maps


_PROG_CACHE = {}


def run(inp, cfg):
    key = repr(sorted(cfg.items()))
    if key not in _PROG_CACHE:
        _PROG_CACHE[key] = Prog(cfg)
    prog = _PROG_CACHE[key]
    maps = host_inputs(inp, cfg.get("n_layers_alloc", DEPTH))
    maps = [{k: v for k, v in m.items() if k in prog.din} for m in maps]
    res = run_bass_kernel_spmd(prog.nc, maps, core_ids=list(range(NCORES)))
    return res.results


def kernel(**inputs):
    inp = {k: np.asarray(v) for k, v in inputs.items()}
    res = run(inp, {})
    L = DEPTH
    y = np.stack([r["y"] for r in res], 0)
    y_sample = np.ascontiguousarray(y[:, 0].reshape(NCORES, D, T).transpose(0, 2, 1))
    y_prompt = np.ascontiguousarray(y[:, 1].reshape(NCORES, D, T).transpose(0, 2, 1)).reshape(32, 256, D)
    nck = np.stack([r["nck"] for r in res], 0)
    new_k = np.ascontiguousarray(nck.reshape(NCORES, L, 2, 128, 4, 256).transpose(0, 4, 1, 5, 2, 3)).reshape(32, L, 256, 2, 128)
    ncv = np.stack([r["ncv"] for r in res], 0)
    new_v = np.ascontiguousarray(ncv.reshape(NCORES, L, 4, 256, 2, 128).transpose(0, 2, 1, 3, 4, 5)).reshape(32, L, 256, 2, 128)
    nhs = np.stack([r["nhs"] for r in res], 0).reshape(32, L, 2, 8, 128, 128)
    ns5 = np.stack([r["ns5"] for r in res], 0)
    a = ns5.reshape(NCORES, L, 2, 64, 4, 2, 32, 2).transpose(0, 4, 1, 5, 6, 2, 3, 7).reshape(32, L, 2, 64, 64, 2)
    f = np.float32
    return (y_prompt.astype(f), y_sample.astype(f), new_k.astype(f), new_v.astype(f), np.ascontiguousarray(nhs).astype(f),
            np.ascontiguousarray(a[..., 0]).astype(f), np.ascontiguousarray(a[..., 1]).astype(f))
```

```python
import contextlib
import math
import numpy as np
import concourse.bass as bass
import concourse.mybir as mybir
from concourse.bass_utils import run_bass_kernel_spmd

F32 = mybir.dt.float32
BF16 = mybir.dt.bfloat16
I32 = mybir.dt.int32
AF = mybir.ActivationFunctionType
ALU = mybir.AluOpType
AX = mybir.AxisListType

ENGS = ("pe", "act", "dve", "pool", "sp")

D = 2048
KT = 16
T = 1024
DEPTH = 4
EPS = 1e-6
NCORES = 8
IN_COLS = 15872
NCT = IN_COLS // 128
CT_HQ, CT_HI, CT_HFF, CT_HFB, CT_HG = 0, 8, 16, 24, 32
CT_AQ, CT_AK, CT_AV, CT_AG = 40, 48, 50, 52
CT_SU, CT_SG = 60, 68
CT_MG = 76


class Region:
    __slots__ = ("w", "r")

    def __init__(self):
        self.w = None
        self.r = []


def regions(n):
    return [Region() for _ in range(n)]


class Builder:
    def __init__(self, nc, n_dma_sems=20, n_w_sems=6):
        self.nc = nc
        self.stack = contextlib.ExitStack()
        self.cnt = {e: 0 for e in ENGS}
        self.seen = {e: {} for e in ENGS}
        self.prog = {e: [] for e in ENGS}
        self.semobj = {}
        for e in ENGS:
            self.semobj[("c", e)] = self.stack.enter_context(nc.semaphore("c_" + e))
        self.dcnt = [0] * n_dma_sems
        for i in range(n_dma_sems):
            self.semobj[("d", i)] = self.stack.enter_context(nc.semaphore("d%d" % i))
        self.wcnt = [0] * n_w_sems
        for i in range(n_w_sems):
            self.semobj[("w", i)] = self.stack.enter_context(nc.semaphore("w%d" % i))
        self.drr = 0
        self.n_ops = 0

    def sbuf(self, name, shape, dt):
        return self.stack.enter_context(self.nc.sbuf_tensor(name, list(shape), dt))

    def psum(self, name, shape, dt):
        return self.stack.enter_context(self.nc.psum_tensor(name, list(shape), dt))

    def _waits(self, eng, reads, writes):
        need = {}
        for r in reads:
            if r.w is not None:
                k, v = r.w
                if need.get(k, 0) < v:
                    need[k] = v
        for w in writes:
            if w.w is not None:
                k, v = w.w
                if need.get(k, 0) < v:
                    need[k] = v
            for (k, v) in w.r:
                if need.get(k, 0) < v:
                    need[k] = v
        out = []
        seen = self.seen[eng]
        for k, v in need.items():
            if k == ("c", eng) and eng == "pe":
                continue
            if seen.get(k, 0) >= v:
                continue
            seen[k] = v
            out.append((k, v))
        return out

    def _commit(self, ev, reads, writes):
        for r in reads:
            r.r.append(ev)
            if len(r.r) > 64:
                mx = {}
                for (k, v) in r.r:
                    if mx.get(k, 0) < v:
                        mx[k] = v
                r.r = list(mx.items())
        for w in writes:
            w.w = ev
            w.r = []

    def op(self, eng, fn, reads=(), writes=()):
        waits = self._waits(eng, reads, writes)
        self.cnt[eng] += 1
        ev = (("c", eng), self.cnt[eng])
        self.prog[eng].append((waits, fn, ("c", eng), 1))
        self._commit(ev, reads, writes)
        self.n_ops += 1
        return ev

    def dma(self, q, out, in_, reads=(), writes=(), wslot=None, **kw):
        waits = self._waits(q, reads, writes)
        if wslot is None:
            i = self.drr
            self.drr = (self.drr + 1) % len(self.dcnt)
            self.dcnt[i] += 16
            ev = (("d", i), self.dcnt[i])
        else:
            self.wcnt[wslot] += 16
            ev = (("w", wslot), self.wcnt[wslot])

        def fn(e, out=out, in_=in_, kw=kw):
            return e.dma_start(out=out, in_=in_, **kw)

        self.prog[q].append((waits, fn, ev[0], 16))
        self._commit(ev, reads, writes)
        self.n_ops += 1
        return ev

    def fence(self):
        for e in ("pe", "act", "dve", "sp"):
            self.wait_all(e, engs=("pe", "act", "dve", "sp"))

    def wait_all(self, eng, engs=ENGS):
        waits = []
        for e in engs:
            if e == eng:
                continue
            if self.cnt[e] > self.seen[eng].get(("c", e), 0):
                self.seen[eng][("c", e)] = self.cnt[e]
                waits.append((("c", e), self.cnt[e]))
        for i, c in enumerate(self.dcnt):
            if c > self.seen[eng].get(("d", i), 0):
                self.seen[eng][("d", i)] = c
                waits.append((("d", i), c))
        self.prog[eng].append((waits, None, None, 0))

    def emit(self):
        nc = self.nc
        handles = {"pe": "tensor", "act": "scalar", "dve": "vector", "pool": "gpsimd", "sp": "sync"}
        semobj = self.semobj
        with nc.Block() as block:
            for e in ENGS:
                prog = self.prog[e]

                def body(h, prog=prog):
                    for waits, fn, inc, amt in prog:
                        for (k, v) in waits:
                            h.wait_ge(semobj[k], v)
                        if fn is not None:
                            fn(h).then_inc(semobj[inc], amt)

                getattr(block, handles[e])(body)
        self.stack.close()


class Prog:
    def __init__(self, cfg):
        self.cfg = cfg
        self.layers = cfg.get("layers", list(range(DEPTH)))
        self.passes = cfg.get("passes", [0, 1])
        self.nL = cfg.get("n_layers_alloc", DEPTH)
        nc = bass.Bass("TRN2", target_bir_lowering=False)
        self.nc = nc
        self.B = Builder(nc)
        self.din = {}
        self.dout = {}
        self.build()

    def inp(self, name, shape, dt=F32):
        t = self.nc.dram_tensor(name, list(shape), dt, kind="ExternalInput").ap()
        self.din[name] = t
        return t

    def outp(self, name, shape, dt=F32):
        t = self.nc.dram_tensor(name, list(shape), dt, kind="ExternalOutput").ap()
        self.dout[name] = t
        return t

    def scratch(self, name, shape, dt=F32):
        return self.nc.dram_tensor(name, list(shape), dt).ap()

    def ps(self):
        i = self.ps_rr
        self.ps_rr = (self.ps_rr + 1) % 8
        return self.PS[i], self.RPS[i]

    def wload(self, src, ncols):
        s = self.w_rr
        self.w_rr = (self.w_rr + 1) % self.NSLOT
        self.B.dma("pool", self.ring[:, s, 0:ncols], src, writes=[self.Rring[s]], wslot=s)
        return self.ring[:, s, :], self.Rring[s]

    def mm(self, out, lhsT, rhs, start, stop, reads, writes):
        self.B.op("pe", lambda e: e.matmul(out, lhsT, rhs, start=start, stop=stop), reads=reads, writes=writes)

    def act(self, out, in_, func, reads, writes, scale=1.0, bias=None):
        if bias is None:
            self.B.op("act", lambda e: e.activation(out=out, in_=in_, func=func, scale=scale), reads=reads, writes=writes)
        else:
            self.B.op("act", lambda e: e.activation(out=out, in_=in_, func=func, scale=scale, bias=bias), reads=reads, writes=writes)

    def tt(self, out, in0, in1, op, reads, writes, eng="dve"):
        self.B.op(eng, lambda e: e.tensor_tensor(out=out, in0=in0, in1=in1, op=op), reads=reads, writes=writes)

    def ts(self, out, in0, s1, s2, op0, op1, reads, writes, eng="dve"):
        if s2 is None:
            self.B.op(eng, lambda e: e.tensor_scalar(out=out, in0=in0, scalar1=s1, scalar2=None, op0=op0), reads=reads, writes=writes)
        else:
            self.B.op(eng, lambda e: e.tensor_scalar(out=out, in0=in0, scalar1=s1, scalar2=s2, op0=op0, op1=op1), reads=reads, writes=writes)

    def stt(self, out, in0, scalar, in1, op0, op1, reads, writes, eng="dve"):
        self.B.op(eng, lambda e: e.scalar_tensor_tensor(out=out, in0=in0, scalar=scalar, in1=in1, op0=op0, op1=op1), reads=reads, writes=writes)

    def cp(self, out, in_, reads, writes, eng="dve"):
        self.B.op(eng, lambda e: e.tensor_copy(out=out, in_=in_), reads=reads, writes=writes)

    def sig_from_exp(self, buf, R_):
        self.act(buf, buf, AF.Ln, reads=[R_, self.Rconst], writes=[R_], scale=1.0, bias=self.eps_t[:, 1:2])
        self.act(buf, buf, AF.Exp, reads=[R_], writes=[R_], scale=-1.0)

    def recip(self, out, in_, reads, writes):
        self.B.op("dve", lambda e: e.reciprocal(out=out, in_=in_), reads=reads, writes=writes)

    def proj_fm(self, w, Rw, kt_n, rhs_fn, rhs_regs, evac):
        banks = [self.ps() for _ in range(2)]
        for c in range(2):
            P_, R_ = banks[c]
            for kt in range(kt_n):
                self.mm(P_[:, :], w[:, kt * 128:(kt + 1) * 128], rhs_fn(kt, c), kt == 0, kt == kt_n - 1,
                        reads=[Rw] + rhs_regs, writes=[R_])
        for c in range(2):
            P_, R_ = banks[c]
            evac(c, P_, R_)

    def proj_tm(self, w, Rw, evac):
        for half in range(2):
            P_, R_ = self.ps()
            for q in range(4):
                tt = half * 4 + q
                for kt in range(KT):
                    self.mm(P_[:, q * 128:(q + 1) * 128], self.h[:, kt, tt * 128:(tt + 1) * 128],
                            w[:, kt * 128:(kt + 1) * 128], kt == 0, kt == KT - 1,
                            reads=[Rw, self.Rh], writes=[R_])
            for q in range(4):
                evac(half * 4 + q, P_[:, q * 128:(q + 1) * 128], R_)

    def rstd_from_sq(self, sq_fn, n_kt, sq_regs, denom, out, Rout, ncols):
        P_, R_ = self.ps()
        for kt in range(n_kt):
            self.mm(P_[:, 0:ncols], self.ones_bf[:, :], sq_fn(kt), kt == 0, kt == n_kt - 1,
                    reads=sq_regs + [self.Rconst], writes=[R_])
        self.act(out, P_[:, 0:ncols], AF.Ln, reads=[R_, self.Rconst], writes=[Rout], scale=1.0 / denom, bias=self.eps_t[:, 0:1])
        self.act(out, out, AF.Exp, reads=[Rout], writes=[Rout], scale=-0.5)

    def build(self):
        nc, B = self.nc, self.B
        nL = self.nL
        xin = self.inp("xin", [2, KT, 128, T])
        cond = self.inp("cond", [128, KT, 2])
        wmod = self.inp("wmod", [nL, 48, 128, KT * 128])
        bmod = self.inp("bmod", [128, nL, 48])
        normw = self.inp("normw", [128, nL, KT])
        finaln = self.inp("finaln", [128, KT])
        win = self.inp("win", [nL, NCT, 128, KT * 128])
        qnw = self.inp("qnw", [128, nL])
        knw = self.inp("knw", [128, nL])
        cmat = self.inp("cmat", [128, 8, 128])
        lblog = self.inp("lblog", [128, 2, 8, DEPTH])
        onw = self.inp("onw", [128, nL])
        if 0 in self.passes:
            self.hst = self.inp("hst", [nL, 2, 8, 128, 128])
        self.nhs = self.outp("nhs", [4, nL, 2, 8, 128, 128])
        self.s5p = self.inp("s5p", [nL, 128, 3, 64])
        self.s5b = self.inp("s5b", [nL, 2, 128, 64 * 16])
        self.s5c = self.inp("s5c", [nL, 2, 128, 64 * 16])
        s5dd = self.inp("s5dd", [128, nL, 64])
        sgnk_d = self.inp("sgnk", [128, 3, 9])
        self.wglu = self.inp("wglu", [nL, 8, 128, 8 * 128])
        self.wbr = self.inp("wbr", [nL, 3, KT, 128, 8 * 128])
        self.wout = self.inp("wout", [nL, KT, 128, KT * 128])
        if 0 in self.passes:
            self.s5s0 = self.inp("s5s0", [nL, 128, 64, 2])
        self.ns5 = self.outp("ns5", [nL, 128, 4, 64, 2])
        mk = self.outp if self.cfg.get("dbg_s5") else self.scratch
        self.toep_d = mk("toep_d", [nL, 64, 128, 128], BF16)
        self.wb_d = mk("wb_d", [nL, 64, 128, 2, 128], BF16)
        self.ca_d = mk("ca_d", [nL, 2, 2, 64, 64, 128], BF16)
        self.a8_d = mk("a8_d", [nL, 2, 128, 64], F32)
        self.ud_d = self.scratch("ud_d", [64, 16, 8, 128], BF16)
        self.yd_d = self.scratch("yd_d", [128, 64, 128], BF16)
        self.win = win
        if 0 in self.passes:
            self.ck = self.inp("ck", [nL, 2, 128, 256])
            self.cv = self.inp("cv", [nL, 256, 256])
            self.ropeC = self.inp("ropeC", [128, T])
            self.ropeS = self.inp("ropeS", [128, T])
        self.xs = self.scratch("xs", [2, KT, 128, T])
        nck = self.outp("nck", [nL, 2, 128, T])
        ncv = self.outp("ncv", [nL, T, 256])
        yout = self.outp("y", [2, KT, 128, T])
        self.nck, self.ncv = nck, ncv

        self.NSLOT = 6
        self.ring = B.sbuf("ring", [128, self.NSLOT, KT * 128], BF16)
        self.Rring = regions(self.NSLOT)
        self.w_rr = 0
        self.PS = [B.psum("ps%d" % i, [128, 512], F32) for i in range(8)]
        self.RPS = regions(8)
        self.ps_rr = 0
        self.h = B.sbuf("h", [128, KT, T], BF16)
        self.Rh = Region()
        cm = B.sbuf("cm", [128, 8, 128], F32)
        self.Rconst = Region()
        self.ones_bf = B.sbuf("ones_bf", [128, 128], BF16)
        self.ident_bf = B.sbuf("ident_bf", [128, 128], BF16)
        self.eps_t = B.sbuf("eps_t", [128, 2], F32)
        cond_s = B.sbuf("cond_s", [128, KT, 2], F32)
        csilu = B.sbuf("csilu", [128, KT, 2], BF16)
        modT = B.sbuf("modT", [128, nL, 48, 2], F32)
        bmod_s = B.sbuf("bmod_s", [128, nL, 48], F32)
        normw_s = B.sbuf("normw_s", [128, nL, KT], F32)
        finaln_s = B.sbuf("finaln_s", [128, KT], F32)
        amod = B.sbuf("amod", [128, nL, KT, 2], F32)
        qnw_s = B.sbuf("qnw_s", [128, nL], F32)
        knw_s = B.sbuf("knw_s", [128, nL], F32)
        Rsm = Region()
        self.Rsm = Rsm
        lbl_s = B.sbuf("lbl_s", [128, 2, 8, DEPTH], F32)
        self.lb = B.sbuf("lb", [128, 2, 8, DEPTH], F32)
        self.oml = B.sbuf("oml", [128, 2, 8, DEPTH], F32)
        lbsum = B.sbuf("lbsum", [128, 2, 8], F32)
        self.onw_s = B.sbuf("onw_s", [128, nL], F32)
        self.resetm = B.sbuf("resetm", [128, T], F32)
        self.cm = cm
        self.o_hg = B.sbuf("o_hg", [128, 8, T], BF16)
        self.Rohg = regions(8)
        self.o_at = B.sbuf("o_at", [128, 8, T], BF16)
        self.Roat = regions(8)
        self.qnw_s, self.knw_s = qnw_s, knw_s
        self.o_s5 = B.sbuf("o_s5", [128, 8, T], BF16)
        self.Ros5 = regions(8)
        self.sgnk = B.sbuf("sgnk_s", [128, 3, 9], F32)
        self.s5dd_s = B.sbuf("s5dd_s", [128, nL, 64], F32)
        self.Rs5d = Region()
        self.finaln_s = finaln_s
        self.yout = yout
        self.Rxs = [regions(KT), regions(KT)]
        self.S_f = [B.sbuf("S_f%d" % d, [128, 128], F32) for d in range(2)]
        self.S_b = [B.sbuf("S_b%d" % d, [128, 128], BF16) for d in range(2)]
        self.RS = regions(2)
        self.RSf = regions(2)
        ARENA = 22 * 1024
        self.arena = B.sbuf("arena", [128, ARENA], F32)
        self.Rarena = Region()

        B.dma("sp", cm[:], cmat, writes=[self.Rconst])
        B.dma("sp", self.sgnk[:], sgnk_d, writes=[self.Rconst])
        B.op("dve", lambda e: e.memset(self.ones_bf[:], 1.0), writes=[self.Rconst])
        B.op("dve", lambda e: e.memset(self.eps_t[:, 0:1], EPS), writes=[self.Rconst])
        B.op("dve", lambda e: e.memset(self.eps_t[:, 1:2], 1.0), writes=[self.Rconst])
        self.cp(self.ident_bf[:], cm[:, 0, :], reads=[self.Rconst], writes=[self.Rconst])
        B.op("dve", lambda e: e.memset(self.resetm[:], 1.0), writes=[self.Rconst])
        B.op("dve", lambda e: e.memset(self.resetm[:].rearrange("p (c j) -> p c j", j=32)[:, :, 0:1], 0.0), reads=[self.Rconst], writes=[self.Rconst])
        for (dst, src) in ((cond_s, cond), (bmod_s, bmod), (normw_s, normw), (finaln_s, finaln), (qnw_s, qnw), (knw_s, knw),
                           (lbl_s, lblog), (self.onw_s, onw), (self.s5dd_s, s5dd)):
            B.dma("sp", dst[:], src, writes=[Rsm])

        self.act(lbl_s[:], lbl_s[:], AF.Exp, reads=[Rsm], writes=[Rsm])
        B.op("dve", lambda e: e.tensor_reduce(out=lbsum[:], in_=lbl_s[:], axis=AX.X, op=ALU.add), reads=[Rsm], writes=[Rsm])
        self.recip(lbsum[:], lbsum[:], reads=[Rsm], writes=[Rsm])
        self.tt(lbl_s[:], lbl_s[:], lbsum[:].unsqueeze(3).to_broadcast([128, 2, 8, DEPTH]), ALU.mult, reads=[Rsm], writes=[Rsm])
        B.op("dve", lambda e: e.memset(self.lb[:, :, :, 0:1], 0.0), writes=[Rsm])
        for li in range(1, DEPTH):
            self.tt(self.lb[:, :, :, li:li + 1], self.lb[:, :, :, li - 1:li], lbl_s[:, :, :, li:li + 1], ALU.add, reads=[Rsm], writes=[Rsm])
        self.ts(self.oml[:], self.lb[:], -1.0, 1.0, ALU.mult, ALU.add, reads=[Rsm], writes=[Rsm])

        tmpc = B.sbuf("tmpc", [128, KT, 2], F32)
        self.act(tmpc[:], cond_s[:], AF.Exp, reads=[Rsm], writes=[Rsm], scale=-1.0)
        self.sig_from_exp(tmpc[:], Rsm)
        self.tt(csilu[:], cond_s[:], tmpc[:], ALU.mult, reads=[Rsm], writes=[Rsm])
        Rmod = Region()
        for l in ([] if self.cfg.get("skip_mod") else self.layers):
            for j in range(48):
                w, Rw = self.wload(wmod[l, j], KT * 128)
                P_, R_ = self.ps()
                for kt in range(KT):
                    self.mm(P_[:, 0:2], w[:, kt * 128:(kt + 1) * 128], csilu[:, kt, :], kt == 0, kt == KT - 1,
                            reads=[Rw, Rsm], writes=[R_])
                self.ts(modT[:, l, j, :], P_[:, 0:2], bmod_s[:, l, j:j + 1], None, ALU.add, None,
                        reads=[R_, Rsm], writes=[Rmod])
            for r in range(2):
                self.stt(amod[:, l, :, r], modT[:, l, 16:32, r], 1.0, normw_s[:, l, :], ALU.add, ALU.mult,
                         reads=[Rmod, Rsm], writes=[Rmod])
        self.modT, self.amod, self.Rmod = modT, amod, Rmod

        if self.cfg.get("do_s5", True):
            for l in self.layers:
                self.s5_prep(l)
        for ps_ in self.passes:
            r = ps_
            for li, l in enumerate(self.layers):
                xsrc = xin if li == 0 else self.xs
                self.norm_phase(xsrc, ps_, l, r)
                if self.cfg.get("do_hgrn", True):
                    self.hgrn_phase(ps_, l)
                if self.cfg.get("do_attn", True):
                    self.attn_phase(ps_, l)
                if self.cfg.get("do_s5", True) and not self.cfg.get("s5_prep_only"):
                    self.s5_phase(ps_, l)
                if self.cfg.get("do_merge", True):
                    self.merge_phase(xsrc, ps_, l, r)
            if self.cfg.get("do_merge", True):
                self.final_phase(ps_)
        B.wait_all("sp")
        B.emit()

    def norm_phase(self, xsrc, ps_, l, r):
        B = self.B
        B.fence()
        xst = self.arena[:, 0:8192].rearrange("p (k t) -> p k t", k=KT)
        sq = self.arena[:, 8192:12288].bitcast(BF16).rearrange("p (k t) -> p k t", k=KT)
        rstd = self.arena[:, 12288:12800]
        tmp = self.arena[:, 12800:13312]
        Rx, Rsq, Rr, Rt = Region(), Region(), Region(), Region()
        for c in range(2):
            B.dma("sp", xst, xsrc[ps_, :, :, c * 512:(c + 1) * 512].rearrange("k p t -> p k t"),
                  reads=self.Rxs[ps_], writes=[Rx])
            self.act(sq, xst, AF.Square, reads=[Rx], writes=[Rsq])
            self.rstd_from_sq(lambda kt: sq[:, kt, :], KT, [Rsq], float(D), rstd, Rr, 512)
            for kt in range(KT):
                self.stt(tmp, xst[:, kt, :], self.amod[:, l, kt, r:r + 1], rstd, ALU.mult, ALU.mult,
                         reads=[Rx, Rr, self.Rmod], writes=[Rt])
                self.act(self.h[:, kt, c * 512:(c + 1) * 512], tmp, AF.Identity, reads=[Rt, self.Rmod], writes=[self.Rh],
                         scale=1.0, bias=self.modT[:, l, kt, r:r + 1])


    def s5_prep(self, l):
        B = self.B
        B.fence()
        A = self.arena
        off = [0]

        def alloc(n):
            a = A[:, off[0]:off[0] + n]
            off[0] += n
            assert off[0] <= 22 * 1024, off[0]
            return a

        TWO_PI = 2.0 * math.pi
        prm = alloc(192).rearrange("p (q g) -> p q g", q=3)
        dt = alloc(64); lre = alloc(64); lim = alloc(64)
        negpi = alloc(1)
        cre = alloc(64); cim = alloc(64); t64a = alloc(64); t64b = alloc(64); den = alloc(64)
        Fre = alloc(64); Fim = alloc(64); Gre = alloc(64); Gim = alloc(64)
        tre = alloc(3 * 576).rearrange("p (q k g) -> p q k g", q=3, k=9)
        tim = alloc(3 * 576).rearrange("p (q k g) -> p q k g", q=3, k=9)
        mark = off[0]
        ang = alloc(3 * 576).rearrange("p (q k g) -> p q k g", q=3, k=9)
        yv = alloc(3 * 576); fr = alloc(3 * 576); msk = alloc(3 * 576); mag = alloc(3 * 576)
        ki = msk.bitcast(I32)
        R = Region()

        B.dma("sp", prm, self.s5p[l], writes=[R])
        B.op("dve", lambda e: e.memset(negpi, -math.pi), writes=[R])
        self.act(dt, prm[:, 2, :], AF.Exp, reads=[R], writes=[R])
        self.tt(lre, prm[:, 0, :], dt, ALU.mult, reads=[R], writes=[R])
        self.tt(lim, prm[:, 1, :], dt, ALU.mult, reads=[R], writes=[R])
        for q in range(3):
            self.tt(ang[:, q], lim.unsqueeze(1).to_broadcast([128, 9, 64]),
                    self.sgnk[:, q, :].unsqueeze(2).to_broadcast([128, 9, 64]), ALU.mult, reads=[R, self.Rconst], writes=[R])
            self.tt(tre[:, q], lre.unsqueeze(1).to_broadcast([128, 9, 64]),
                    self.sgnk[:, q, :].unsqueeze(2).to_broadcast([128, 9, 64]), ALU.mult, reads=[R, self.Rconst], writes=[R])
        angf = ang.rearrange("p q k g -> p (q k g)")
        tref = tre.rearrange("p q k g -> p (q k g)")
        timf = tim.rearrange("p q k g -> p (q k g)")
        self.act(mag, tref, AF.Exp, reads=[R], writes=[R])

        def sin_of(dst, shift):
            self.ts(yv, angf, 1.0 / TWO_PI, 64.5 + shift, ALU.mult, ALU.add, reads=[R], writes=[R])
            self.cp(ki, yv, reads=[R], writes=[R])
            self.cp(fr, ki, reads=[R], writes=[R])
            self.tt(fr, yv, fr, ALU.subtract, reads=[R], writes=[R])
            B.op("dve", lambda e: e.tensor_single_scalar(out=msk, in_=fr, scalar=0.0, op=ALU.is_lt), reads=[R], writes=[R])
            self.tt(fr, fr, msk, ALU.add, reads=[R], writes=[R])
            self.act(dst, fr, AF.Sin, reads=[R], writes=[R], scale=TWO_PI, bias=negpi)

        if self.cfg.get("prep_stop", 9) <= 1:
            return
        sin_of(timf, 0.0)
        sin_of(tref, 0.25)
        if self.cfg.get("prep_stop", 9) <= 2:
            return
        self.tt(timf, timf, mag, ALU.mult, reads=[R], writes=[R])
        self.tt(tref, tref, mag, ALU.mult, reads=[R], writes=[R])
        are_, aim_ = prm[:, 0, :], prm[:, 1, :]
        a1r, a1i = tre[:, 2, 1, :], tim[:, 2, 1, :]
        self.tt(den, are_, are_, ALU.mult, reads=[R], writes=[R])
        self.tt(t64a, aim_, aim_, ALU.mult, reads=[R], writes=[R])
        self.tt(den, den, t64a, ALU.add, reads=[R], writes=[R])
        self.recip(den, den, reads=[R], writes=[R])
        self.ts(t64a, a1r, -1.0, None, ALU.add, None, reads=[R], writes=[R])
        self.tt(cre, t64a, are_, ALU.mult, reads=[R], writes=[R])
        self.tt(t64b, a1i, aim_, ALU.mult, reads=[R], writes=[R])
        self.tt(cre, cre, t64b, ALU.add, reads=[R], writes=[R])
        self.tt(cre, cre, den, ALU.mult, reads=[R], writes=[R])
        self.tt(cim, a1i, are_, ALU.mult, reads=[R], writes=[R])
        self.tt(t64b, t64a, aim_, ALU.mult, reads=[R], writes=[R])
        self.tt(cim, cim, t64b, ALU.subtract, reads=[R], writes=[R])
        self.tt(cim, cim, den, ALU.mult, reads=[R], writes=[R])
        a8st = yv[:, 0:128].rearrange("p (r m q) -> p r m q", r=2, m=2)
        for ri, tab in enumerate((tre, tim)):
            self.cp(a8st[:, ri], tab[:, 2, 8, :].rearrange("p (q m) -> p m q", m=2), reads=[R], writes=[R])
        for ri in range(2):
            for d in range(2):
                for m in range(2):
                    B.dma("sp", self.a8_d[l, ri, m * 64:(m + 1) * 64, d * 32:(d + 1) * 32],
                          a8st[d * 64:(d + 1) * 64, ri, m, :], reads=[R], writes=[self.Rs5d])
        B.op("dve", lambda e: e.memset(Fre[64:128, :], 1.0), writes=[R])
        B.op("dve", lambda e: e.memset(Fim[64:128, :], 0.0), writes=[R])
        self.cp(Fre[0:64, :], tre[0:64, 2, 7, :], reads=[R], writes=[R])
        self.cp(Fim[0:64, :], tim[0:64, 2, 7, :], reads=[R], writes=[R])
        self.cp(Gre[0:64, :], tre[0:64, 2, 1, :], reads=[R], writes=[R])
        self.cp(Gim[0:64, :], tim[0:64, 2, 1, :], reads=[R], writes=[R])
        self.cp(Gre[64:128, :], tre[64:128, 2, 8, :], reads=[R], writes=[R])
        self.cp(Gim[64:128, :], tim[64:128, 2, 8, :], reads=[R], writes=[R])
        if self.cfg.get("prep_stop", 9) <= 3:
            return
        B.fence()
        off[0] = mark
        Bre = alloc(1024).rearrange("p (g c) -> p g c", g=64); Bim = alloc(1024).rearrange("p (g c) -> p g c", g=64)
        Cre = alloc(1024).rearrange("p (g c) -> p g c", g=64); Cim = alloc(1024).rearrange("p (g c) -> p g c", g=64)
        bbr = alloc(1024).rearrange("p (g c) -> p g c", g=64); bbi = alloc(1024).rearrange("p (g c) -> p g c", g=64)
        X = [alloc(1024).rearrange("p (g k c) -> p g k c", g=8, k=8) for _ in range(6)]
        t1 = alloc(1024).rearrange("p (g k c) -> p g k c", g=8, k=8)
        t2 = alloc(1024).rearrange("p (g k c) -> p g k c", g=8, k=8)
        tA = alloc(128); tB = alloc(128)
        toep_st = alloc(512).bitcast(BF16).rearrange("p (g x) -> p g x", g=8)
        wb_st = alloc(1024).bitcast(BF16).rearrange("p (g r x) -> p g r x", g=8, r=2)
        ca_st = alloc(1024).bitcast(BF16).rearrange("p (r g x) -> p r g x", r=2, g=8)
        RX = regions(6); Rt1, Rt2, RtA, RtB, Rtoep, Rwb, Rca = [Region() for _ in range(7)]
        B.dma("sp", Bre.rearrange("p g c -> p (g c)"), self.s5b[l, 0], writes=[R])
        B.dma("sp", Bim.rearrange("p g c -> p (g c)"), self.s5b[l, 1], writes=[R])
        B.dma("sp", Cre.rearrange("p g c -> p (g c)"), self.s5c[l, 0], writes=[R])
        B.dma("sp", Cim.rearrange("p g c -> p (g c)"), self.s5c[l, 1], writes=[R])
        bc = lambda v: v.unsqueeze(2).to_broadcast([128, 64, 16])
        self.tt(bbr, Bre, bc(cre), ALU.mult, reads=[R], writes=[R])
        self.tt(bbi, Bim, bc(cim), ALU.mult, reads=[R], writes=[R])
        self.tt(bbr, bbr, bbi, ALU.subtract, reads=[R], writes=[R])
        self.tt(bbi, Bim, bc(cre), ALU.mult, reads=[R], writes=[R])
        self.tt(Bim, Bre, bc(cim), ALU.mult, reads=[R], writes=[R])
        self.tt(bbi, bbi, Bim, ALU.add, reads=[R], writes=[R])

        def cmul(dre, dim_, Rd, are, aim, bre, bim, rds, negate_im=False):
            self.tt(dre, are, bre, ALU.mult, reads=rds, writes=[Rd[0]])
            self.tt(t1, aim, bim, ALU.mult, reads=rds, writes=[Rt1])
            self.tt(dre, dre, t1, ALU.subtract, reads=[Rd[0], Rt1], writes=[Rd[0]])
            self.tt(dim_, are, bim, ALU.mult, reads=rds, writes=[Rd[1]])
            self.tt(t2, aim, bre, ALU.mult, reads=rds, writes=[Rt2])
            if negate_im:
                self.stt(dim_, dim_, -1.0, t2, ALU.mult, ALU.subtract, reads=[Rd[1], Rt2], writes=[Rd[1]])
            else:
                self.tt(dim_, dim_, t2, ALU.add, reads=[Rd[1], Rt2], writes=[Rd[1]])

        if self.cfg.get("prep_stop", 9) <= 4:
            return
        for gb in range(8):
            g0 = gb * 8
            tabk = lambda tab, q: tab[:, q, 0:8, g0:g0 + 8].rearrange("p k g -> p g k").unsqueeze(3).to_broadcast([128, 8, 8, 16])
            gk = lambda v: v[:, g0:g0 + 8, :].unsqueeze(2).to_broadcast([128, 8, 8, 16])
            fg = lambda v: v[:, g0:g0 + 8].unsqueeze(2).unsqueeze(3).to_broadcast([128, 8, 8, 16])
            Lre, Lim, Rre, Rim, CAre, CAim = X
            cmul(Lre, Lim, (RX[0], RX[1]), tabk(tre, 1), tabk(tim, 1), gk(bbr), gk(bbi), [R])
            cmul(Rre, Rim, (RX[2], RX[3]), tabk(tre, 0), tabk(tim, 0), gk(Cre), gk(Cim), [R])
            cmul(CAre, CAim, (RX[4], RX[5]), fg(Gre), fg(Gim), Rre, Rim, [R, RX[2], RX[3]], negate_im=True)
            self.ts(Rim, Rim, -1.0, None, ALU.mult, None, reads=[RX[3]], writes=[RX[3]])
            if self.cfg.get("prep_stop", 9) <= 5:
                return
            for ri, src in enumerate((CAre, CAim)):
                self.act(ca_st[:, ri], src.rearrange("p g k c -> p g (k c)"), AF.Copy, reads=[RX[4 + ri]], writes=[Rca])
            for ri in range(2):
                for d in range(2):
                    B.dma("sp", self.ca_d[l, d, ri, g0:g0 + 8].rearrange("g p x -> p g x"),
                          ca_st[d * 64:(d + 1) * 64, ri], reads=[Rca], writes=[self.Rs5d])
            f2 = lambda x, gl: x[:, gl].rearrange("p k c -> p (k c)")
            for gl in range(8):
                g = g0 + gl
                Pfs = [self.ps(), self.ps()]
                for d in range(2):
                    sl = slice(d * 64, (d + 1) * 64)
                    Pf, Rpf = Pfs[d]
                    self.mm(Pf[:, 0:128], f2(Lre, gl)[sl, :], f2(Rre, gl)[sl, :], True, False, reads=[RX[0], RX[2]], writes=[Rpf])
                    self.mm(Pf[:, 0:128], f2(Lim, gl)[sl, :], f2(Rim, gl)[sl, :], False, True, reads=[RX[1], RX[3]], writes=[Rpf])
                self.tt(tA, Pfs[0][0][:, 0:128], self.cm[:, 5, :], ALU.mult, reads=[Pfs[0][1], self.Rconst], writes=[RtA])
                self.tt(tB, Pfs[1][0][:, 0:128], self.cm[:, 6, :], ALU.mult, reads=[Pfs[1][1], self.Rconst], writes=[RtB])
                self.tt(tA, tA, tB, ALU.add, reads=[RtA, RtB], writes=[RtA])
                self.stt(toep_st[:, gl, :], self.cm[:, 0, :], self.s5dd_s[:, l, g:g + 1], tA, ALU.mult, ALU.add,
                         reads=[RtA, self.Rconst, self.Rsm], writes=[Rtoep])
            B.dma("sp", self.toep_d[l, g0:g0 + 8].rearrange("g p x -> p g x"), toep_st, reads=[Rtoep], writes=[self.Rs5d])
            if self.cfg.get("prep_stop", 9) <= 6:
                return
            cmul(Rre, Rim, (RX[2], RX[3]), fg(Fre), fg(Fim), Lre, Lim, [R, RX[0], RX[1]])
            for gl in range(8):
                Pf, Rpf = self.ps()
                for ri, src in enumerate((Rre, Rim)):
                    B.op("pe", lambda e, src=src, ri=ri, Pf=Pf, gl=gl: e.transpose(Pf[:, ri * 128:(ri + 1) * 128],
                                                                                  src[:, gl].rearrange("p k c -> p (k c)"), self.cm[:, 0, :]),
                         reads=[RX[2 + ri], self.Rconst], writes=[Rpf])
                self.act(wb_st[:, gl].rearrange("p r x -> p (r x)"), Pf[:, 0:256], AF.Copy, reads=[Rpf], writes=[Rwb])
            B.dma("sp", self.wb_d[l, g0:g0 + 8].rearrange("g p r x -> p g r x"), wb_st, reads=[Rwb], writes=[self.Rs5d])

    def s5_phase(self, ps_, l):
        B = self.B
        B.fence()
        A = self.arena
        win = self.win
        NS, L = (1, 1024) if ps_ == 0 else (4, 256)
        NL = L // 8
        off = [0]

        def alloc(n):
            a = A[:, off[0]:off[0] + n]
            off[0] += n
            return a

        bufA = alloc(4096).bitcast(BF16)
        bufB = alloc(4096).bitcast(BF16)
        u_dl = bufA.rearrange("p (t j n) -> p t j n", t=8, j=8)
        y_unf = bufA.rearrange("p (g n) -> p g n", g=64)
        U_unf = bufB.rearrange("p (g n) -> p g n", g=64)
        y_dl = bufB.rearrange("p (t x) -> p t x", t=8)
        WE = alloc(NS * (NL + 1) * 64).bitcast(BF16).rearrange("p (s n j r) -> p s n j r", s=NS, n=NL + 1, j=64)
        st = alloc(NS * 128).rearrange("p (s j r) -> p s j r", s=NS, j=64)
        t1 = alloc(NS * 128).rearrange("p (s j r) -> p s j r", s=NS, j=64)
        t2 = alloc(NS * 128).rearrange("p (s j r) -> p s j r", s=NS, j=64)
        nw = alloc(NS * 128).rearrange("p (s j r) -> p s j r", s=NS, j=64)
        a8 = alloc(128).rearrange("p (r j) -> p r j", r=2)
        NR = 2
        s5r = [alloc(320 * 2).bitcast(BF16) for _ in range(NR)]
        tmpf = alloc(1024)
        tmpg = alloc(1024)
        assert off[0] <= 22 * 1024, off[0]
        Ru, RU, Ry, Ryd, RWE, Rst, Rt1, Rt2, Rnw, Ra8, Rtf, Rtg = [Region() for _ in range(12)]
        Rs5r = regions(NR)
        hfn = lambda kt, c: self.h[:, kt, c * 512:(c + 1) * 512]
        sgate = self.o_s5

        for pt in range(8):
            w, Rw = self.wload(win[l, CT_SU + pt], KT * 128)

            def evu(c, P_, R_, pt=pt):
                self.act(u_dl[:, pt, :, c * 64:(c + 1) * 64].rearrange("p j n -> p n j"),
                         P_[:, :].rearrange("p (n j) -> p n j", j=8), AF.Copy, reads=[R_], writes=[Ru])

            self.proj_fm(w, Rw, KT, hfn, [self.Rh], evu)
            w, Rw = self.wload(win[l, CT_SG + pt], KT * 128)

            def evg(c, P_, R_, pt=pt):
                sl = slice(c * 512, (c + 1) * 512)
                self.act(tmpf[:, 0:512], P_[:, :], AF.Exp, reads=[R_], writes=[Rtf], scale=-1.0)
                self.sig_from_exp(tmpf[:, 0:512], Rtf)
                self.tt(sgate[:, pt, sl], P_[:, :], tmpf[:, 0:512], ALU.mult, reads=[R_, Rtf], writes=[self.Ros5[pt]])

            self.proj_fm(w, Rw, KT, hfn, [self.Rh], evg)
        Rud = Region()
        for pt in range(8):
            B.dma("sp", self.ud_d[pt * 8:(pt + 1) * 8].rearrange("g c i n -> (g c) i n"), u_dl[:, pt], reads=[Ru], writes=[Rud])
        for i in range(8):
            B.dma("sp", U_unf[i * 16:(i + 1) * 16, :, :], self.ud_d[:, :, i, :].rearrange("g c n -> c g n"), reads=[Rud], writes=[RU])
        B.dma("sp", a8, self.a8_d[l].rearrange("r p j -> p r j"), reads=[self.Rs5d], writes=[Ra8])
        if ps_ == 0:
            B.dma("sp", st[:, 0], self.s5s0[l], writes=[Rst])
        else:
            B.op("dve", lambda e: e.memset(st, 0.0), writes=[Rst])
        self.act(WE[:, :, 0, :, :], st, AF.Copy, reads=[Rst], writes=[RWE])

        def load_pair(q, what):
            k = self.s5_rr
            self.s5_rr = (self.s5_rr + 1) % NR
            buf, Rb = s5r[k], Rs5r[k]
            if what == "wb":
                v = buf[:, 0:512].rearrange("p (m r x) -> p m r x", m=2, r=2)
                B.dma("sp", v, self.wb_d[l, 2 * q:2 * q + 2].rearrange("m p r x -> p m r x"), reads=[self.Rs5d], writes=[Rb])
                return v, Rb
            vt = buf[:, 0:256].rearrange("p (m x) -> p m x", m=2)
            B.dma("sp", vt, self.toep_d[l, 2 * q:2 * q + 2].rearrange("m p x -> p m x"), reads=[self.Rs5d], writes=[Rb])
            vc = buf[:, 256:768].rearrange("p (d r x) -> p d r x", d=2, r=2)
            B.dma("sp", vc, self.ca_d[l, :, :, 2 * q:2 * q + 2].rearrange("d r m p x -> (m p) d r x"), reads=[self.Rs5d], writes=[Rb])
            return (vt, vc), Rb

        self.s5_rr = 0
        for q in range(32):
            wbv, Rb = load_pair(q, "wb")
            Pw, Rpw = self.ps()
            for m in range(2):
                g = 2 * q + m
                for d in range(2):
                    for ri in range(2):
                        blk = (d * 2 + ri) * 128
                        self.mm(Pw[m * 64:(m + 1) * 64, blk:blk + 128], wbv[:, m, ri, d * 64:(d + 1) * 64], U_unf[:, g, :], True, True,
                                reads=[Rb, RU], writes=[Rpw])
            for d in range(2):
                src = Pw[:, d * 256:(d + 1) * 256].rearrange("p (r s n) -> p s n r", r=2, s=NS)
                if d == 0:
                    dst = WE[:, :, 1:NL + 1, q, :]
                else:
                    dst = WE[:, :, NL:0:-1, 32 + q, :]
                self.cp(dst, src, reads=[Rpw], writes=[RWE])
        ar = a8[:, 0, :].unsqueeze(1).unsqueeze(3).to_broadcast([128, NS, 64, 2])
        ai = a8[:, 1, :].unsqueeze(1).unsqueeze(3).to_broadcast([128, NS, 64, 2])
        for k in range(NL):
            self.tt(t1, st, ar, ALU.mult, reads=[Rst, Ra8], writes=[Rt1])
            self.tt(t2, st, ai, ALU.mult, reads=[Rst, Ra8], writes=[Rt2])
            self.tt(nw[:, :, :, 0], t1[:, :, :, 0], t2[:, :, :, 1], ALU.subtract, reads=[Rt1, Rt2], writes=[Rnw])
            self.tt(nw[:, :, :, 1], t1[:, :, :, 1], t2[:, :, :, 0], ALU.add, reads=[Rt1, Rt2], writes=[Rnw])
            self.tt(st, nw, WE[:, :, k + 1, :, :], ALU.add, reads=[Rnw, RWE], writes=[Rst])
            self.act(WE[:, :, k + 1, :, :], st, AF.Copy, reads=[Rst], writes=[RWE])
        if ps_ == 1:
            B.dma("sp", self.ns5[l], st, reads=[Rst])
        for q in range(32):
            (tv, cv_), Rb = load_pair(q, "tc")
            if q % 4 == 0:
                Pys = [self.ps(), self.ps()]
            for m in range(2):
                g = 2 * q + m
                Py, Rpy = Pys[m]
                blk = (q % 4) * 128
                self.mm(Py[:, blk:blk + 128], tv[:, m, :], U_unf[:, g, :], True, False, reads=[Rb, RU], writes=[Rpy])
                msl = slice(m * 64, (m + 1) * 64)
                for d in range(2):
                    for ri in range(2):
                        if d == 0:
                            rhs = WE[msl, :, 0:NL, q, ri]
                        else:
                            rhs = WE[msl, :, 0:NL, 32 + q, ri][:, :, ::-1]
                        self.mm(Py[:, blk:blk + 128], cv_[msl, d, ri, :], rhs, False, d == 1 and ri == 1, reads=[Rb, RWE], writes=[Rpy])
            if q % 4 == 3:
                g0 = 2 * (q - 3)
                for m in range(2):
                    Py, Rpy = Pys[m]
                    self.act(y_unf[:, g0 + m:g0 + 8:2, :], Py[:, :].rearrange("p (g n) -> p g n", g=4), AF.Copy,
                             reads=[Rpy, Rud], writes=[Ry, Ru])
        B.dma("sp", self.yd_d, y_unf, reads=[Ry], writes=[Ryd])
        Ryl = Region()
        for gl in range(8):
            B.dma("sp", y_dl[gl * 16:(gl + 1) * 16].rearrange("c t (j n) -> c t j n", j=8),
                  self.yd_d.rearrange("(j c) (t g) n -> g c t j n", c=16, g=8)[gl], reads=[Ryd], writes=[Ryl, RU])
        C2 = 2.0 * math.sqrt(2.0 / math.pi)
        a_, b_ = tmpf[:, 0:1024], tmpg[:, 0:1024]
        for pt in range(8):
            y = y_dl[:, pt, :]
            self.tt(a_, y, y, ALU.mult, reads=[Ryl], writes=[Rtf])
            self.ts(a_, a_, 0.044715, 1.0, ALU.mult, ALU.add, reads=[Rtf], writes=[Rtf])
            self.tt(a_, a_, y, ALU.mult, reads=[Rtf, Ryl], writes=[Rtf])
            self.ts(a_, a_, -25.0, None, ALU.max, None, reads=[Rtf], writes=[Rtf])
            self.act(b_, a_, AF.Exp, reads=[Rtf], writes=[Rtg], scale=-C2)
            self.sig_from_exp(b_, Rtg)
            self.tt(y, y, b_, ALU.mult, reads=[Ryl, Rtg], writes=[Ryl])
        for f in range(8):
            w, Rw = self.wload(self.wglu[l, f], 8 * 128)

            def evglu(c, P_, R_, f=f):
                self.act(tmpf[:, 0:512], P_[:, :], AF.Exp, reads=[R_], writes=[Rtf], scale=-1.0)
                self.sig_from_exp(tmpf[:, 0:512], Rtf)
                self.tt(tmpg[:, 0:512], y_dl[:, f, c * 512:(c + 1) * 512], tmpf[:, 0:512], ALU.mult, reads=[Ryl, Rtf], writes=[Rtg])
                ov = self.o_s5[:, f, :].rearrange("p (n j) -> p j n", j=8)[:, 4 * c:4 * c + 4, :]
                self.tt(ov, tmpg[:, 0:512].rearrange("p (j n) -> p j n", j=4), ov, ALU.mult, reads=[Rtg, self.Ros5[f]], writes=[self.Ros5[f]])

            self.proj_fm(w, Rw, 8, lambda kt, c: y_dl[:, kt, c * 512:(c + 1) * 512], [Ryl], evglu)

    def merge_phase(self, xsrc, ps_, l, r):
        B = self.B
        B.fence()
        A = self.arena
        merged = A[:, 0:8192].bitcast(BF16).rearrange("p (k t) -> p k t", k=KT)
        acc = A[:, 8192:9216]
        tmp = A[:, 9216:9728]
        tmp2 = A[:, 9728:10240]
        xt = [A[:, 10240:11264], A[:, 11264:12288]]
        Rm, Racc, Rt, Rt2 = Region(), Region(), Region(), Region()
        Rxt = regions(2)
        branches = ((self.o_hg, self.Rohg), (self.o_at, self.Roat), (self.o_s5, self.Ros5))
        hfn = lambda kt, c: self.h[:, kt, c * 512:(c + 1) * 512]
        for f in range(KT):
            for b, (ob, Rob) in enumerate(branches):
                w, Rw = self.wload(self.wbr[l, b, f], 8 * 128)
                Pb = [self.ps() for _ in range(2)]
                for c in range(2):
                    for kt in range(8):
                        self.mm(Pb[c][0][:, :], w[:, kt * 128:(kt + 1) * 128], ob[:, kt, c * 512:(c + 1) * 512], kt == 0, kt == 7,
                                reads=[Rw, Rob[kt]], writes=[Pb[c][1]])
                w2, Rw2 = self.wload(self.win[l, CT_MG + b * 16 + f], KT * 128)

                def evm(c, P_, R_, b=b, Pb=Pb):
                    sl = slice(c * 512, (c + 1) * 512)
                    self.act(tmp, P_[:, :], AF.Exp, reads=[R_], writes=[Rt], scale=-1.0)
                    self.sig_from_exp(tmp, Rt)
                    if b == 0:
                        self.tt(acc[:, sl], Pb[c][0][:, :], tmp, ALU.mult, reads=[Pb[c][1], Rt], writes=[Racc])
                    else:
                        self.tt(tmp2, Pb[c][0][:, :], tmp, ALU.mult, reads=[Pb[c][1], Rt], writes=[Rt2])
                        self.tt(acc[:, sl], acc[:, sl], tmp2, ALU.add, reads=[Racc, Rt2], writes=[Racc])

                self.proj_fm(w2, Rw2, KT, hfn, [self.Rh], evm)
            self.act(merged[:, f, :], acc, AF.Copy, reads=[Racc], writes=[Rm])
        for f in range(KT):
            w, Rw = self.wload(self.wout[l, f], KT * 128)
            k = f % 2
            B.dma("sp", xt[k], xsrc[ps_, f], reads=[self.Rxs[ps_][f]], writes=[Rxt[k]])

            def evo(c, P_, R_, f=f, k=k):
                sl = slice(c * 512, (c + 1) * 512)
                self.stt(xt[k][:, sl], P_[:, :], self.modT[:, l, 32 + f, r:r + 1], xt[k][:, sl], ALU.mult, ALU.add,
                         reads=[R_, self.Rmod, Rxt[k]], writes=[Rxt[k]])

            self.proj_fm(w, Rw, KT, lambda kt, c: merged[:, kt, c * 512:(c + 1) * 512], [Rm], evo)
            B.dma("sp", self.xs[ps_, f], xt[k], reads=[Rxt[k]], writes=[self.Rxs[ps_][f]])

    def final_phase(self, ps_):
        B = self.B
        B.fence()
        xst = self.arena[:, 0:8192].rearrange("p (k t) -> p k t", k=KT)
        sq = self.arena[:, 8192:12288].bitcast(BF16).rearrange("p (k t) -> p k t", k=KT)
        rstd = self.arena[:, 12288:12800]
        Rx, Rsq, Rr = Region(), Region(), Region()
        for c in range(2):
            B.dma("sp", xst, self.xs[ps_, :, :, c * 512:(c + 1) * 512].rearrange("k p t -> p k t"),
                  reads=self.Rxs[ps_], writes=[Rx])
            self.act(sq, xst, AF.Square, reads=[Rx], writes=[Rsq])
            self.rstd_from_sq(lambda kt: sq[:, kt, :], KT, [Rsq], float(D), rstd, Rr, 512)
            for kt in range(KT):
                self.stt(xst[:, kt, :], xst[:, kt, :], self.finaln_s[:, kt:kt + 1], rstd, ALU.mult, ALU.mult,
                         reads=[Rx, Rr, self.Rsm], writes=[Rx])
            B.dma("sp", self.yout[ps_, :, :, c * 512:(c + 1) * 512].rearrange("k p t -> p k t"), xst, reads=[Rx])

    def ps_grp(self, grp):
        lo, n = grp
        i = lo + self.ps_grr.get(grp, 0)
        self.ps_grr[grp] = (self.ps_grr.get(grp, 0) + 1) % n
        return self.PS[i], self.RPS[i]

    def hgrn_phase(self, ps_, l):
        B = self.B
        B.fence()
        self.ps_grr = {}
        A = self.arena
        NS, L = (1, 1024) if ps_ == 0 else (4, 256)
        off = [0]

        def alloc(n):
            a = A[:, off[0]:off[0] + n]
            off[0] += n
            assert off[0] <= 22 * 1024, off[0]
            return a

        X1 = self.o_at[:, :, :].rearrange("p k t -> p (k t)").bitcast(F32)
        X2 = self.o_s5[:, :, :].rearrange("p k t -> p (k t)").bitcast(F32)
        bf = lambda a: a.bitcast(BF16)
        k4 = lambda a: a.bitcast(BF16).rearrange("p (t c d) -> p t c d", t=8, c=4)
        S = []
        S.append(dict(qt=[bf(alloc(512)), bf(alloc(512))], ktl=[bf(alloc(512)), bf(alloc(512))],
                      khat4=[k4(alloc(2048)), k4(alloc(2048))],
                      vtok=bf(alloc(512)).rearrange("p (t c) -> p t c", t=8), sg=alloc(1024), etot=[alloc(32), alloc(32)]))
        S.append(dict(qt=[bf(X2[:, 0:512]), bf(X2[:, 512:1024])], ktl=[bf(X2[:, 1024:1536]), bf(X2[:, 1536:2048])],
                      khat4=[k4(X1[:, 0:2048]), k4(X1[:, 2048:4096])],
                      vtok=bf(X2[:, 2048:2560]).rearrange("p (t c) -> p t c", t=8), sg=X2[:, 2560:3584], etot=[alloc(32), alloc(32)]))
        for st_ in S:
            st_["R"] = dict(qt=regions(2), ktl=regions(2), khat4=regions(2), vtok=Region(), sg=Region(), etot=regions(2))
        q_f = alloc(1024); tmp = alloc(1024); tmp2 = alloc(1024)
        tmpD = [tmp, alloc(1024)]; tmp2D = [tmp2, alloc(1024)]
        RtD = [Region(), Region()]; Rt2D = [Region(), Region()]
        bb = [alloc(1024), alloc(1024)]; kk = [alloc(1024), alloc(1024)]
        khat = [bf(alloc(512)), bf(alloc(512))]
        Rq, Rt, Rt2 = Region(), Region(), Region()
        Rb, Rk, Rkh = regions(2), regions(2), regions(2)
        o_d = [alloc(1024), alloc(1024)]
        masked = [bf(alloc(64)), bf(alloc(64))]
        sqo = bf(X2[:, 3584:4096]); rstd = alloc(1024); ntmp = alloc(1024)
        Ro, Rms = regions(2), regions(2)
        Rsq, Rr, Rnt = Region(), Region(), Region()
        win = self.win
        hfn = lambda kt, c: self.h[:, kt, c * 512:(c + 1) * 512]
        G_PREP, G_REC = (4, 4), (2, 2)

        def proj_fm(w, Rw, evac):
            banks = [self.ps_grp(G_PREP) for _ in range(2)]
            for c in range(2):
                P_, R_ = banks[c]
                for kt in range(KT):
                    self.mm(P_[:, :], w[:, kt * 128:(kt + 1) * 128], hfn(kt, c), kt == 0, kt == KT - 1, reads=[Rw, self.Rh], writes=[R_])
            for c in range(2):
                evac(c, *banks[c])

        def prep(hd):
            D_ = S[hd % 2]
            RD = D_["R"]
            qt, ktl, khat4, vtok, sg, etot = D_["qt"], D_["ktl"], D_["khat4"], D_["vtok"], D_["sg"], D_["etot"]
            w, Rw = self.wload(win[l, CT_HQ + hd], KT * 128)
            proj_fm(w, Rw, lambda c, P_, R_: self.act(q_f[:, c * 512:(c + 1) * 512], P_[:, :], AF.Copy, reads=[R_], writes=[Rq]))
            yield
            w, Rw = self.wload(win[l, CT_HI + hd], KT * 128)
            for half in range(2):
                P_, R_ = self.ps_grp(G_PREP)
                for q in range(4):
                    tt_ = half * 4 + q
                    for kt in range(KT):
                        self.mm(P_[:, q * 128:(q + 1) * 128], self.h[:, kt, tt_ * 128:(tt_ + 1) * 128], w[:, kt * 128:(kt + 1) * 128],
                                kt == 0, kt == KT - 1, reads=[Rw, self.Rh], writes=[R_])
                for q in range(4):
                    self.act(vtok[:, half * 4 + q, :], P_[:, q * 128:(q + 1) * 128], AF.Copy, reads=[R_], writes=[RD["vtok"]])
                yield
            def dchain(d):
                tmp, tmp2, Rt, Rt2 = tmpD[d], tmp2D[d], RtD[d], Rt2D[d]
                w, Rw = self.wload(win[l, (CT_HFF if d == 0 else CT_HFB) + hd], KT * 128)
                proj_fm(w, Rw, lambda c, P_, R_: self.act(tmp[:, c * 512:(c + 1) * 512], P_[:, :], AF.Exp, reads=[R_], writes=[Rt], scale=-1.0))
                yield
                self.act(tmp, tmp, AF.Ln, reads=[Rt, self.Rconst], writes=[Rt], scale=1.0, bias=self.eps_t[:, 1:2])
                yield
                self.act(tmp, tmp, AF.Exp, reads=[Rt], writes=[Rt], scale=-1.0)
                yield
                self.ts(tmp, tmp, self.oml[:, d, hd, l:l + 1], self.lb[:, d, hd, l:l + 1], ALU.mult, ALU.add, reads=[Rt, self.Rsm], writes=[Rt])
                yield
                self.ts(kk[d], tmp, -1.0, 1.0, ALU.mult, ALU.add, reads=[Rt], writes=[Rk[d]])
                self.act(tmp, tmp, AF.Ln, reads=[Rt], writes=[Rt])
                yield
                B.op("dve", lambda e, d=d: e.tensor_tensor_scan(out=bb[d], data0=self.resetm[:], data1=tmp, initial=0.0, op0=ALU.mult, op1=ALU.add),
                     reads=[Rt, self.Rconst], writes=[Rb[d]])
                yield
                b3 = bb[d].rearrange("p (c j) -> p c j", j=32)
                if d == 1:
                    self.tt(tmp, tmp, bb[d], ALU.subtract, reads=[Rt, Rb[d]], writes=[Rt])
                    yield
                    self.tt(tmp2.rearrange("p (c j) -> p c j", j=32), tmp.rearrange("p (c j) -> p c j", j=32),
                            b3[:, :, 31:32].to_broadcast([128, 32, 32]), ALU.add, reads=[Rt, Rb[d]], writes=[Rt2])
                    yield
                    self.cp(bb[d], tmp2, reads=[Rt2], writes=[Rb[d]])
                    yield
                    tot = b3[:, :, 0:1]
                else:
                    tot = b3[:, :, 31:32]
                self.act(etot[d].unsqueeze(2), tot, AF.Exp, reads=[Rb[d]], writes=[RD["etot"][d]])
                self.act(tmp, bb[d], AF.Exp, reads=[Rb[d]], writes=[Rt])
                yield
                self.tt(qt[d], q_f, tmp, ALU.mult, reads=[Rq, Rt], writes=[RD["qt"][d]])
                yield
                self.act(tmp, bb[d], AF.Exp, reads=[Rb[d]], writes=[Rt], scale=-1.0)
                yield
                self.tt(tmp2, kk[d], tmp, ALU.mult, reads=[Rk[d], Rt], writes=[Rt2])
                yield
                self.act(ktl[d], tmp2, AF.Copy, reads=[Rt2], writes=[RD["ktl"][d]])
                self.tt(khat[d].rearrange("p (c j) -> p c j", j=32), tmp2.rearrange("p (c j) -> p c j", j=32),
                        etot[d].unsqueeze(2).to_broadcast([128, 32, 32]), ALU.mult, reads=[Rt2, RD["etot"][d]], writes=[Rkh[d]])
                yield
                for half in range(2):
                    P_, R_ = self.ps_grp(G_PREP)
                    Pb = P_[:, :].bitcast(BF16)
                    for qd in range(4):
                        tt_ = half * 4 + qd
                        B.op("pe", lambda e, Pb=Pb, qd=qd, tt_=tt_, d=d: e.transpose(Pb[:, qd * 128:(qd + 1) * 128], khat[d][:, tt_ * 128:(tt_ + 1) * 128], self.ident_bf[:, :]),
                             reads=[Rkh[d], self.Rconst], writes=[R_])
                    for qd in range(4):
                        tt_ = half * 4 + qd
                        self.tt(khat4[d][:, tt_, :, :], Pb[:, qd * 128:(qd + 1) * 128].unsqueeze(1).to_broadcast([128, 4, 128]),
                                self.cm[:, 3, 0:4].unsqueeze(2).to_broadcast([128, 4, 128]), ALU.mult,
                                reads=[R_, self.Rconst], writes=[RD["khat4"][d]])
                        if qd % 2 == 1:
                            yield

            gens = [dchain(0), dchain(1)]
            alive = [True, True]
            while any(alive):
                for gi in range(2):
                    if alive[gi]:
                        try:
                            next(gens[gi])
                        except StopIteration:
                            alive[gi] = False
                yield
            Rt = RtD[0]
            w, Rw = self.wload(win[l, CT_HG + hd], KT * 128)

            def evg(c, P_, R_):
                sl = slice(c * 512, (c + 1) * 512)
                self.act(tmp[:, sl], P_[:, :], AF.Exp, reads=[R_], writes=[Rt], scale=-1.0)
                self.sig_from_exp(tmp[:, sl], Rt)
                self.tt(sg[:, sl], P_[:, :], tmp[:, sl], ALU.mult, reads=[R_, Rt], writes=[RD["sg"]])

            proj_fm(w, Rw, evg)
            yield

        def rec(hd):
            D_ = S[hd % 2]
            RD = D_["R"]
            qt, ktl, khat4, vtok, sg, etot = D_["qt"], D_["ktl"], D_["khat4"], D_["vtok"], D_["sg"], D_["etot"]
            ntile = L // 128
            for sq_ in range(NS):
                for d in range(2):
                    if ps_ == 0:
                        B.dma("sp", self.S_f[d][:, :], self.hst[l, d, hd], writes=[self.RSf[d]])
                        self.act(self.S_b[d][:, :], self.S_f[d][:, :], AF.Copy, reads=[self.RSf[d]], writes=[self.RS[d]])
                    else:
                        B.op("dve", lambda e, d=d: e.memset(self.S_f[d][:, :], 0.0), writes=[self.RSf[d]])
                        B.op("dve", lambda e, d=d: e.memset(self.S_b[d][:, :], 0.0), writes=[self.RS[d]])
                for it in range(ntile):
                    for d in range(2):
                        tl = sq_ * ntile + (it if d == 0 else ntile - 1 - it)
                        ts_ = slice(tl * 128, (tl + 1) * 128)
                        Psc, Rsc = self.ps_grp(G_REC)
                        self.mm(Psc[:, 0:128], ktl[d][:, ts_], qt[d][:, ts_], True, True, reads=[RD["ktl"][d], RD["qt"][d]], writes=[Rsc])
                        self.tt(masked[d], Psc[:, 0:128], self.cm[:, 1 + d, :], ALU.mult, reads=[Rsc, self.Rconst], writes=[Rms[d]])
                        Po, Rpo = self.PS[d], self.RPS[d]
                        self.mm(Po[:, 0:128], vtok[:, tl, :], masked[d], True, False, reads=[RD["vtok"], Rms[d]], writes=[Rpo])
                        corder = range(4) if d == 0 else range(3, -1, -1)
                        for ci, c in enumerate(corder):
                            gc = tl * 4 + c
                            cs = slice(tl * 128 + c * 32, tl * 128 + (c + 1) * 32)
                            self.mm(Po[:, c * 32:(c + 1) * 32], self.S_b[d][:, :], qt[d][:, cs], False, ci == 3,
                                    reads=[self.RS[d], RD["qt"][d]], writes=[Rpo])
                            Pu, Rpu = self.ps_grp(G_REC)
                            self.mm(Pu[:, 0:128], khat4[d][:, tl, c, :], vtok[:, tl, :], True, True, reads=[RD["khat4"][d], RD["vtok"]], writes=[Rpu])
                            self.stt(self.S_b[d][:, :], self.S_f[d][:, :], etot[d][:, gc:gc + 1], Pu[:, 0:128], ALU.mult, ALU.add,
                                     reads=[self.RSf[d], RD["etot"][d], Rpu], writes=[self.RS[d]])
                            self.stt(self.S_f[d][:, :], self.S_f[d][:, :], etot[d][:, gc:gc + 1], Pu[:, 0:128], ALU.mult, ALU.add,
                                     reads=[self.RSf[d], RD["etot"][d], Rpu], writes=[self.RSf[d]])
                            yield
                        self.act(o_d[d][:, ts_], Po[:, 0:128], AF.Copy, reads=[Rpo], writes=[Ro[d]])
                if ps_ == 1:
                    for d in range(2):
                        B.dma("sp", self.nhs[sq_, l, d, hd], self.S_f[d][:, :], reads=[self.RSf[d]])
            self.tt(o_d[0], o_d[0], o_d[1], ALU.add, reads=[Ro[0], Ro[1]], writes=[Ro[0]])
            self.act(sqo, o_d[0], AF.Square, reads=[Ro[0]], writes=[Rsq])
            yield
            for c in range(2):
                P_, R_ = self.ps_grp(G_REC)
                self.mm(P_[:, 0:512], self.ones_bf[:, :], sqo[:, c * 512:(c + 1) * 512], True, True, reads=[Rsq, self.Rconst], writes=[R_])
                self.act(rstd[:, c * 512:(c + 1) * 512], P_[:, 0:512], AF.Ln, reads=[R_, self.Rconst], writes=[Rr], scale=1.0 / 128.0, bias=self.eps_t[:, 0:1])
            self.act(rstd, rstd, AF.Exp, reads=[Rr], writes=[Rr], scale=-0.5)
            yield
            self.stt(ntmp, o_d[0], self.onw_s[:, l:l + 1], rstd, ALU.mult, ALU.mult, reads=[Ro[0], Rr, self.Rsm], writes=[Rnt])
            self.tt(self.o_hg[:, hd, :], ntmp, sg, ALU.mult, reads=[Rnt, RD["sg"]], writes=[self.Rohg[hd]])
            yield

        heads = list(self.cfg.get("heads", range(8)))
        for _ in prep(heads[0]):
            pass
        for i, hd in enumerate(heads):
            gr = rec(hd)
            gp = prep(heads[i + 1]) if i + 1 < len(heads) else iter(())
            done_r = done_p = False
            while not (done_r and done_p):
                if not done_r:
                    try:
                        next(gr)
                    except StopIteration:
                        done_r = True
                if not done_p:
                    try:
                        next(gp)
                    except StopIteration:
                        done_p = True
        B.fence()

    def attn_phase(self, ps_, l):
        B = self.B
        B.fence()
        A = self.arena
        win = self.win
        NS, L = (1, 1024) if ps_ == 0 else (4, 256)
        sample = ps_ == 0
        koff = 256 if sample else 0
        NK = koff + 1024
        off = [0]

        def alloc(n):
            a = A[:, off[0]:off[0] + n]
            off[0] += n
            return a

        k_use = alloc(NK).bitcast(BF16).rearrange("p (h n) -> p h n", h=2)
        vtok = alloc(1280).bitcast(BF16).rearrange("p (t h d) -> p t h d", t=10, h=2)
        kraw = alloc(1024); rstd = alloc(1024); kn = alloc(1024); sg = alloc(1024); tmp = alloc(1024); tmp2 = alloc(1024)
        sqk = alloc(512).bitcast(BF16)
        q_use = alloc(512).bitcast(BF16)
        vtf = alloc(2048).rearrange("p (t c) -> p t c", t=8)
        Eb = [alloc(256).bitcast(BF16) for _ in range(4)]
        rden = alloc(512)
        if sample:
            rC = alloc(1024); rS = alloc(1024)
        assert off[0] <= 20 * 1024, off[0]
        Rk, Rsq, Rr, Rkn, Rv, Rvt, Rku, Rsg, Rt, Rt2, Rqu, Rrd, Rrope = [Region() for _ in range(13)]
        RE = regions(4)
        hfn = lambda kt, c: self.h[:, kt, c * 512:(c + 1) * 512]
        if sample:
            B.dma("sp", rC, self.ropeC, writes=[Rrope])
            B.dma("sp", rS, self.ropeS, writes=[Rrope])

        def normed(ct, nw_ap):
            w, Rw = self.wload(win[l, ct], KT * 128)

            def evac(c, P_, R_):
                self.act(kraw[:, c * 512:(c + 1) * 512], P_[:, :], AF.Copy, reads=[R_], writes=[Rk])
                self.act(sqk[:, c * 512:(c + 1) * 512], P_[:, :], AF.Square, reads=[R_], writes=[Rsq])

            self.proj_fm(w, Rw, KT, hfn, [self.Rh], evac)
            for c in range(2):
                self.rstd_from_sq(lambda kt, c=c: sqk[:, c * 512:(c + 1) * 512], 1, [Rsq], 128.0,
                                  rstd[:, c * 512:(c + 1) * 512], Rr, 512)
            self.stt(kn, kraw, nw_ap, rstd, ALU.mult, ALU.mult, reads=[Rk, Rr, self.Rsm], writes=[Rkn])

        def rope_to(dst, Rdst):
            for c in range(2):
                sl = slice(c * 512, (c + 1) * 512)
                P_, R_ = self.ps()
                self.mm(P_[:, :], self.cm[:, 4, :], kn[:, sl], True, True, reads=[Rkn, self.Rconst], writes=[R_])
                self.tt(tmp[:, sl], kn[:, sl], rC[:, sl], ALU.mult, reads=[Rkn, Rrope], writes=[Rt])
                self.tt(tmp2[:, sl], P_[:, :], rS[:, sl], ALU.mult, reads=[R_, Rrope], writes=[Rt2])
                self.tt(dst[:, sl], tmp[:, sl], tmp2[:, sl], ALU.add, reads=[Rt, Rt2], writes=[Rdst])

        for kvh in range(2):
            normed(CT_AK + kvh, self.knw_s[:, l:l + 1])
            if sample:
                rope_to(k_use[:, kvh, koff:koff + 1024], Rku)
                B.dma("sp", tmp[:, 0:256], self.ck[l, kvh], reads=[Rt], writes=[Rt])
                self.act(k_use[:, kvh, 0:256], tmp[:, 0:256], AF.Copy, reads=[Rt], writes=[Rku])
            else:
                B.dma("sp", self.nck[l, kvh], kn, reads=[Rkn])
                self.act(k_use[:, kvh, 0:1024], kn, AF.Copy, reads=[Rkn], writes=[Rku])
        kt0 = 2 if sample else 0
        for kvh in range(2):
            w, Rw = self.wload(win[l, CT_AV + kvh], KT * 128)

            def evacv(tt, P_, R_, kvh=kvh):
                self.act(vtok[:, kt0 + tt, kvh, :], P_, AF.Copy, reads=[R_], writes=[Rvt])
                if not sample:
                    self.act(vtf[:, tt, kvh * 128:(kvh + 1) * 128], P_, AF.Copy, reads=[R_], writes=[Rv])

            self.proj_tm(w, Rw, evacv)
        if sample:
            B.dma("sp", vtf[:, 0:2, :], self.cv[l].rearrange("(t p) c -> p t c", p=128), writes=[Rv])
            self.act(vtok[:, 0:2, :, :], vtf[:, 0:2, :].rearrange("p t (h d) -> p t h d", h=2), AF.Copy, reads=[Rv], writes=[Rvt])
        else:
            B.dma("sp", self.ncv[l].rearrange("(t p) c -> p t c", p=128), vtf, reads=[Rv])
        scale = 128.0 ** -0.5
        e_rr = 0
        a_rr = 0
        s_rr = 0
        for hq in range(self.cfg.get("n_qheads", 8)):
            kvh = hq // 4
            normed(CT_AQ + hq, self.qnw_s[:, l:l + 1])
            if sample:
                rope_to(q_use, Rqu)
            else:
                self.act(q_use, kn, AF.Copy, reads=[Rkn], writes=[Rqu])
            w, Rw = self.wload(win[l, CT_AG + hq], KT * 128)

            def evg(c, P_, R_):
                sl = slice(c * 512, (c + 1) * 512)
                self.act(tmp[:, sl], P_[:, :], AF.Exp, reads=[R_], writes=[Rt], scale=-1.0)
                self.sig_from_exp(tmp[:, sl], Rt)
                self.tt(sg[:, sl], P_[:, :], tmp[:, sl], ALU.mult, reads=[R_, Rt], writes=[Rsg])

            self.proj_fm(w, Rw, KT, hfn, [self.Rh], evg)
            for sq_ in range(NS):
                if sample:
                    ktiles = list(range(10))
                    qchunks = [(0, 512), (512, 512)]
                else:
                    ktiles = [2 * sq_, 2 * sq_ + 1]
                    qchunks = [(sq_ * 256, 256)]
                for (q0, qn_) in qchunks:
                    Po, Rpo = self.PS[a_rr], self.RPS[a_rr]
                    Pd, Rpd = self.PS[2 + a_rr], self.RPS[2 + a_rr]
                    a_rr = 1 - a_rr
                    for ji, j in enumerate(ktiles):
                        Psc, Rsc = self.PS[4 + s_rr], self.RPS[4 + s_rr]
                        s_rr = (s_rr + 1) % 4
                        self.mm(Psc[:, 0:qn_], k_use[:, kvh, j * 128:(j + 1) * 128], q_use[:, q0:q0 + qn_], True, True,
                                reads=[Rku, Rqu], writes=[Rsc])
                        E, RE_ = Eb[e_rr], RE[e_rr]
                        e_rr = (e_rr + 1) % 4
                        self.act(E[:, 0:qn_], Psc[:, 0:qn_], AF.Exp, reads=[Rsc], writes=[RE_], scale=scale)
                        first, last = ji == 0, ji == len(ktiles) - 1
                        self.mm(Po[:, 0:qn_], vtok[:, j, kvh, :], E[:, 0:qn_], first, last, reads=[Rvt, RE_], writes=[Rpo])
                        self.mm(Pd[:, 0:qn_], self.ones_bf[:, :], E[:, 0:qn_], first, last, reads=[self.Rconst, RE_], writes=[Rpd])
                    self.act(rden[:, 0:qn_], Pd[:, 0:qn_], AF.Ln, reads=[Rpd], writes=[Rrd])
                    self.act(rden[:, 0:qn_], rden[:, 0:qn_], AF.Exp, reads=[Rrd], writes=[Rrd], scale=-1.0)
                    self.tt(tmp2[:, q0:q0 + qn_], Po[:, 0:qn_], rden[:, 0:qn_], ALU.mult, reads=[Rpo, Rrd], writes=[Rt2])
                    self.tt(self.o_at[:, hq, q0:q0 + qn_], tmp2[:, q0:q0 + qn_], sg[:, q0:q0 + qn_], ALU.mult,
                            reads=[Rt2, Rsg], writes=[self.Roat[hq]])


def _const_mats():
    cm = np.zeros((128, 8, 128), np.float32)
    cm[:, 0, :] = np.eye(128, dtype=np.float32)
    idx = np.arange(128)
    same = (idx[:, None] // 32) == (idx[None, :] // 32)
    cm[:, 1, :] = (same & (idx[:, None] <= idx[None, :])).astype(np.float32)
    cm[:, 2, :] = (same & (idx[:, None] >= idx[None, :])).astype(np.float32)
    cm[:, 3, 0:4] = (idx[:, None] // 32 == np.arange(4)[None, :]).astype(np.float32)
    partner = np.where((idx % 64) < 32, idx + 32, idx - 32)
    cm[partner, 4, idx] = 1.0
    blk = idx // 16
    cm[:, 5, :] = (blk[:, None] <= blk[None, :]).astype(np.float32)
    cm[:, 6, :] = (blk[:, None] >= blk[None, :]).astype(np.float32)
    return cm


def _rope_tables():
    f = np.float32
    t = np.arange(T)
    row = (t // 64).astype(f)
    col = (t % 64).astype(f)
    inv = (f(10000.0) ** (-(np.arange(32, dtype=f) / f(32)))).astype(f)
    ar = (row[None, :] * inv[:, None]).astype(f)
    ac = (col[None, :] * inv[:, None]).astype(f)
    C = np.concatenate([np.cos(ar), np.cos(ar), np.cos(ac), np.cos(ac)], 0).astype(f)
    S = np.concatenate([-np.sin(ar), np.sin(ar), -np.sin(ac), np.sin(ac)], 0).astype(f)
    return np.ascontiguousarray(C), np.ascontiguousarray(S)


def host_inputs(inp, nL=DEPTH):
    f = np.float32
    shared = {}
    wm = inp["w_mod"][:nL].reshape(nL, KT, 128, 48, 128).transpose(0, 3, 2, 1, 4)
    shared["wmod"] = np.ascontiguousarray(wm).reshape(nL, 48, 128, KT * 128)
    shared["bmod"] = np.ascontiguousarray(inp["b_mod"][:nL].reshape(nL, 48, 128).transpose(2, 0, 1))
    shared["normw"] = np.ascontiguousarray(inp["norm_w"][:nL].reshape(nL, KT, 128).transpose(2, 0, 1))
    shared["finaln"] = np.ascontiguousarray(inp["final_norm"].reshape(KT, 128).T)
    wi = inp["w_in"][:nL].reshape(nL, KT, 128, NCT, 128).transpose(0, 3, 2, 1, 4)
    shared["win"] = np.ascontiguousarray(wi).reshape(nL, NCT, 128, KT * 128)
    shared["qnw"] = np.ascontiguousarray(inp["at_q_norm"][:nL].T)
    shared["knw"] = np.ascontiguousarray(inp["at_k_norm"][:nL].T)
    shared["cmat"] = _const_mats()
    shared["ropeC"], shared["ropeS"] = _rope_tables()
    are = inp["s5_a_re"][:nL].transpose(0, 1, 3, 2).reshape(nL, 128, 64)
    aim = inp["s5_a_im"][:nL].transpose(0, 1, 3, 2).reshape(nL, 128, 64)
    ldt = np.broadcast_to(inp["s5_log_dt"][:nL][:, :, None, :], (nL, 2, 64, 64)).reshape(nL, 128, 64)
    shared["s5p"] = np.ascontiguousarray(np.stack([are, aim, ldt], 2))
    bre = inp["s5_b_re"][:nL].transpose(0, 1, 3, 2, 4).reshape(nL, 128, 1024)
    bim = inp["s5_b_im"][:nL].transpose(0, 1, 3, 2, 4).reshape(nL, 128, 1024)
    shared["s5b"] = np.ascontiguousarray(np.stack([bre, bim], 1))
    cre = inp["s5_c_re"][:nL].transpose(0, 3, 1, 2).reshape(nL, 1, 64, 1024)
    cim = inp["s5_c_im"][:nL].transpose(0, 3, 1, 2).reshape(nL, 1, 64, 1024)
    cre = np.broadcast_to(cre, (nL, 2, 64, 1024)).reshape(nL, 128, 1024)
    cim = np.broadcast_to(cim, (nL, 2, 64, 1024)).reshape(nL, 128, 1024)
    shared["s5c"] = np.ascontiguousarray(np.stack([cre, cim], 1))
    dd = inp["s5_d"][:nL].reshape(nL, 64, 16)
    shared["s5dd"] = np.ascontiguousarray(np.broadcast_to(dd.transpose(2, 0, 1)[None], (8, 16, nL, 64)).reshape(128, nL, 64))
    sg = np.where(np.arange(128) < 64, 1.0, -1.0).astype(f)[:, None] * np.arange(9, dtype=f)[None, :]
    shared["sgnk"] = np.ascontiguousarray(np.stack([sg, -sg, np.broadcast_to(np.arange(9, dtype=f), (128, 9))], 1))
    wg = inp["s5_w_glu"][:nL].reshape(nL, 8, 128, 8, 128).transpose(0, 3, 2, 1, 4)
    shared["wglu"] = np.ascontiguousarray(wg).reshape(nL, 8, 128, 1024)
    wbr = np.stack([inp[k][:nL].reshape(nL, 8, 128, KT, 128).transpose(0, 3, 2, 1, 4) for k in ("w_br_hg", "w_br_at", "w_br_s5")], 1)
    shared["wbr"] = np.ascontiguousarray(wbr).reshape(nL, 3, KT, 128, 1024)
    wo = inp["w_out"][:nL].reshape(nL, KT, 128, KT, 128).transpose(0, 3, 2, 1, 4)
    shared["wout"] = np.ascontiguousarray(wo).reshape(nL, KT, 128, KT * 128)
    shared["lblog"] = np.ascontiguousarray(inp["hg_lb_logits"].reshape(2, DEPTH, 8, 128).transpose(3, 0, 2, 1))
    shared["onw"] = np.ascontiguousarray(inp["hg_onorm"][:nL].T)
    maps = []
    for i in range(NCORES):
        m = dict(shared)
        xs = inp["x_sample"][i]
        xp = inp["x_prompt"][4 * i:4 * i + 4].reshape(T, D)
        xin = np.stack([xs.T.reshape(KT, 128, T), xp.T.reshape(KT, 128, T)], 0)
        m["xin"] = np.ascontiguousarray(xin)
        cond = np.stack([inp["c"][i].reshape(KT, 128).T, inp["c_ctx"].reshape(KT, 128).T], -1)
        m["cond"] = np.ascontiguousarray(cond)
        m["hst"] = np.ascontiguousarray(inp["state_hgrn"][i, :nL])
        s0 = np.stack([inp["state_s5_re"][i, :nL], inp["state_s5_im"][i, :nL]], -1)
        s0 = s0.reshape(nL, 2, 32, 2, 64, 2).transpose(0, 3, 4, 1, 2, 5)
        m["s5s0"] = np.ascontiguousarray(s0).reshape(nL, 128, 64, 2)
        m["ck"] = np.ascontiguousarray(inp["cache_k"][i, :nL].transpose(0, 2, 3, 1))
        m["cv"] = np.ascontiguousarray(inp["cache_v"][i, :nL].reshape(nL, 256, 256))
        maps.append(m)
    return maps


_PROG_CACHE = {}


def run(inp, cfg):
    key = repr(sorted(cfg.items()))
    if key not in _PROG_CACHE:
        _PROG_CACHE[key] = Prog(cfg)
    prog = _PROG_CACHE[key]
    maps = host_inputs(inp, cfg.get("n_layers_alloc", DEPTH))
    maps = [{k: v for k, v in m.items() if k in prog.din} for m in maps]
    res = run_bass_kernel_spmd(prog.nc, maps, core_ids=list(range(NCORES)))
    return res.results


def kernel(**inputs):
    inp = {k: np.asarray(v) for k, v in inputs.items()}
    res = run(inp, {})
    L = DEPTH
    y = np.stack([r["y"] for r in res], 0)
    y_sample = np.ascontiguousarray(y[:, 0].reshape(NCORES, D, T).transpose(0, 2, 1))
    y_prompt = np.ascontiguousarray(y[:, 1].reshape(NCORES, D, T).transpose(0, 2, 1)).reshape(32, 256, D)
    nck = np.stack([r["nck"] for r in res], 0)
    new_k = np.ascontiguousarray(nck.reshape(NCORES, L, 2, 128, 4, 256).transpose(0, 4, 1, 5, 2, 3)).reshape(32, L, 256, 2, 128)
    ncv = np.stack([r["ncv"] for r in res], 0)
    new_v = np.ascontiguousarray(ncv.reshape(NCORES, L, 4, 256, 2, 128).transpose(0, 2, 1, 3, 4, 5)).reshape(32, L, 256, 2, 128)
    nhs = np.stack([r["nhs"] for r in res], 0).reshape(32, L, 2, 8, 128, 128)
    ns5 = np.stack([r["ns5"] for r in res], 0)
    a = ns5.reshape(NCORES, L, 2, 64, 4, 2, 32, 2).transpose(0, 4, 1, 5, 6, 2, 3, 7).reshape(32, L, 2, 64, 64, 2)
    f = np.float32
    return (y_prompt.astype(f), y_sample.astype(f), new_k.astype(f), new_v.astype(f), np.ascontiguousarray(nhs).astype(f),
            np.ascontiguousarray(a[..., 0]).astype(f), np.ascontiguousarray(a[..., 1]).astype(f))
```
